# Optimizing a Trainium2 kernel written in Bass

```python
import math
import jax, jax.numpy as jnp
from jax import lax
import numpy as np

D_MODEL = 1024
BATCH = 2
SEQ = 16384
DEPTH = 4

GRID_W = 64
CTX_LEN = 256
N_MIXERS = 2
EXPAND_W = 1536
RG_BLOCKS = 16
RG_BLOCK_W = EXPAND_W // RG_BLOCKS
RG_CONV_W = 4
RG_CONV_PAD = (1, 2)
RG_C = 8.0
RG_A_MIN = 0.9
RG_A_MAX = 0.999
HY_ORDER = 2
HY_CONV_W = 3
HY_CONV_PAD = (1, 1)
HY_EMB_BANDS = 8
HY_EMB_DIM = 1 + 2 * HY_EMB_BANDS
HY_FILTER_HIDDEN = 64
HY_FAST_DECAY_PCT = 0.3
HY_SLOW_DECAY_PCT = 1.5
HY_DECAY_TARGET = 1e-2
HY_MOD_SHIFT = 0.05
LN_EPS = 1e-5
DEEPNORM_ALPHA = (2 * DEPTH) ** 0.25
DEEPNORM_BETA = (8 * DEPTH) ** -0.25

kernel_name = "hybrid_rglru_hyena_diffusion_trunk"


def _layer_norm(x, g, b):
    xf = x.astype(jnp.float32)
    mu = jnp.mean(xf, axis=-1, keepdims=True)
    var = jnp.mean(jnp.square(xf - mu), axis=-1, keepdims=True)
    return ((xf - mu) * lax.rsqrt(var + LN_EPS) * g + b).astype(x.dtype)


def _dwconv(x, w, b, pad):
    y = lax.conv_general_dilated(
        x, w[:, None, :].astype(x.dtype), window_strides=(1,), padding=[pad],
        dimension_numbers=("NWC", "WIO", "NWC"), feature_group_count=x.shape[-1])
    return y + b


def _to_col_major(x, rows):
    b, s, d = x.shape
    return x.reshape(b, rows, GRID_W, d).transpose(0, 2, 1, 3).reshape(b, s, d)


def _to_row_major(x, rows):
    b, s, d = x.shape
    return x.reshape(b, GRID_W, rows, d).transpose(0, 2, 1, 3).reshape(b, s, d)


def _linear_scan(a, bx, h0, reverse):
    edge = -1 if reverse else 0
    bx = bx.at[:, edge].add(a[:, edge] * h0)

    def combine(left, right):
        a_l, b_l = left
        a_r, b_r = right
        return a_l * a_r, a_r * b_l + b_r

    _, h = lax.associative_scan(combine, (a, bx), reverse=reverse, axis=1)
    return h


def _rglru_coeffs(xc, w_r, b_r, w_i, b_i, lam):
    bsz, L, _ = xc.shape
    xb = xc.reshape(bsz, L, RG_BLOCKS, RG_BLOCK_W)
    r = jax.nn.sigmoid(jnp.einsum("blhi,hij->blhj", xb, w_r).reshape(bsz, L, EXPAND_W) + b_r)
    i = jax.nn.sigmoid(jnp.einsum("blhi,hij->blhj", xb, w_i).reshape(bsz, L, EXPAND_W) + b_i)
    log_a = (-RG_C * r * jax.nn.softplus(-lam)).astype(jnp.float32)
    a = jnp.exp(log_a)
    mult = jnp.sqrt(-jnp.expm1(2.0 * log_a))
    return a, (mult * (i * xc)).astype(jnp.float32)


def _rglru_mixer(h_lat, h_ctx, w_in, conv_w, conv_b, w_r, b_r, w_i, b_i, lam, w_out, need_ctx_out):
    E = EXPAND_W
    dt = h_lat.dtype
    ug_l = h_lat @ w_in
    u_l, g_l = ug_l[..., :E], ug_l[..., E:]
    xc_l = _dwconv(u_l, conv_w, conv_b, RG_CONV_PAD).astype(jnp.float32)
    if need_ctx_out:
        ug_c = h_ctx @ w_in
        u_c, g_c = ug_c[..., :E], ug_c[..., E:]
    else:
        u_c = h_ctx @ w_in[:, :E]
    xc_c = _dwconv(u_c, conv_w, conv_b, RG_CONV_PAD).astype(jnp.float32)
    zeros = jnp.zeros((xc_c.shape[0], E), jnp.float32)
    outs_l, outs_c = [], []
    for d, reverse in enumerate((False, True)):
        a_c, bx_c = _rglru_coeffs(xc_c, w_r[d], b_r[d], w_i[d], b_i[d], lam[d])
        hs_c = _linear_scan(a_c, bx_c, zeros, reverse)
        h_end = hs_c[:, 0] if reverse else hs_c[:, -1]
        a_l, bx_l = _rglru_coeffs(xc_l, w_r[d], b_r[d], w_i[d], b_i[d], lam[d])
        outs_l.append(_linear_scan(a_l, bx_l, h_end, reverse))
        outs_c.append(hs_c)
    y_l = outs_l[0] + outs_l[1]
    out_l = (y_l.astype(dt) * jax.nn.silu(g_l)) @ w_out
    if not need_ctx_out:
        return out_l, None
    y_c = outs_c[0] + outs_c[1]
    out_c = (y_c.astype(h_ctx.dtype) * jax.nn.silu(g_c)) @ w_out
    return out_l, out_c


def _hyena_filters(L, w1, b1, w2, b2, w3, freq):
    f32 = jnp.float32
    t = jnp.linspace(0.0, 1.0, L, dtype=f32)[:, None]
    bands = jnp.linspace(1e-4, HY_EMB_BANDS - 1, HY_EMB_BANDS, dtype=f32)
    w = (2.0 * math.pi) * jnp.arange(L, dtype=f32)[:, None] / L
    z = jnp.concatenate([t, jnp.cos(bands * w), -jnp.sin(bands * w)], axis=-1)
    fr = freq.astype(f32)
    hid = jnp.sin(fr * (z @ w1.astype(f32) + b1.astype(f32)))
    hid = jnp.sin(fr * (hid @ w2.astype(f32) + b2.astype(f32)))
    h = (hid @ w3.astype(f32)).reshape(L, HY_ORDER, 2, EXPAND_W)
    max_decay = math.log(HY_DECAY_TARGET) / HY_FAST_DECAY_PCT
    min_decay = math.log(HY_DECAY_TARGET) / HY_SLOW_DECAY_PCT
    deltas = jnp.linspace(min_decay, max_decay, EXPAND_W, dtype=f32)
    window = jnp.exp(-t * jnp.abs(deltas)) + HY_MOD_SHIFT
    h = h * window[:, None, None, :]
    fwd = h[:, :, 0]
    bwd = h[1:, :, 1]
    gap = jnp.zeros((1, HY_ORDER, EXPAND_W), f32)
    k = jnp.concatenate([fwd, gap, bwd[::-1]], axis=0)
    k = k / jnp.sum(jnp.abs(k), axis=0, keepdims=True)
    return jnp.fft.rfft(k, axis=0)


def _fftconv(u, k_f, d_bias):
    L = u.shape[1]
    u_f = jnp.fft.rfft(u, n=2 * L, axis=1)
    y = jnp.fft.irfft(u_f * k_f, n=2 * L, axis=1)[:, :L]
    return y + u * d_bias


def _hyena_mixer(h, w_in, conv_w, conv_b, w1, b1, w2, b2, w3, freq, d_bias, w_out):
    E = EXPAND_W
    dt = h.dtype
    L = h.shape[1]
    proj = h @ w_in
    vx = _dwconv(proj[..., :3 * E], conv_w, conv_b, HY_CONV_PAD).astype(jnp.float32)
    g = proj[..., 3 * E:]
    v, x1, x2 = vx[..., :E], vx[..., E:2 * E], vx[..., 2 * E:]
    k_f = _hyena_filters(L, w1, b1, w2, b2, w3, freq)
    dd = d_bias.astype(jnp.float32)
    zz = x1 * _fftconv(v, k_f[:, 0], dd[0])
    y = x2 * _fftconv(zz, k_f[:, 1], dd[1])
    return (y.astype(dt) * jax.nn.silu(g)) @ w_out


def setup_inputs(seed: int = 0) -> dict:
    key = jax.random.key(seed)
    keys = iter(jax.random.split(key, 64))

    def nrm(shape, scale):
        return jax.random.normal(next(keys), shape, jnp.float32) * scale

    D, E, FH = D_MODEL, EXPAND_W, HY_FILTER_HIDDEN
    n_a = len(range(0, DEPTH, N_MIXERS))
    n_b = len(range(1, DEPTH, N_MIXERS))
    a_pow = jax.random.uniform(next(keys), (n_a, 2, E), jnp.float32, RG_A_MIN, RG_A_MAX)
    a_base = a_pow ** (1.0 / RG_C)
    rg_lambda = jnp.log(a_base) - jnp.log1p(-a_base)
    return {
        "x": nrm((BATCH, SEQ, D), 1.0),
        "c": nrm((BATCH, D), 1.0),
        "ctx": nrm((BATCH, CTX_LEN, D), 1.0),
        "c_ctx": nrm((D,), 1.0),
        "w_mod": nrm((DEPTH, D, 3 * D), 0.5 * D ** -0.5),
        "b_mod": nrm((DEPTH, 3 * D), 0.02),
        "ln_g": 1.0 + nrm((DEPTH, D), 0.02),
        "ln_b": nrm((DEPTH, D), 0.02),
        "rg_w_in": nrm((n_a, D, 2 * E), D ** -0.5),
        "rg_conv_w": nrm((n_a, RG_CONV_W, E), RG_CONV_W ** -0.5),
        "rg_conv_b": nrm((n_a, E), 0.02),
        "rg_w_r": nrm((n_a, 2, RG_BLOCKS, RG_BLOCK_W, RG_BLOCK_W), RG_BLOCK_W ** -0.5),
        "rg_b_r": nrm((n_a, 2, E), 0.1),
        "rg_w_i": nrm((n_a, 2, RG_BLOCKS, RG_BLOCK_W, RG_BLOCK_W), RG_BLOCK_W ** -0.5),
        "rg_b_i": nrm((n_a, 2, E), 0.1),
        "rg_lambda": rg_lambda,
        "rg_w_out": nrm((n_a, E, D), DEEPNORM_BETA * E ** -0.5),
        "hy_w_in": nrm((n_b, D, 4 * E), D ** -0.5),
        "hy_conv_w": nrm((n_b, HY_CONV_W, 3 * E), HY_CONV_W ** -0.5),
        "hy_conv_b": nrm((n_b, 3 * E), 0.02),
        "hy_f_w1": nrm((n_b, HY_EMB_DIM, FH), HY_EMB_DIM ** -0.5),
        "hy_f_b1": nrm((n_b, FH), 0.1),
        "hy_f_w2": nrm((n_b, FH, FH), FH ** -0.5),
        "hy_f_b2": nrm((n_b, FH), 0.1),
        "hy_f_w3": nrm((n_b, FH, HY_ORDER * 2 * E), FH ** -0.5),
        "hy_f_freq": 1.0 + nrm((n_b, FH), 0.1),
        "hy_d": nrm((n_b, HY_ORDER, E), 1.0),
        "hy_w_out": nrm((n_b, E, D), DEEPNORM_BETA * E ** -0.5),
    }


def reference(x, c, ctx, c_ctx, w_mod, b_mod, ln_g, ln_b,
              rg_w_in, rg_conv_w, rg_conv_b, rg_w_r, rg_b_r, rg_w_i, rg_b_i, rg_lambda, rg_w_out,
              hy_w_in, hy_conv_w, hy_conv_b, hy_f_w1, hy_f_b1, hy_f_w2, hy_f_b2, hy_f_w3, hy_f_freq,
              hy_d, hy_w_out):
    rows = x.shape[1] // GRID_W
    act_lat = jax.nn.silu(c)
    act_ctx = jax.nn.silu(c_ctx)
    for i in range(DEPTH):
        kind = i % N_MIXERS
        occ = i // N_MIXERS
        need_ctx_out = any(j % N_MIXERS == 0 for j in range(i + 1, DEPTH))
        shift, scale, gate = jnp.split(act_lat @ w_mod[i] + b_mod[i], 3, axis=-1)
        h = x * (1.0 + scale[:, None, :]) + shift[:, None, :]
        col_major = occ % 2 == 1
        if col_major:
            h = _to_col_major(h, rows)
        if kind == 0 or need_ctx_out:
            shift_c, scale_c, gate_c = jnp.split(act_ctx @ w_mod[i] + b_mod[i], 3, axis=-1)
            hc = ctx * (1.0 + scale_c) + shift_c
        if kind == 0:
            y, yc = _rglru_mixer(h, hc, rg_w_in[occ], rg_conv_w[occ], rg_conv_b[occ], rg_w_r[occ],
                                 rg_b_r[occ], rg_w_i[occ], rg_b_i[occ], rg_lambda[occ], rg_w_out[occ],
                                 need_ctx_out)
        else:
            hy_args = (hy_w_in[occ], hy_conv_w[occ], hy_conv_b[occ], hy_f_w1[occ], hy_f_b1[occ],
                       hy_f_w2[occ], hy_f_b2[occ], hy_f_w3[occ], hy_f_freq[occ], hy_d[occ], hy_w_out[occ])
            y = _hyena_mixer(h, *hy_args)
            yc = _hyena_mixer(hc, *hy_args) if need_ctx_out else None
        if col_major:
            y = _to_row_major(y, rows)
        x = _layer_norm(DEEPNORM_ALPHA * x + gate[:, None, :] * y, ln_g[i], ln_b[i])
        if need_ctx_out:
            ctx = _layer_norm(DEEPNORM_ALPHA * ctx + gate_c * yc, ln_g[i], ln_b[i])
    return x
```

```python
import contextlib
import numpy as np
import concourse.bass as bass
import concourse.mybir as mybir
from concourse.bass_utils import run_bass_kernel_spmd

F32 = mybir.dt.float32
BF16 = mybir.dt.bfloat16
ALU = mybir.AluOpType
AF = mybir.ActivationFunctionType
AX = mybir.AxisListType

D = 1024
E = 1536
GRID_W = 64
CTX = 256
DEPTH = 4
ALPHA = (2 * DEPTH) ** 0.25
LN_EPS = 1e-5


class Buf:
    def __init__(self, name):
        self.name = name
        self.w = {}
        self.r = {}


class T:
    def __init__(self, t, name):
        self.t = t
        self.buf = Buf(name)

    def __getitem__(self, k):
        return self.t[k]


class Q:
    def __init__(self, ts, rows_per):
        self.ts = ts
        self.rp = rows_per

    def at(self, r0, r1):
        q = r0 // self.rp
        assert (r1 - 1) // self.rp == q
        t = self.ts[q]
        return t, t.t.ap()[r0 - q * self.rp:r1 - q * self.rp]


class Prog:
    ENG = ("sp", "act", "dve", "pool", "pe")

    def __init__(self, nc):
        self.nc = nc
        self.root = contextlib.ExitStack()
        self.phase = None
        self.sems = {}
        self.latest = {}
        self.ops = {e: [] for e in self.ENG}
        self.waited = {e: {} for e in self.ENG}
        self.nsem = 0

    def sem(self, key):
        if key not in self.sems:
            free = getattr(self, "free_phys", None)
            if free:
                h, base = free.pop()
                self.sems[key] = h
                self.latest[key] = base
            else:
                self.nsem += 1
                self.sems[key] = self.root.enter_context(self.nc.semaphore("s%d" % self.nsem))
            if self.phase is not None and isinstance(key, tuple) and key[1] in getattr(self, "phase_names", ()):
                self.phase_keys.append(key)
        return self.sems[key]

    def _stack(self, persist):
        return self.root if (persist or self.phase is None) else self.phase

    def _uname(self, name):
        self.nname = getattr(self, "nname", 0) + 1
        return "%s_%d" % (name, self.nname)

    def sb(self, name, shape, dt, persist=False):
        name = self._uname(name)
        if not persist and self.phase is not None:
            self.phase_names.add(name)
        t = self._stack(persist).enter_context(self.nc.sbuf_tensor(name, list(shape), dt))
        return T(t, name)

    def ps(self, name, shape, dt, persist=False):
        name = self._uname(name)
        t = self._stack(persist).enter_context(self.nc.psum_tensor(name, list(shape), dt))
        return T(t, name)

    def dram(self, name, shape, dt, kind="Internal"):
        t = self.nc.dram_tensor(name, list(shape), dt, kind=kind)
        return T(t, name)

    def op(self, eng, fn, reads=(), writes=(), dma=None):
        need = {}

        def add(evs):
            for k, v in evs.items():
                if need.get(k, 0) < v:
                    need[k] = v

        for b in reads:
            add(b.buf.w)
        for b in writes:
            add(b.buf.w)
            add(b.buf.r)
        waits = []
        for k, v in need.items():
            if k == eng and eng == "pe":
                continue
            if k not in self.sems:
                continue
            if self.waited[eng].get(k, 0) >= v:
                continue
            self.waited[eng][k] = v
            waits.append((k, v))
        if dma is not None:
            key = ("dma", dma.buf.name)
            inc = 16
        else:
            key = eng
            inc = 1
        self.sem(key)
        val = self.latest.get(key, 0) + inc
        self.latest[key] = val
        self.ops[eng].append((waits, fn, key, inc))
        for b in reads:
            if b.buf.r.get(key, 0) < val:
                b.buf.r[key] = val
        for b in writes:
            if b.buf.w.get(key, 0) < val:
                b.buf.w[key] = val

    def dma(self, q, out_t, out_ap, in_t, in_ap, **kw):
        self.op(q, lambda e: e.dma_start(out=out_ap, in_=in_ap, **kw), reads=[in_t], writes=[out_t], dma=out_t)

    def barrier(self):
        for e in self.ENG:
            waits = []
            for k, v in self.latest.items():
                if self.waited[e].get(k, 0) >= v:
                    continue
                self.waited[e][k] = v
                waits.append((k, v))
            self.ops[e].append((waits, None, None, 0))

    def begin(self):
        self.phase = contextlib.ExitStack()
        self.phase_names = set()
        self.phase_keys = []
        if not hasattr(self, "free_phys"):
            self.free_phys = []

    def end(self):
        self.barrier()
        with self.nc.Block() as block:
            decos = {"sp": block.sync, "act": block.scalar, "dve": block.vector,
                     "pool": block.gpsimd, "pe": block.tensor}
            for en in self.ENG:
                ops = self.ops[en]

                def body(e, ops=ops):
                    for waits, fn, key, inc in ops:
                        for k, v in waits:
                            e.wait_ge(self.sem(k), v)
                        if fn is not None:
                            fn(e).then_inc(self.sem(key), inc)

                decos[en](body)
        self.ops = {e: [] for e in self.ENG}
        for key in self.phase_keys:
            self.free_phys.append((self.sems.pop(key), self.latest.pop(key)))
            for e in self.ENG:
                self.waited[e].pop(key, None)
        self.phase.close()
        self.phase = None


def emit_mod(P, G, l):
    P.begin()
    wm = [P.sb("wm%d" % i, [128, 8, 512], F32) for i in range(2)]
    rep = P.sb("rep", [128, 2, 8, 128], F32)
    bm = P.sb("bm", [1, 3072], F32)
    pm = [P.ps("pmod%d" % i, [128, 512], F32) for i in range(2)]
    wv = G.w_mod.t.ap()[l].rearrange("(k p) n -> p k n", p=128)
    P.dma("act", bm, bm[:], G.wsrc, G.b_mod.t.ap()[l:l + 1, :])
    for j in range(2):
        for k in range(8):
            P.op("dve", lambda e, j=j, k=k: e.tensor_scalar_mul(out=rep[:, j, k, :], in0=G.ones[:], scalar1=G.csil[:, j, k:k + 1]),
                 reads=[G.ones, G.csil], writes=[rep])
    for n in range(6):
        w = wm[n % 2]
        P.dma("sp", w, w[:], G.wsrc, wv[:, :, n * 512:(n + 1) * 512])
        for j in range(2):
            p = pm[j]
            for k in range(8):
                P.op("pe", lambda e, p=p, j=j, k=k, w=w: e.matmul(p[:], lhsT=rep[:, j, k, :], rhs=w[:, k, :],
                                                                start=(k == 0), stop=False),
                     reads=[rep, w], writes=[p])
            P.op("pe", lambda e, p=p, n=n: e.matmul(p[:], lhsT=G.ones[0:1, :], rhs=bm[0:1, n * 512:(n + 1) * 512],
                                                    start=False, stop=True), reads=[G.ones, bm], writes=[p])
            P.op("act", lambda e, p=p, j=j, n=n: e.copy(out=G.modt[j][:, n * 512:(n + 1) * 512], in_=p[:]),
                 reads=[p], writes=[G.modt[j]])
    for j in range(2):
        P.op("dve", lambda e, j=j: e.tensor_scalar_add(out=G.modt[j][:, 1024:2048], in0=G.modt[j][:, 1024:2048], scalar1=1.0),
             reads=[G.modt[j]], writes=[G.modt[j]])
    P.end()


def x_rows(xt, ti, colmajor, L):
    ap = xt.t.ap()
    if not colmajor:
        return ap[ti * 128:(ti + 1) * 128, :]
    rows = L // GRID_W
    per_col = rows // 128
    col, rb = ti // per_col, ti % per_col
    v = ap.rearrange("(r c) d -> c r d", c=GRID_W)
    return v[col, rb * 128:(rb + 1) * 128, :]


def emit_proj(P, G, xsrc, L, j, w_dram, W, MT, outT, pad, colmajor):
    P.begin()
    NK = 8
    wbf = P.sb("wbf", [128, NK, W], BF16)
    wst = [P.sb("wst%d" % i, [128, 1536], F32) for i in range(2)]
    xt = [P.sb("xt%d" % i, [128, D], F32) for i in range(3)]
    hb = [P.sb("hb%d" % i, [128, D], BF16) for i in range(2)]
    hT = [P.sb("hT%d" % i, [128, NK, 512], BF16) for i in range(2)]
    ost = [P.sb("ost%d" % i, [128, 512], F32) for i in range(4)]
    pT = [P.ps("pT%d" % i, [128, NK * 128], BF16) for i in range(2)]
    pm = [P.ps("pm%d" % i, [128, 512], F32) for i in range(4)]
    modt = G.modt[j]
    wv = w_dram
    i = 0
    for k in range(NK):
        for cc in range(W // 1536):
            s = wst[i % 2]
            P.dma("sp" if i % 2 == 0 else "act", s, s[:], G.wsrc, wv[k * 128:(k + 1) * 128, cc * 1536:(cc + 1) * 1536])
            eng = "pool" if i % 2 == 0 else "dve"
            P.op(eng, lambda e, s=s, k=k, cc=cc: e.tensor_copy(out=wbf[:, k, cc * 1536:(cc + 1) * 1536], in_=s[:]),
                 reads=[s], writes=[wbf])
            i += 1
    CH = min(512, L)
    NS = CH // 128
    ntile = L // 128
    nchunk = L // CH
    nmt = W // MT

    def load(ti):
        t = xt[ti % 3]
        P.dma("sp", t, t[:], xsrc, x_rows(xsrc, ti, colmajor, L))

    load(0)
    load(1)
    ev = 0
    for c in range(nchunk):
        h = hT[c % 2]
        for sub in range(NS):
            ti = c * NS + sub
            if ti + 2 < ntile:
                load(ti + 2)
            t = xt[ti % 3]
            b = hb[ti % 2]
            p = pT[ti % 2]
            P.op("dve", lambda e, t=t: e.tensor_tensor(out=t[:], in0=t[:], in1=modt[:, 1024:2048], op=ALU.mult),
                 reads=[t, modt], writes=[t])
            P.op("pool", lambda e, t=t, b=b: e.tensor_tensor(out=b[:], in0=t[:], in1=modt[:, 0:1024], op=ALU.add),
                 reads=[t, modt], writes=[b])
            for k in range(NK):
                P.op("pe", lambda e, p=p, b=b, k=k: e.transpose(out=p[:, k * 128:(k + 1) * 128], in_=b[:, k * 128:(k + 1) * 128],
                                                              identity=G.ident[:]),
                     reads=[b, G.ident], writes=[p])
            P.op("act", lambda e, h=h, p=p, sub=sub: e.copy(out=h[:, :, sub * 128:(sub + 1) * 128],
                                                            in_=p[:].rearrange("p (k t) -> p k t", k=NK)),
                 reads=[p], writes=[h])
        for mt in range(nmt):
            p = pm[ev % 4]
            o = ost[ev % 4]
            for k in range(NK):
                P.op("pe", lambda e, p=p, h=h, k=k, mt=mt: e.matmul(p[:MT, :CH], lhsT=wbf[:, k, mt * MT:(mt + 1) * MT], rhs=h[:, k, :CH],
                                                                  start=(k == 0), stop=(k == NK - 1)),
                     reads=[wbf, h], writes=[p])
            if ev % 2 == 0:
                P.op("act", lambda e, p=p, o=o: e.copy(out=o[:MT, :CH], in_=p[:MT, :CH]), reads=[p], writes=[o])
            else:
                P.op("dve", lambda e, p=p, o=o: e.tensor_copy(out=o[:MT, :CH], in_=p[:MT, :CH]), reads=[p], writes=[o])
            ot_, orow = outT.at(mt * MT, (mt + 1) * MT)
            P.dma("sp" if ev % 2 == 0 else "act", ot_, orow[:, pad + c * CH: pad + (c + 1) * CH], o, o[:MT, :CH])
            ev += 1
    P.end()


def emit_out(P, G, l, xsrc, xdst, L, j, ygT, w_dram, KT, colmajor):
    P.begin()
    NKT = E // KT
    wbf = P.sb("wobf", [128, NKT, D], BF16)
    wst = [P.sb("wost%d" % i, [128, D], F32) for i in range(2)]
    yg = [P.sb("yg%d" % i, [128, NKT, 512], BF16) for i in range(2)]
    xt = [P.sb("xo%d" % i, [128, D], F32) for i in range(3)]
    zt = [P.sb("zt%d" % i, [128, D], F32) for i in range(2)]
    st = [P.sb("st%d" % i, [128, 2, 6], F32) for i in range(2)]
    mv = [P.sb("mv%d" % i, [128, 4], F32) for i in range(2)]
    po = [P.ps("po%d" % i, [128, D], F32) for i in range(2)]
    modt = G.modt[j]
    for k in range(NKT):
        s = wst[k % 2]
        P.dma("sp" if k % 2 == 0 else "act", s, s[:KT, :], G.wsrc, w_dram[k * KT:(k + 1) * KT, :])
        P.op("pool" if k % 2 == 0 else "dve", lambda e, s=s, k=k: e.tensor_copy(out=wbf[:KT, k, :], in_=s[:KT, :]),
             reads=[s], writes=[wbf])
    CH = min(512, L)
    NS = CH // 128
    ntile = L // 128
    nchunk = L // CH
    ygv = ygT.t.ap().rearrange("(k p) t -> p k t", p=KT)
    lng = P.sb("lng", [128, D], F32)
    lnb = P.sb("lnb", [128, D], F32)
    P.dma("sp", lng, lng[:], G.wsrc, G.ln_g.t.ap()[l:l + 1, :].partition_broadcast(128))
    P.dma("act", lnb, lnb[:], G.wsrc, G.ln_b.t.ap()[l:l + 1, :].partition_broadcast(128))

    def loadx(ti):
        t = xt[ti % 3]
        P.dma("sp", t, t[:], xsrc, x_rows(xsrc, ti, colmajor, L))

    def loady(c):
        y = yg[c % 2]
        P.dma("act", y, y[:KT, :, :CH], ygT, ygv[:, :, c * CH:(c + 1) * CH])

    loadx(0)
    loadx(1)
    loady(0)
    for c in range(nchunk):
        if c + 1 < nchunk:
            loady(c + 1)
        y = yg[c % 2]
        for sub in range(NS):
            ti = c * NS + sub
            if ti + 2 < ntile:
                loadx(ti + 2)
            t = xt[ti % 3]
            p = po[ti % 2]
            z = zt[ti % 2]
            s = st[ti % 2]
            m = mv[ti % 2]
            for n in range(2):
                for k in range(NKT):
                    P.op("pe", lambda e, p=p, y=y, k=k, n=n, sub=sub: e.matmul(
                        p[:, n * 512:(n + 1) * 512], lhsT=y[:KT, k, sub * 128:(sub + 1) * 128],
                        rhs=wbf[:KT, k, n * 512:(n + 1) * 512], start=(k == 0), stop=(k == NKT - 1)),
                        reads=[y, wbf], writes=[p])
            P.op("dve", lambda e, p=p, z=z: e.tensor_tensor(out=z[:], in0=p[:], in1=modt[:, 2048:3072], op=ALU.mult),
                 reads=[p, modt], writes=[z])
            P.op("dve", lambda e, t=t, z=z: e.scalar_tensor_tensor(out=z[:], in0=t[:], scalar=ALPHA, in1=z[:],
                                                                   op0=ALU.mult, op1=ALU.add),
                 reads=[t, z], writes=[z])
            for n in range(2):
                P.op("dve", lambda e, s=s, z=z, n=n: e.bn_stats(out=s[:, n, :], in_=z[:, n * 512:(n + 1) * 512]),
                     reads=[z], writes=[s])
            P.op("dve", lambda e, s=s, m=m: e.bn_aggr(out=m[:, 0:2], in_=s[:].rearrange("p a b -> p (a b)")),
                 reads=[s], writes=[m])
            P.op("act", lambda e, m=m: e.activation(out=m[:, 2:3], in_=m[:, 1:2], func=AF.Sqrt, bias=G.eps[:, 0:1], scale=1.0),
                 reads=[m, G.eps], writes=[m])
            P.op("dve", lambda e, m=m: e.reciprocal(out=m[:, 3:4], in_=m[:, 2:3]), reads=[m], writes=[m])
            P.op("dve", lambda e, m=m, z=z: e.tensor_scalar(out=z[:], in0=z[:], scalar1=m[:, 0:1], scalar2=m[:, 3:4],
                                                            op0=ALU.subtract, op1=ALU.mult), reads=[m, z], writes=[z])
            P.op("pool", lambda e, z=z: e.tensor_tensor(out=z[:], in0=z[:], in1=lng[:], op=ALU.mult),
                 reads=[z, lng], writes=[z])
            P.op("pool", lambda e, z=z: e.tensor_tensor(out=z[:], in0=z[:], in1=lnb[:], op=ALU.add),
                 reads=[z, lnb], writes=[z])
            P.dma("sp", xdst, x_rows(xdst, ti, colmajor, L), z, z[:])
    P.end()


def emit_rg(P, G, occ, uT, ucT, pad, L, ygT, ygcT, need_ctx):
    P.begin()
    TC = 1024
    NT = E // 96
    par = P.sb("rgpar", [96, NT, 16], F32)
    cst = P.sb("rgcst", [96, NT, 4], F32)
    wg32 = [P.sb("wg32_%d" % i, [96, 4, 96], F32) for i in range(2)]
    wg = [P.sb("wg%d" % i, [96, 4, 96], BF16) for i in range(2)]
    ut = [P.sb("ut%d" % i, [96, TC + 3], F32) for i in range(2)]
    gt = [P.sb("gt%d" % i, [96, TC], F32) for i in range(2)]
    xcs = [P.sb("xcs%d" % i, [96, TC], F32) for i in range(2)]
    xb = [P.sb("xb%d" % i, [96, TC], BF16) for i in range(2)]
    hf = P.sb("hf", [96, L], BF16)
    hfc = P.sb("hfc", [96, CTX], F32)
    rr = [P.sb("rr%d" % i, [96, TC], F32) for i in range(2)]
    ii = [P.sb("ii%d" % i, [96, TC], F32) for i in range(2)]
    aa = [P.sb("aa%d" % i, [96, TC], F32) for i in range(2)]
    mm = [P.sb("mm%d" % i, [96, TC], F32) for i in range(2)]
    bx = [P.sb("bx%d" % i, [96, TC], F32) for i in range(2)]
    hh = [P.sb("hh%d" % i, [96, TC], F32) for i in range(3)]
    yy = [P.sb("yy%d" % i, [96, TC], BF16) for i in range(2)]
    zero = P.sb("zero1", [96, 1], F32)
    pr = [P.ps("pr%d" % i, [96, TC], F32) for i in range(2)]
    P.op("dve", lambda e: e.memset(zero[:], 0.0), writes=[zero])
    P.dma("sp", par, par[:], G.rgpar, G.rgpar.t.ap()[occ])
    P.op("act", lambda e: e.activation(out=cst[:, :, 0:2], in_=par[:, :, 9:11], func=AF.Exp, scale=-1.0), reads=[par], writes=[cst])
    P.op("act", lambda e: e.activation(out=cst[:, :, 0:2], in_=cst[:, :, 0:2], func=AF.Ln, bias=1.0), reads=[cst], writes=[cst])
    P.op("dve", lambda e: e.tensor_scalar_mul(out=cst[:, :, 2:4], in0=cst[:, :, 0:2], scalar1=-16.0), reads=[cst], writes=[cst])
    P.op("dve", lambda e: e.tensor_scalar_mul(out=cst[:, :, 0:2], in0=cst[:, :, 0:2], scalar1=-8.0), reads=[cst], writes=[cst])
    cnt = {"c": 0, "g": 0}

    def rev(ap, d):
        return ap if d == 0 else ap[:, ::-1]

    def chunk(tj, w, src, col0, n, d, hinit):
        ci = cnt["c"]
        cnt["c"] += 1
        u, xct, xbt, p = ut[ci % 2], xcs[ci % 2], xb[ci % 2], pr[ci % 2]
        r, i_, a, m, b = rr[ci % 2], ii[ci % 2], aa[ci % 2], mm[ci % 2], bx[ci % 2]
        h = hh[ci % 3]
        st_, srow = src.at(tj * 96, (tj + 1) * 96)
        P.dma("sp", u, u[:, :n + 3], st_, srow[:, col0:col0 + n + 3])
        P.op("act", lambda e: e.activation(out=xct[:, :n], in_=u[:, 0:n], func=AF.Identity,
                                           scale=par[:, tj, 0:1], bias=par[:, tj, 4:5]), reads=[u, par], writes=[xct])
        for k in range(1, 4):
            P.op("dve", lambda e, k=k: e.scalar_tensor_tensor(out=xct[:, :n], in0=u[:, k:k + n], scalar=par[:, tj, k:k + 1],
                                                             in1=xct[:, :n], op0=ALU.mult, op1=ALU.add),
                 reads=[u, par, xct], writes=[xct])
        P.op("pool", lambda e: e.tensor_copy(out=xbt[:, :n], in_=xct[:, :n]), reads=[xct], writes=[xbt])
        for g in range(2):
            for s0 in range(0, n, 512):
                s1 = min(n, s0 + 512)
                P.op("pe", lambda e, g=g, s0=s0, s1=s1: e.matmul(p[:, s0:s1], lhsT=w[:, 2 * d + g, :], rhs=xbt[:, s0:s1],
                                                               start=True, stop=True), reads=[w, xbt], writes=[p])
            dst = r if g == 0 else i_
            bcol = par[:, tj, 5 + 2 * d + g: 6 + 2 * d + g]
            P.op("act", lambda e, dst=dst, bcol=bcol: e.activation(out=dst[:, :n], in_=p[:, :n], func=AF.Sigmoid, bias=bcol),
                 reads=[p, par], writes=[dst])
        P.op("act", lambda e: e.activation(out=a[:, :n], in_=rev(r[:, :n], d), func=AF.Exp, scale=cst[:, tj, d:d + 1]),
             reads=[r, cst], writes=[a])
        P.op("act", lambda e: e.activation(out=m[:, :n], in_=rev(r[:, :n], d), func=AF.Exp, scale=cst[:, tj, 2 + d:3 + d]),
             reads=[r, cst], writes=[m])
        P.op("act", lambda e: e.activation(out=m[:, :n], in_=m[:, :n], func=AF.Sqrt, scale=-1.0, bias=1.0), reads=[m], writes=[m])
        P.op("pool", lambda e: e.tensor_tensor(out=i_[:, :n], in0=i_[:, :n], in1=xct[:, :n], op=ALU.mult), reads=[i_, xct], writes=[i_])
        P.op("pool", lambda e: e.tensor_tensor(out=b[:, :n], in0=rev(i_[:, :n], d), in1=m[:, :n], op=ALU.mult), reads=[i_, m], writes=[b])
        init_t, init_ap = hinit
        P.op("dve", lambda e: e.tensor_tensor_scan(out=h[:, :n], data0=a[:, :n], data1=b[:, :n], initial=init_ap,
                                                   op0=ALU.mult, op1=ALU.add), reads=[a, b, init_t], writes=[h])
        return h, b, (h, h[:, n - 1:n])

    def gate_out(tj, h, b, n, hfwd_ap, hfwd_t, gsrc, gcol0, dst, dcol0):
        gi = cnt["g"]
        cnt["g"] += 1
        g, y = gt[gi % 2], yy[gi % 2]
        gs_, grow = gsrc.at(E + tj * 96, E + (tj + 1) * 96)
        P.dma("act", g, g[:, :n], gs_, grow[:, gcol0:gcol0 + n])
        P.op("act", lambda e: e.activation(out=g[:, :n], in_=g[:, :n], func=AF.Silu), reads=[g], writes=[g])
        P.op("dve", lambda e: e.tensor_tensor(out=b[:, :n], in0=h[:, :n][:, ::-1], in1=hfwd_ap, op=ALU.add),
             reads=[h, hfwd_t], writes=[b])
        P.op("pool", lambda e: e.tensor_tensor(out=y[:, :n], in0=b[:, :n], in1=g[:, :n], op=ALU.mult), reads=[b, g], writes=[y])
        P.dma("act", dst, dst.t.ap()[tj * 96:(tj + 1) * 96, dcol0:dcol0 + n], y, y[:, :n])

    for tj in range(NT):
        w32, w = wg32[tj % 2], wg[tj % 2]
        P.dma("act", w32, w32[:], G.rgw, G.rgw.t.ap()[occ, tj])
        P.op("pool", lambda e, w=w, w32=w32: e.tensor_copy(out=w[:], in_=w32[:]), reads=[w32], writes=[w])
        z0 = (zero, zero[:, 0:1])
        h, b, carry = chunk(tj, w, ucT, 0, CTX, 0, z0)
        if need_ctx:
            P.op("pool", lambda e, h=h: e.tensor_copy(out=hfc[:], in_=h[:, :CTX]), reads=[h], writes=[hfc])
        for c in range(L // TC):
            h, b, carry = chunk(tj, w, uT, c * TC, TC, 0, carry)
            P.op("pool", lambda e, h=h, c=c: e.tensor_copy(out=hf[:, c * TC:(c + 1) * TC], in_=h[:]), reads=[h], writes=[hf])
        h, b, carry = chunk(tj, w, ucT, 0, CTX, 1, z0)
        if need_ctx:
            gate_out(tj, h, b, CTX, hfc[:], hfc, ucT, pad, ygcT, 0)
        for c in range(L // TC - 1, -1, -1):
            h, b, carry = chunk(tj, w, uT, c * TC, TC, 1, carry)
            gate_out(tj, h, b, TC, hf[:, c * TC:(c + 1) * TC], hf, uT, pad + c * TC, ygT, c * TC)
    P.end()


PI = float(np.pi)
SKIP_CTX_HY = False
CBL = 8
CBC = 64
FH = 64
NEMB = 17


def emit_hyfilt(P, G, occ, Lf, L, zt_d, ztr_d, trow_d, tapsd, normd):
    P.begin()
    NCT = E // 128
    w1 = P.sb("fw1", [NEMB, FH], F32)
    w2 = P.sb("fw2", [FH, FH], F32)
    fp = P.sb("fpar", [FH, 4], F32)
    fb = P.sb("fb", [FH, 2], F32)
    w3s = [P.sb("w3s%d" % i, [FH, 1536], F32) for i in range(2)]
    w3 = P.sb("w3bf", [FH, 4 * E], BF16)
    hid = [P.sb("hid%d" % i, [FH, Lf], BF16) for i in range(2)]
    zc = [P.sb("zc%d" % i, [NEMB, 512], F32) for i in range(2)]
    vv = [P.sb("vv%d" % i, [FH, 512], F32) for i in range(2)]
    mk = [P.sb("mk%d" % i, [FH, 512], F32) for i in range(2)]
    h1 = [P.sb("h1%d" % i, [FH, 512], F32) for i in range(2)]
    nd = P.sb("negd", [128, NCT], F32)
    tw = [P.sb("tw%d" % i, [128, 512], F32) for i in range(2)]
    win = [P.sb("win%d" % i, [128, 512], F32) for i in range(2)]
    tp = [P.sb("tp%d" % i, [128, 512], BF16) for i in range(4)]
    junk = P.sb("junk", [128, 512], BF16)
    ac = [P.sb("ac%d" % i, [128, 1], F32) for i in range(4)]
    nacc = P.sb("nacc", [128, 2, NCT], F32)
    pa = [P.ps("pfa%d" % i, [FH, 512], F32) for i in range(2)]
    pb = [P.ps("pfb%d" % i, [128, 512], F32) for i in range(4)]
    P.dma("sp", w1, w1[:], G.wsrc, G.hy_f_w1.t.ap()[occ])
    P.dma("sp", w2, w2[:], G.wsrc, G.hy_f_w2.t.ap()[occ])
    P.dma("sp", fp, fp[:, 0:3], G.wsrc, G.hyfp.t.ap()[occ])
    P.dma("sp", nd, nd[:], G.wsrc, G.negd.t.ap())
    P.op("dve", lambda e: e.memset(nacc[:], 0.0), writes=[nacc])
    P.op("dve", lambda e: e.tensor_scalar_mul(out=fb[:, 0:1], in0=fp[:, 1:2], scalar1=fp[:, 0:1]), reads=[fp], writes=[fb])
    P.op("dve", lambda e: e.tensor_scalar_mul(out=fb[:, 1:2], in0=fp[:, 2:3], scalar1=fp[:, 0:1]), reads=[fp], writes=[fb])
    for i in range(4):
        s = w3s[i % 2]
        P.dma("act", s, s[:], G.wsrc, G.hy_f_w3.t.ap()[occ][:, i * 1536:(i + 1) * 1536])
        P.op("pool", lambda e, s=s, i=i: e.tensor_copy(out=w3[:, i * 1536:(i + 1) * 1536], in_=s[:]), reads=[s], writes=[w3])

    def sin_layer(ps, n, bcol, out_ap, out_t, v, m):
        P.op("act", lambda e: e.activation(out=v[:, :n], in_=ps[:, :n], func=AF.Identity, scale=fp[:, 0:1], bias=bcol),
             reads=[ps, fp, fb], writes=[v])
        for it in range(2):
            for thr, add, cmp in ((PI, -2 * PI, ALU.is_gt), (-PI, 2 * PI, ALU.is_lt)):
                P.op("dve", lambda e, thr=thr, add=add, cmp=cmp: e.tensor_scalar(out=m[:, :n], in0=v[:, :n], scalar1=thr, scalar2=add,
                                                                               op0=cmp, op1=ALU.mult), reads=[v], writes=[m])
                P.op("dve", lambda e: e.tensor_tensor(out=v[:, :n], in0=v[:, :n], in1=m[:, :n], op=ALU.add), reads=[v, m], writes=[v])
        P.op("act", lambda e: e.activation(out=out_ap, in_=v[:, :n], func=AF.Sin), reads=[v], writes=[out_t])

    it = 0
    for dr, ztab in ((0, zt_d), (1, ztr_d)):
        for c0 in range(0, Lf, 512):
            n = min(512, Lf - c0)
            z, v, m, hh1 = zc[it % 2], vv[it % 2], mk[it % 2], h1[it % 2]
            p1 = pa[it % 2]
            it += 1
            P.dma("sp", z, z[:, :n], G.wsrc, ztab.t.ap()[:, c0:c0 + n])
            P.op("pe", lambda e, p1=p1, z=z, n=n: e.matmul(p1[:, :n], lhsT=w1[:], rhs=z[:, :n], start=True, stop=True),
                 reads=[w1, z], writes=[p1])
            sin_layer(p1, n, fb[:, 0:1], hh1[:, :n], hh1, v, m)
            P.op("pe", lambda e, p1=p1, hh1=hh1, n=n: e.matmul(p1[:, :n], lhsT=w2[:], rhs=hh1[:, :n], start=True, stop=True),
                 reads=[w2, hh1], writes=[p1])
            sin_layer(p1, n, fb[:, 1:2], hid[dr][:, c0:c0 + n], hid[dr], v, m)
    it = 0
    for ct in range(NCT):
        for dr in range(2):
            for c0 in range(0, Lf, 512):
                n = min(512, Lf - c0)
                t_, w_ = tw[it % 2], win[it % 2]
                P.dma("sp", t_, t_[:, :n], G.wsrc, trow_d.t.ap()[dr, c0:c0 + n].partition_broadcast(128))
                P.op("act", lambda e, t_=t_, w_=w_, n=n, ct=ct: e.activation(out=w_[:, :n], in_=t_[:, :n], func=AF.Exp, scale=nd[:, ct:ct + 1]),
                     reads=[t_, nd], writes=[w_])
                for o in range(2):
                    k = (it * 2 + o) % 4
                    p, tq, a_ = pb[k], tp[k], ac[k]
                    col = o * 2 * E + dr * E + ct * 128
                    P.op("pe", lambda e, p=p, col=col, c0=c0, n=n, dr=dr: e.matmul(p[:, :n], lhsT=w3[:, col:col + 128], rhs=hid[dr][:, c0:c0 + n],
                                                                          start=True, stop=True), reads=[w3, hid[dr]], writes=[p])
                    P.op("dve", lambda e, p=p, tq=tq, w_=w_, n=n: e.scalar_tensor_tensor(out=tq[:, :n], in0=w_[:, :n], scalar=0.05, in1=p[:, :n],
                                                                                      op0=ALU.add, op1=ALU.mult), reads=[w_, p], writes=[tq])
                    if dr == 1 and c0 == 0:
                        P.op("dve", lambda e, tq=tq: e.memset(tq[:, 0:1], 0.0), reads=[tq], writes=[tq])
                    P.op("act", lambda e, tq=tq, a_=a_, n=n: e.activation(out=junk[:, :n], in_=tq[:, :n], func=AF.Abs, accum_out=a_[:, 0:1]),
                         reads=[tq], writes=[junk, a_])
                    P.op("dve", lambda e, a_=a_, o=o, ct=ct: e.tensor_tensor(out=nacc[:, o, ct:ct + 1], in0=nacc[:, o, ct:ct + 1], in1=a_[:, 0:1], op=ALU.add),
                         reads=[a_, nacc], writes=[nacc])
                    pos = c0 if dr == 0 else 2 * L - Lf + c0
                    P.dma("act", tapsd, tapsd.t.ap()[o, ct * 128:(ct + 1) * 128, pos:pos + n], tq, tq[:, :n])
                it += 1
    P.dma("sp", normd, normd.t.ap().rearrange("o (c p) -> p o c", p=128), nacc, nacc[:], allow_slow_non_contiguous=True)
    if getattr(G, "dbg_hid", None) is not None:
        for dr in range(2):
            P.dma("sp", G.dbg_hid, G.dbg_hid.t.ap()[dr], hid[dr], hid[dr][:])
        P.dma("sp", G.dbg_nacc, G.dbg_nacc.t.ap(), nacc, nacc[:])
    P.end()


def emit_hyconv(P, G, occ, uT, L, Pn, CB, tb_d, tf_d, hypar_d, tapsd, normd, ygT, nb_limit=None):
    assert L == Pn * Pn
    GS = 512 // Pn
    P.begin()
    NB = E // CB
    tb = P.sb("tb16", [Pn, 13 * Pn], BF16)
    tf = P.sb("tf32", [Pn, 4 * Pn], F32)
    FA0, FA1 = tb[:, 0:2 * Pn], tb[:, 2 * Pn:4 * Pn]
    FCre, FCim, FCimN = tb[:, 4 * Pn:5 * Pn], tb[:, 5 * Pn:6 * Pn], tb[:, 6 * Pn:7 * Pn]
    FCc, FCc2 = tb[:, 7 * Pn:9 * Pn], tb[:, 9 * Pn:11 * Pn]
    CA, SA = tb[:, 11 * Pn:12 * Pn], tb[:, 12 * Pn:13 * Pn]

    def bc(ap2d):
        return ap2d.unsqueeze(1).to_broadcast([Pn, CB, Pn])

    TW = (bc(tf[:, 0:Pn]), bc(tf[:, Pn:2 * Pn]))
    TWT = (bc(tf[:, 2 * Pn:3 * Pn]), bc(tf[:, 3 * Pn:4 * Pn]))
    P.dma("sp", tb, tb[:], G.wsrc, tb_d.t.ap())
    P.dma("sp", tf, tf[:], G.wsrc, tf_d.t.ap())
    xs = [P.sb("xs%d" % i, [Pn, 3 * CB, Pn], F32) for i in range(3)]
    vx = P.sb("vx", [Pn, 3 * CB, Pn], F32)
    tca = P.sb("tca", [Pn, 3 * CB, Pn], F32)
    tcb = P.sb("tcb", [Pn, 3 * CB, Pn], F32)
    vb = P.sb("vb", [Pn, CB, Pn], BF16)
    gt = P.sb("gth", [Pn, CB, Pn], F32)
    kt = [P.sb("kt%d" % i, [Pn, CB, 2, Pn], BF16) for i in range(2)]
    KF = [P.sb("KF%d" % i, [Pn, 2, CB, Pn], F32) for i in range(2)]
    hp = P.sb("hp", [Pn, 14 * CB], F32)
    rn = P.sb("rn", [Pn, 2, CB], F32)
    Asb = [P.sb("Asb%d" % i, [Pn, 2 * CB * Pn], F32) for i in range(2)]
    t1 = [P.sb("t1_%d" % i, [Pn, CB, Pn], F32) for i in range(1)]
    t2 = [P.sb("t2_%d" % i, [Pn, CB, Pn], F32) for i in range(1)]
    t3 = [P.sb("t3_%d" % i, [Pn, CB, Pn], F32) for i in range(1)]
    t4 = [P.sb("t4_%d" % i, [Pn, CB, Pn], F32) for i in range(1)]
    Bre = [P.sb("Bre%d" % i, [Pn, CB, Pn], BF16) for i in range(2)]
    Bim = [P.sb("Bim%d" % i, [Pn, CB, Pn], BF16) for i in range(2)]
    zz = P.sb("zz", [Pn, CB, Pn], F32)
    zzb = P.sb("zzb", [Pn, CB, Pn], BF16)
    tcm = P.sb("tcm", [Pn, CB, Pn], F32)
    ygo = P.sb("ygo", [Pn, CB, Pn], BF16)
    pA = P.ps("pA", [Pn, 2 * CB * Pn], F32)
    pC = P.ps("pC", [Pn, 2 * CB * Pn], F32)
    cnt = {"x": 0}
    hv = hypar_d.t.ap()

    def cmul(src, tab_re, tab_im, ore, oim, srcv):
        k = cnt["x"] % 2
        cnt["x"] += 1
        a, b_, c_, d_ = t1[0], t2[0], t3[0], t4[0]
        sre, sim = srcv(0), srcv(1)
        P.op("dve", lambda e: e.tensor_tensor(out=a[:], in0=sre, in1=tab_re, op=ALU.mult), reads=[src] + tab_t, writes=[a])
        P.op("dve", lambda e: e.tensor_tensor(out=b_[:], in0=sim, in1=tab_im, op=ALU.mult), reads=[src] + tab_t, writes=[b_])
        P.op("pool", lambda e: e.tensor_tensor(out=c_[:], in0=sre, in1=tab_im, op=ALU.mult), reads=[src] + tab_t, writes=[c_])
        P.op("pool", lambda e: e.tensor_tensor(out=d_[:], in0=sim, in1=tab_re, op=ALU.mult), reads=[src] + tab_t, writes=[d_])
        P.op("dve", lambda e: e.tensor_tensor(out=ore[0], in0=a[:], in1=b_[:], op=ALU.subtract), reads=[a, b_], writes=[ore[1]])
        P.op("pool", lambda e: e.tensor_tensor(out=oim[0], in0=c_[:], in1=d_[:], op=ALU.add), reads=[c_, d_], writes=[oim[1]])

    tab_t = [tf]

    def fwd(lhs_list):
        k = cnt["x"] % 2
        for cb in range(CB):
            ents = lhs_list[cb]
            for j, (lap, tab, tt) in enumerate(ents):
                P.op("pe", lambda e, cb=cb, lap=lap, tab=tab, j=j, nj=len(ents): e.matmul(
                    pA[:, cb * 2 * Pn:(cb + 1) * 2 * Pn], lhsT=lap, rhs=tab, start=(j == 0), stop=(j == nj - 1)),
                    reads=[tt, tb], writes=[pA])
        A_ = Asb[k]
        P.op("act", lambda e: e.copy(out=A_[:], in_=pA[:]), reads=[pA], writes=[A_])
        Av = A_[:].rearrange("p (c r k) -> p c r k", c=CB, r=2)
        br, bi = Bre[k], Bim[k]
        cmul(A_, TW[0], TW[1], (br[:], br), (bi[:], bi), lambda r_: Av[:, :, r_, :])
        for g in range(CB // GS):
            rb = br[:, g * GS:(g + 1) * GS, :].rearrange("p c k -> p (c k)")
            ib = bi[:, g * GS:(g + 1) * GS, :].rearrange("p c k -> p (c k)")
            ore = pC[:, g * 512:(g + 1) * 512]
            oim = pC[:, CB * Pn + g * 512:CB * Pn + (g + 1) * 512]
            P.op("pe", lambda e, ore=ore, rb=rb: e.matmul(ore, lhsT=FCre, rhs=rb, start=True, stop=False), reads=[tb, br], writes=[pC])
            P.op("pe", lambda e, ore=ore, ib=ib: e.matmul(ore, lhsT=FCimN, rhs=ib, start=False, stop=True), reads=[tb, bi], writes=[pC])
            P.op("pe", lambda e, oim=oim, ib=ib: e.matmul(oim, lhsT=FCre, rhs=ib, start=True, stop=False), reads=[tb, bi], writes=[pC])
            P.op("pe", lambda e, oim=oim, rb=rb: e.matmul(oim, lhsT=FCim, rhs=rb, start=False, stop=True), reads=[tb, br], writes=[pC])
        Y_ = Asb[(k + 1) % 2]
        P.op("act", lambda e: e.copy(out=Y_[:], in_=pC[:]), reads=[pC], writes=[Y_])
        return Y_

    def inv(Y_, o):
        k = cnt["x"] % 2
        Yv = Y_[:].rearrange("p (r c k) -> p r c k", r=2, c=CB)
        zr, zi = Bre[k], Bim[k]
        tabs_save = list(tab_t)
        tab_t[:] = [KF[o]]
        cmul(Y_, KF[o][:, 0, :, :], KF[o][:, 1, :, :], (zr[:], zr), (zi[:], zi), lambda r_: Yv[:, r_, :, :])
        tab_t[:] = tabs_save
        for cb in range(CB):
            P.op("pe", lambda e, cb=cb: e.matmul(pA[:, cb * 2 * Pn:(cb + 1) * 2 * Pn], lhsT=zr[:, cb, :], rhs=FCc, start=True, stop=False),
                 reads=[zr, tb], writes=[pA])
            P.op("pe", lambda e, cb=cb: e.matmul(pA[:, cb * 2 * Pn:(cb + 1) * 2 * Pn], lhsT=zi[:, cb, :], rhs=FCc2, start=False, stop=True),
                 reads=[zi, tb], writes=[pA])
        k2 = cnt["x"] % 2
        A_ = Asb[k2]
        P.op("act", lambda e: e.copy(out=A_[:], in_=pA[:]), reads=[pA], writes=[A_])
        Av = A_[:].rearrange("p (c r k) -> p c r k", c=CB, r=2)
        cr, ci = Bre[k2], Bim[k2]
        cmul(A_, TWT[0], TWT[1], (cr[:], cr), (ci[:], ci), lambda r_: Av[:, :, r_, :])
        for g in range(CB // GS):
            rb = cr[:, g * GS:(g + 1) * GS, :].rearrange("p c k -> p (c k)")
            ib = ci[:, g * GS:(g + 1) * GS, :].rearrange("p c k -> p (c k)")
            oy = pC[:, g * 512:(g + 1) * 512]
            P.op("pe", lambda e, oy=oy, rb=rb: e.matmul(oy, lhsT=CA, rhs=rb, start=True, stop=False), reads=[tb, cr], writes=[pC])
            P.op("pe", lambda e, oy=oy, ib=ib: e.matmul(oy, lhsT=SA, rhs=ib, start=False, stop=True), reads=[tb, ci], writes=[pC])

    def bcw(ap2d, n):
        return ap2d.unsqueeze(2).to_broadcast([Pn, n, Pn])

    yv = pC[:, 0:CB * Pn].rearrange("p (c k) -> p c k", c=CB)
    for bi_ in range(NB if nb_limit is None else nb_limit):
        c0 = bi_ * CB
        P.dma("sp", hp, hp[:], G.wsrc, hv[occ, bi_, :].partition_broadcast(Pn))
        P.dma("sp", rn, rn[:], normd, normd.t.ap()[:, c0:c0 + CB].partition_broadcast(Pn))
        for s in range(3):
            for q in range(3):
                ut_, urow = uT.at(q * E + c0, q * E + c0 + CB)
                P.dma("sp" if (s + q) % 2 == 0 else "act", xs[s], xs[s][:, q * CB:(q + 1) * CB, :], ut_,
                      urow[:, s:s + L].rearrange("c (a b) -> a c b", b=Pn))
        ut_, urow = uT.at(3 * E + c0, 3 * E + c0 + CB)
        P.dma("act", gt, gt[:], ut_, urow[:, 1:1 + L].rearrange("c (a b) -> a c b", b=Pn))
        for o in range(2):
            P.dma("sp" if o == 0 else "act", kt[o], kt[o][:], tapsd,
                  tapsd.t.ap()[o, c0:c0 + CB, :].rearrange("c (h a b) -> a c h b", h=2, b=Pn))
        P.op("dve", lambda e: e.reciprocal(out=rn[:], in_=rn[:]), reads=[rn], writes=[rn])
        P.op("dve", lambda e: e.tensor_tensor(out=vx[:], in0=xs[0][:], in1=bcw(hp[:, 0:3 * CB], 3 * CB), op=ALU.mult), reads=[xs[0], hp], writes=[vx])
        P.op("pool", lambda e: e.tensor_tensor(out=tca[:], in0=xs[1][:], in1=bcw(hp[:, 3 * CB:6 * CB], 3 * CB), op=ALU.mult), reads=[xs[1], hp], writes=[tca])
        P.op("pool", lambda e: e.tensor_tensor(out=tcb[:], in0=xs[2][:], in1=bcw(hp[:, 6 * CB:9 * CB], 3 * CB), op=ALU.mult), reads=[xs[2], hp], writes=[tcb])
        P.op("dve", lambda e: e.tensor_tensor(out=vx[:], in0=vx[:], in1=tca[:], op=ALU.add), reads=[vx, tca], writes=[vx])
        P.op("pool", lambda e: e.tensor_tensor(out=tcb[:], in0=tcb[:], in1=bcw(hp[:, 9 * CB:12 * CB], 3 * CB), op=ALU.add), reads=[tcb, hp], writes=[tcb])
        P.op("dve", lambda e: e.tensor_tensor(out=vx[:], in0=vx[:], in1=tcb[:], op=ALU.add), reads=[vx, tcb], writes=[vx])
        P.op("act", lambda e: e.copy(out=vb[:], in_=vx[:, 0:CB, :]), reads=[vx], writes=[vb])
        P.op("act", lambda e: e.activation(out=gt[:], in_=gt[:], func=AF.Silu), reads=[gt], writes=[gt])
        for o in range(2):
            Y_ = fwd([[(kt[o][:, cb, 0, :], FA0, kt[o]), (kt[o][:, cb, 1, :], FA1, kt[o])] for cb in range(CB)])
            P.op("dve", lambda e, Y_=Y_, o=o: e.tensor_tensor(
                out=KF[o][:], in0=Y_[:].rearrange("p (r c k) -> p r c k", r=2, c=CB),
                in1=rn[:, o, :].unsqueeze(1).unsqueeze(3).to_broadcast([Pn, 2, CB, Pn]), op=ALU.mult), reads=[Y_, rn], writes=[KF[o]])
        Y_ = fwd([[(vb[:, cb, :], FA0, vb)] for cb in range(CB)])
        inv(Y_, 0)
        P.op("pool", lambda e: e.tensor_tensor(out=tcm[:], in0=vx[:, 0:CB, :], in1=bcw(hp[:, 12 * CB:13 * CB], CB), op=ALU.mult), reads=[vx, hp], writes=[tcm])
        P.op("dve", lambda e: e.tensor_tensor(out=tcm[:], in0=tcm[:], in1=yv, op=ALU.add), reads=[tcm, pC], writes=[tcm])
        P.op("pool", lambda e: e.tensor_tensor(out=zz[:], in0=tcm[:], in1=vx[:, CB:2 * CB, :], op=ALU.mult), reads=[tcm, vx], writes=[zz])
        P.op("act", lambda e: e.copy(out=zzb[:], in_=zz[:]), reads=[zz], writes=[zzb])
        Y_ = fwd([[(zzb[:, cb, :], FA0, zzb)] for cb in range(CB)])
        inv(Y_, 1)
        P.op("pool", lambda e: e.tensor_tensor(out=tcm[:], in0=zz[:], in1=bcw(hp[:, 13 * CB:14 * CB], CB), op=ALU.mult), reads=[zz, hp], writes=[tcm])
        P.op("dve", lambda e: e.tensor_tensor(out=tcm[:], in0=tcm[:], in1=yv, op=ALU.add), reads=[tcm, pC], writes=[tcm])
        P.op("pool", lambda e: e.tensor_tensor(out=tcm[:], in0=tcm[:], in1=vx[:, 2 * CB:3 * CB, :], op=ALU.mult), reads=[tcm, vx], writes=[tcm])
        P.op("pool", lambda e: e.tensor_tensor(out=ygo[:], in0=tcm[:], in1=gt[:], op=ALU.mult), reads=[tcm, gt], writes=[ygo])
        P.dma("act", ygT, ygT.t.ap()[c0:c0 + CB, :].rearrange("c (a b) -> a c b", b=Pn), ygo, ygo[:])
    P.end()


class G_:
    pass


def emit_setup(P, G, L):
    P.begin()
    zt = P.sb("zpad", [128, 4], F32)
    P.op("dve", lambda e: e.memset(G.ones[:], 1.0), writes=[G.ones])
    P.op("dve", lambda e: e.memset(G.eps[:], LN_EPS), writes=[G.eps])
    P.op("pool", lambda e: e.memset(zt[:], 0.0), writes=[zt])
    P.dma("sp", G.ident, G.ident[:], G.wsrc, G.identd.t.ap())
    P.dma("sp", G.csil, G.csil[:], G.wsrc, G.cc.t.ap())
    P.op("act", lambda e: e.activation(out=G.csil[:], in_=G.csil[:], func=AF.Silu), reads=[G.csil], writes=[G.csil])
    for t_, n in [(t, L) for t in G.uT.ts] + [(t, CTX) for t in G.ucT.ts]:
        rows = t_.t.ap().shape[0]
        for r0 in range(0, rows, 128):
            P.dma("sp", t_, t_.t.ap()[r0:r0 + 128, 0:1], zt, zt[:, 0:1], allow_slow_non_contiguous=True)
            P.dma("act", t_, t_.t.ap()[r0:r0 + 128, 1 + n:3 + n], zt, zt[:, 0:2], allow_slow_non_contiguous=True)
    P.end()


def build_nc(L, nlayers=DEPTH, debug=False):
    nc = bass.Bass("TRN2", target_bir_lowering=False)
    P = Prog(nc)
    G = G_()

    def inp(name, shape, dt=F32):
        return P.dram(name, shape, dt, kind="ExternalInput")

    G.x_in = inp("x", [L, D])
    G.ctx_in = inp("ctx", [CTX, D])
    G.cc = inp("cc", [128, 2, 8])
    G.identd = inp("ident", [128, 128], BF16)
    G.w_mod = inp("w_mod", [DEPTH, D, 3 * D])
    G.b_mod = inp("b_mod", [DEPTH, 3 * D])
    G.ln_g = inp("ln_g", [DEPTH, D])
    G.ln_b = inp("ln_b", [DEPTH, D])
    G.rg_w_in = inp("rg_w_in", [2, D, 2 * E])
    G.rgpar = inp("rgpar", [2, 96, E // 96, 16])
    G.rgw = inp("rgw", [2, E // 96, 96, 4, 96])
    G.rg_w_out = inp("rg_w_out", [2, E, D])
    G.hy_w_in = inp("hy_w_in", [2, D, 4 * E])
    G.hy_w_out = inp("hy_w_out", [2, E, D])
    G.hy_f_w1 = inp("hy_f_w1", [2, NEMB, FH])
    G.hy_f_w2 = inp("hy_f_w2", [2, FH, FH])
    G.hy_f_w3 = inp("hy_f_w3", [2, FH, 4 * E])
    G.hyfp = inp("hyfp", [2, FH, 3])
    G.hypar = inp("hypar", [2, E // CBL, 14 * CBL])
    G.hyparc = inp("hyparc", [2, E // CBC, 14 * CBC])
    G.tb16c = inp("tb16c", [16, 13 * 16], BF16)
    G.tf32c = inp("tf32c", [16, 4 * 16])
    G.ztc = inp("ztc", [NEMB, CTX])
    G.ztrc = inp("ztrc", [NEMB, CTX])
    G.trowc = inp("trowc", [2, CTX])
    G.tapsc = P.dram("tapsc", [2, E, 2 * CTX], BF16)
    G.normc = P.dram("normc", [2, E], F32)
    G.negd = inp("negd", [128, E // 128])
    G.tb16 = inp("tb16", [128, 1664], BF16)
    G.tf32 = inp("tf32", [128, 512])
    G.zt = inp("zt", [NEMB, L])
    G.ztr = inp("ztr", [NEMB, L])
    G.trow = inp("trow", [2, L])
    G.tapsd = P.dram("tapsd", [2, E, 2 * L], BF16, kind=("ExternalOutput" if debug == 2 else "Internal"))
    G.normd = P.dram("normd", [2, E], F32, kind=("ExternalOutput" if debug == 2 else "Internal"))
    G.wsrc = G.w_mod
    out = P.dram("out", [L, D], F32, kind="ExternalOutput")
    kind = "ExternalOutput" if debug else "Internal"
    G.xA = P.dram("xA", [L, D], F32, kind=kind)
    G.xB = P.dram("xB", [L, D], F32)
    G.ctxA = P.dram("ctxA", [CTX, D], F32, kind=kind)
    G.ctxB = P.dram("ctxB", [CTX, D], F32, kind=kind)
    G.uT = Q([P.dram("uT%d" % q, [E, L + 3], F32, kind=kind) for q in range(4)], E)
    G.ucT = Q([P.dram("ucT%d" % q, [E, CTX + 3], F32) for q in range(4)], E)
    G.ygT = P.dram("ygT", [E, L], BF16, kind=kind)
    G.ygcT = P.dram("ygcT", [E, CTX], BF16)
    G.ones = P.sb("ones", [128, 128], F32, persist=True)
    G.ident = P.sb("identsb", [128, 128], BF16, persist=True)
    G.csil = P.sb("csil", [128, 2, 8], F32, persist=True)
    G.eps = P.sb("epsc", [128, 1], F32, persist=True)
    G.modt = [P.sb("modt%d" % j, [128, 3 * D], F32, persist=True) for j in range(2)]

    emit_setup(P, G, L)
    if debug == 2:
        G.dbg_hid = P.dram("dbg_hid", [2, FH, L], BF16, kind="ExternalOutput")
        G.dbg_nacc = P.dram("dbg_nacc", [128, 2, E // 128], F32, kind="ExternalOutput")
        emit_mod(P, G, 1)
        emit_proj(P, G, G.x_in, L, 0, G.hy_w_in.t.ap()[0], 4 * E, 128, G.uT, 1, False)
        emit_hyfilt(P, G, 0, L, L, G.zt, G.ztr, G.trow, G.tapsd, G.normd)
        emit_hyconv(P, G, 0, G.uT, L, 128, CBL, G.tb16, G.tf32, G.hypar, G.tapsd, G.normd, G.ygT, nb_limit=2)
        P.begin()
        P.op("sp", lambda e: e.dma_start(out=out.t.ap()[0:1, 0:2], in_=G.ygcT.t.ap()[1:2, 0:1].bitcast(F32) if False else G.xB.t.ap()[1:2, 0:2]), reads=[G.ygT], writes=[out], dma=out)
        P.end()
        P.root.close()
        return nc
    xs, cs = G.x_in, G.ctx_in
    xbufs, cbufs = [G.xA, G.xB], [G.ctxA, G.ctxB]
    for l in range(nlayers):
        kindl, occ = l % 2, l // 2
        need_ctx = any(jj % 2 == 0 for jj in range(l + 1, DEPTH))
        colmajor = occ % 2 == 1
        xd = out if l == nlayers - 1 and not debug else xbufs[l % 2]
        if l == nlayers - 1 and debug:
            xd = out
        cd = cbufs[l % 2]
        emit_mod(P, G, l)
        if kindl == 0:
            emit_proj(P, G, xs, L, 0, G.rg_w_in.t.ap()[occ], 2 * E, 96, G.uT, 1, colmajor)
            emit_proj(P, G, cs, CTX, 1, G.rg_w_in.t.ap()[occ], 2 * E, 96, G.ucT, 1, False)
            emit_rg(P, G, occ, G.uT, G.ucT, 1, L, G.ygT, G.ygcT, need_ctx)
            emit_out(P, G, l, xs, xd, L, 0, G.ygT, G.rg_w_out.t.ap()[occ], 96, colmajor)
            if need_ctx:
                emit_out(P, G, l, cs, cd, CTX, 1, G.ygcT, G.rg_w_out.t.ap()[occ], 96, False)
        else:
            emit_proj(P, G, xs, L, 0, G.hy_w_in.t.ap()[occ], 4 * E, 128, G.uT, 1, colmajor)
            emit_hyfilt(P, G, occ, L, L, G.zt, G.ztr, G.trow, G.tapsd, G.normd)
            emit_hyconv(P, G, occ, G.uT, L, 128, CBL, G.tb16, G.tf32, G.hypar, G.tapsd, G.normd, G.ygT)
            emit_out(P, G, l, xs, xd, L, 0, G.ygT, G.hy_w_out.t.ap()[occ], 128, colmajor)
            if need_ctx and not SKIP_CTX_HY:
                emit_proj(P, G, cs, CTX, 1, G.hy_w_in.t.ap()[occ], 4 * E, 128, G.ucT, 1, False)
                emit_hyfilt(P, G, occ, CTX, CTX, G.ztc, G.ztrc, G.trowc, G.tapsc, G.normc)
                emit_hyconv(P, G, occ, G.ucT, CTX, 16, CBC, G.tb16c, G.tf32c, G.hyparc, G.tapsc, G.normc, G.ygcT)
                emit_out(P, G, l, cs, cd, CTX, 1, G.ygcT, G.hy_w_out.t.ap()[occ], 128, False)
        xs = xd
        if need_ctx:
            cs = cd
    P.begin()
    P.op("sp", lambda e: e.dma_start(out=G.ygcT.t.ap()[0:1, 0:2], in_=G.ygcT.t.ap()[1:2, 0:2]), reads=[out], writes=[G.ygcT], dma=G.ygcT)
    P.end()
    P.root.close()
    return nc


def emb_tables(Lf):
    t = np.linspace(0.0, 1.0, Lf, dtype=np.float32)
    bands = np.linspace(1e-4, 7, 8, dtype=np.float32)
    w = (np.float32(2.0 * np.pi) * np.arange(Lf, dtype=np.float32) / np.float32(Lf))
    z = np.concatenate([t[:, None], np.cos(bands[None, :] * w[:, None]), -np.sin(bands[None, :] * w[:, None])], axis=1).astype(np.float32)
    idx = np.concatenate([[0], Lf - np.arange(1, Lf)])
    zt = np.ascontiguousarray(z.T)
    ztr = np.ascontiguousarray(z[idx].T)
    trow = np.stack([t, t[idx]]).astype(np.float32)
    return zt, ztr, trow


def hy_tables(L, Pn=128, sfx=""):
    import ml_dtypes
    assert L == Pn * Pn
    N = 2 * L
    N1 = 2 * Pn
    n1 = np.arange(Pn, dtype=np.float64)[:, None]
    k1 = np.arange(Pn, dtype=np.float64)[None, :]
    ang = 2 * np.pi * n1 * (k1 + 0.5) / N1
    FA0 = np.concatenate([np.cos(ang), -np.sin(ang)], 1)
    ang1 = 2 * np.pi * (n1 + Pn) * (k1 + 0.5) / N1
    FA1 = -np.concatenate([np.cos(ang1), -np.sin(ang1)], 1)
    angt = 2 * np.pi * n1 * (k1 + 0.5) / N
    angc = 2 * np.pi * n1 * k1 / Pn
    FCre, FCim = np.cos(angc), -np.sin(angc)
    FCc = np.concatenate([np.cos(angc.T), np.sin(angc.T)], 1)
    FCc2 = np.concatenate([-np.sin(angc.T), np.cos(angc.T)], 1)
    CA = (2.0 / N) * np.cos(ang.T)
    SA = -(2.0 / N) * np.sin(ang.T)
    tb16 = np.concatenate([FA0, FA1, FCre, FCim, -FCim, FCc, FCc2, CA, SA], 1).astype(np.float32).astype(ml_dtypes.bfloat16)
    tf32 = np.concatenate([np.cos(angt), -np.sin(angt), np.cos(angt.T), np.sin(angt.T)], 1).astype(np.float32)
    zt, ztr, trow = emb_tables(L)
    max_decay = np.log(1e-2) / 0.3
    min_decay = np.log(1e-2) / 1.5
    deltas = np.abs(np.linspace(min_decay, max_decay, E, dtype=np.float32))
    negd = np.ascontiguousarray((-deltas).reshape(E // 128, 128).T).astype(np.float32)
    return {"tb16" + sfx: tb16, "tf32" + sfx: tf32, "zt" + sfx: zt, "ztr" + sfx: ztr, "trow" + sfx: trow, "negd": negd}


def host_inputs(inputs, b):
    import ml_dtypes
    f = lambda a: np.ascontiguousarray(a, dtype=np.float32)
    cc = np.zeros((128, 2, 8), np.float32)
    cc[:, 0, :] = inputs["c"][b].reshape(8, 128).T
    cc[:, 1, :] = inputs["c_ctx"].reshape(8, 128).T
    NT = E // 96
    rgpar = np.zeros((2, 96, NT, 16), np.float32)

    def chan(v):
        return v.reshape(2, NT, 96).transpose(0, 2, 1)

    for k in range(4):
        rgpar[:, :, :, k] = chan(inputs["rg_conv_w"][:, k, :])
    rgpar[:, :, :, 4] = chan(inputs["rg_conv_b"])
    for d in range(2):
        rgpar[:, :, :, 5 + 2 * d] = chan(inputs["rg_b_r"][:, d, :])
        rgpar[:, :, :, 6 + 2 * d] = chan(inputs["rg_b_i"][:, d, :])
        rgpar[:, :, :, 9 + d] = chan(inputs["rg_lambda"][:, d, :])
    rgw = np.zeros((2, NT, 96, 4, 96), np.float32)
    for d in range(2):
        rgw[:, :, :, 2 * d + 0, :] = inputs["rg_w_r"][:, d]
        rgw[:, :, :, 2 * d + 1, :] = inputs["rg_w_i"][:, d]
    L = inputs["x"].shape[1]
    hy = hy_tables(L)
    hy.update(hy_tables(CTX, 16, "c"))

    def mk_hypar(CB):
        NB = E // CB
        hypar = np.zeros((2, NB, 14 * CB), np.float32)
        cw = inputs["hy_conv_w"].reshape(2, 3, 3, NB, CB)
        for tap in range(3):
            hypar[:, :, tap * 3 * CB:(tap + 1) * 3 * CB] = cw[:, tap].transpose(0, 2, 1, 3).reshape(2, NB, 3 * CB)
        hypar[:, :, 9 * CB:12 * CB] = inputs["hy_conv_b"].reshape(2, 3, NB, CB).transpose(0, 2, 1, 3).reshape(2, NB, 3 * CB)
        hypar[:, :, 12 * CB:13 * CB] = inputs["hy_d"][:, 0].reshape(2, NB, CB)
        hypar[:, :, 13 * CB:14 * CB] = inputs["hy_d"][:, 1].reshape(2, NB, CB)
        return hypar

    hypar = mk_hypar(CBL)
    hyparc = mk_hypar(CBC)
    hyfp = np.stack([inputs["hy_f_freq"], inputs["hy_f_b1"], inputs["hy_f_b2"]], axis=-1).astype(np.float32)
    extra = {"hy_w_in": f(inputs["hy_w_in"]), "hy_w_out": f(inputs["hy_w_out"]), "hy_f_w1": f(inputs["hy_f_w1"]),
             "hy_f_w2": f(inputs["hy_f_w2"]), "hy_f_w3": f(inputs["hy_f_w3"]), "hyfp": hyfp, "hypar": hypar, "hyparc": hyparc}
    extra.update(hy)
    return {
        **extra,
        "x": f(inputs["x"][b]), "ctx": f(inputs["ctx"][b]), "cc": cc,
        "ident": np.eye(128, dtype=np.float32).astype(ml_dtypes.bfloat16),
        "w_mod": f(inputs["w_mod"]), "b_mod": f(inputs["b_mod"]), "ln_g": f(inputs["ln_g"]), "ln_b": f(inputs["ln_b"]),
        "rg_w_in": f(inputs["rg_w_in"]), "rgpar": rgpar, "rgw": rgw, "rg_w_out": f(inputs["rg_w_out"]),
    }


def kernel(**inputs):
    L = inputs["x"].shape[1]
    nc = build_nc(L)
    in_maps = [host_inputs(inputs, b) for b in range(2)]
    res = run_bass_kernel_spmd(nc, in_maps, core_ids=[0, 1])
    return np.stack([res.results[b]["out"] for b in range(2)], axis=0).astype(np.float32)
```

```python
import contextlib
import numpy as np
import concourse.bass as bass
import concourse.mybir as mybir
from concourse.bass_utils import run_bass_kernel_spmd

F32 = mybir.dt.float32
BF16 = mybir.dt.bfloat16
ALU = mybir.AluOpType
AF = mybir.ActivationFunctionType
AX = mybir.AxisListType

D = 1024
E = 1536
GRID_W = 64
CTX = 256
DEPTH = 4
ALPHA = (2 * DEPTH) ** 0.25
LN_EPS = 1e-5


class Buf:
    def __init__(self, name):
        self.name = name
        self.w = {}
        self.r = {}


class T:
    def __init__(self, t, name):
        self.t = t
        self.buf = Buf(name)

    def __getitem__(self, k):
        return self.t[k]


class Q:
    def __init__(self, ts, rows_per):
        self.ts = ts
        self.rp = rows_per

    def at(self, r0, r1):
        q = r0 // self.rp
        assert (r1 - 1) // self.rp == q
        t = self.ts[q]
        return t, t.t.ap()[r0 - q * self.rp:r1 - q * self.rp]


class Prog:
    ENG = ("sp", "act", "dve", "pool", "pe")

    def __init__(self, nc):
        self.nc = nc
        self.root = contextlib.ExitStack()
        self.phase = None
        self.sems = {}
        self.latest = {}
        self.ops = {e: [] for e in self.ENG}
        self.waited = {e: {} for e in self.ENG}
        self.nsem = 0

    def sem(self, key):
        if key not in self.sems:
            free = getattr(self, "free_phys", None)
            if free:
                h, base = free.pop()
                self.sems[key] = h
                self.latest[key] = base
            else:
                self.nsem += 1
                self.sems[key] = self.root.enter_context(self.nc.semaphore("s%d" % self.nsem))
            if self.phase is not None and isinstance(key, tuple) and key[1] in getattr(self, "phase_names", ()):
                self.phase_keys.append(key)
        return self.sems[key]

    def _stack(self, persist):
        return self.root if (persist or self.phase is None) else self.phase

    def _scope_id(self):
        self.nscope = getattr(self, "nscope", 0) + 1
        return self.nscope

    def _uname(self, name):
        self.nname = getattr(self, "nname", 0) + 1
        return "%s_%d" % (name, self.nname)

    def sb(self, name, shape, dt, persist=False):
        name = self._uname(name)
        if not persist and self.phase is not None:
            self.phase_names.add(name)
        t = self._stack(persist).enter_context(self.nc.sbuf_tensor(name, list(shape), dt))
        return T(t, name)

    def ps(self, name, shape, dt, persist=False):
        name = self._uname(name)
        t = self._stack(persist).enter_context(self.nc.psum_tensor(name, list(shape), dt))
        return T(t, name)

    def dram(self, name, shape, dt, kind="Internal"):
        t = self.nc.dram_tensor(name, list(shape), dt, kind=kind)
        return T(t, name)

    def op(self, eng, fn, reads=(), writes=(), dma=None):
        need = {}

        def add(evs):
            for k, v in evs.items():
                if need.get(k, 0) < v:
                    need[k] = v

        for b in reads:
            add(b.buf.w)
        for b in writes:
            add(b.buf.w)
            add(b.buf.r)
        waits = []
        for k, v in need.items():
            if k == eng and eng == "pe":
                continue
            if k not in self.sems:
                continue
            if self.waited[eng].get(k, 0) >= v:
                continue
            self.waited[eng][k] = v
            waits.append((k, v))
        if dma is not None:
            key = ("dma", dma.buf.name)
            inc = 16
        else:
            key = eng
            inc = 1
        self.sem(key)
        val = self.latest.get(key, 0) + inc
        self.latest[key] = val
        self.ops[eng].append((waits, fn, key, inc))
        for b in reads:
            if b.buf.r.get(key, 0) < val:
                b.buf.r[key] = val
        for b in writes:
            if b.buf.w.get(key, 0) < val:
                b.buf.w[key] = val

    def dma(self, q, out_t, out_ap, in_t, in_ap, **kw):
        self.op(q, lambda e: e.dma_start(out=out_ap, in_=in_ap, **kw), reads=[in_t], writes=[out_t], dma=out_t)

    def barrier(self):
        for e in self.ENG:
            waits = []
            for k, v in self.latest.items():
                if self.waited[e].get(k, 0) >= v:
                    continue
                self.waited[e][k] = v
                waits.append((k, v))
            self.ops[e].append((waits, None, None, 0))

    def begin(self, name="phase"):
        self.phase_label = name
        self.phase = contextlib.ExitStack()
        self.phase_names = set()
        self.phase_keys = []
        if not hasattr(self, "free_phys"):
            self.free_phys = []

    def end(self):
        self.barrier()
        scope = self.nc.named_scope("%s_%d" % (self.phase_label, self._scope_id()), notify=True) if PROFILE_SCOPES else contextlib.nullcontext()
        with scope, self.nc.Block() as block:
            decos = {"sp": block.sync, "act": block.scalar, "dve": block.vector,
                     "pool": block.gpsimd, "pe": block.tensor}
            for en in self.ENG:
                ops = self.ops[en]

                def body(e, ops=ops):
                    for waits, fn, key, inc in ops:
                        for k, v in waits:
                            e.wait_ge(self.sem(k), v)
                        if fn is not None:
                            fn(e).then_inc(self.sem(key), inc)

                decos[en](body)
        self.ops = {e: [] for e in self.ENG}
        for key in self.phase_keys:
            self.free_phys.append((self.sems.pop(key), self.latest.pop(key)))
            for e in self.ENG:
                self.waited[e].pop(key, None)
        self.phase.close()
        self.phase = None


def emit_mod(P, G, l):
    P.begin("mod")
    wm = [P.sb("wm%d" % i, [128, 8, 512], F32) for i in range(2)]
    rep = P.sb("rep", [128, 2, 8, 128], F32)
    bm = P.sb("bm", [1, 3072], F32)
    pm = [P.ps("pmod%d" % i, [128, 512], F32) for i in range(2)]
    wv = G.w_mod.t.ap()[l].rearrange("(k p) n -> p k n", p=128)
    P.dma("act", bm, bm[:], G.wsrc, G.b_mod.t.ap()[l:l + 1, :])
    for j in range(2):
        for k in range(8):
            P.op("dve", lambda e, j=j, k=k: e.tensor_scalar_mul(out=rep[:, j, k, :], in0=G.ones[:], scalar1=G.csil[:, j, k:k + 1]),
                 reads=[G.ones, G.csil], writes=[rep])
    for n in range(6):
        w = wm[n % 2]
        P.dma("sp", w, w[:], G.wsrc, wv[:, :, n * 512:(n + 1) * 512])
        for j in range(2):
            p = pm[j]
            for k in range(8):
                P.op("pe", lambda e, p=p, j=j, k=k, w=w: e.matmul(p[:], lhsT=rep[:, j, k, :], rhs=w[:, k, :],
                                                                start=(k == 0), stop=False),
                     reads=[rep, w], writes=[p])
            P.op("pe", lambda e, p=p, n=n: e.matmul(p[:], lhsT=G.ones[0:1, :], rhs=bm[0:1, n * 512:(n + 1) * 512],
                                                    start=False, stop=True), reads=[G.ones, bm], writes=[p])
            P.op("act", lambda e, p=p, j=j, n=n: e.copy(out=G.modt[j][:, n * 512:(n + 1) * 512], in_=p[:]),
                 reads=[p], writes=[G.modt[j]])
    for j in range(2):
        P.op("dve", lambda e, j=j: e.tensor_scalar_add(out=G.modt[j][:, 1024:2048], in0=G.modt[j][:, 1024:2048], scalar1=1.0),
             reads=[G.modt[j]], writes=[G.modt[j]])
    P.end()


def x_rows(xt, ti, colmajor, L):
    ap = xt.t.ap()
    if not colmajor:
        return ap[ti * 128:(ti + 1) * 128, :]
    rows = L // GRID_W
    per_col = rows // 128
    col, rb = ti // per_col, ti % per_col
    v = ap.rearrange("(r c) d -> c r d", c=GRID_W)
    return v[col, rb * 128:(rb + 1) * 128, :]


def emit_proj(P, G, xsrc, L, j, w_dram, W, MT, outT, pad, colmajor):
    P.begin("proj")
    NK = 8
    wbf = P.sb("wbf", [128, NK, W], BF16)
    wst = [P.sb("wst%d" % i, [128, 1536], F32) for i in range(2)]
    xt = [P.sb("xt%d" % i, [128, D], F32) for i in range(3)]
    hb = [P.sb("hb%d" % i, [128, D], BF16) for i in range(2)]
    hT = [P.sb("hT%d" % i, [128, NK, 512], BF16) for i in range(2)]
    ost = [P.sb("ost%d" % i, [128, 512], F32) for i in range(4)]
    pT = [P.ps("pT%d" % i, [128, NK * 128], BF16) for i in range(2)]
    pm = [P.ps("pm%d" % i, [128, 512], F32) for i in range(4)]
    modt = G.modt[j]
    wv = w_dram
    i = 0
    for k in range(NK):
        for cc in range(W // 1536):
            s = wst[i % 2]
            P.dma("sp" if i % 2 == 0 else "act", s, s[:], G.wsrc, wv[k * 128:(k + 1) * 128, cc * 1536:(cc + 1) * 1536])
            eng = "pool" if i % 2 == 0 else "dve"
            P.op(eng, lambda e, s=s, k=k, cc=cc: e.tensor_copy(out=wbf[:, k, cc * 1536:(cc + 1) * 1536], in_=s[:]),
                 reads=[s], writes=[wbf])
            i += 1
    CH = min(512, L)
    NS = CH // 128
    ntile = L // 128
    nchunk = L // CH
    nmt = W // MT

    def load(ti):
        t = xt[ti % 3]
        P.dma("sp", t, t[:], xsrc, x_rows(xsrc, ti, colmajor, L))

    load(0)
    load(1)
    ev = 0
    for c in range(nchunk):
        h = hT[c % 2]
        for sub in range(NS):
            ti = c * NS + sub
            if ti + 2 < ntile:
                load(ti + 2)
            t = xt[ti % 3]
            b = hb[ti % 2]
            p = pT[ti % 2]
            P.op("dve", lambda e, t=t: e.tensor_tensor(out=t[:], in0=t[:], in1=modt[:, 1024:2048], op=ALU.mult),
                 reads=[t, modt], writes=[t])
            P.op("pool", lambda e, t=t, b=b: e.tensor_tensor(out=b[:], in0=t[:], in1=modt[:, 0:1024], op=ALU.add),
                 reads=[t, modt], writes=[b])
            for k in range(NK):
                P.op("pe", lambda e, p=p, b=b, k=k: e.transpose(out=p[:, k * 128:(k + 1) * 128], in_=b[:, k * 128:(k + 1) * 128],
                                                              identity=G.ident[:]),
                     reads=[b, G.ident], writes=[p])
            P.op("act", lambda e, h=h, p=p, sub=sub: e.copy(out=h[:, :, sub * 128:(sub + 1) * 128],
                                                            in_=p[:].rearrange("p (k t) -> p k t", k=NK)),
                 reads=[p], writes=[h])
        for mt in range(nmt):
            p = pm[ev % 4]
            o = ost[ev % 4]
            for k in range(NK):
                P.op("pe", lambda e, p=p, h=h, k=k, mt=mt: e.matmul(p[:MT, :CH], lhsT=wbf[:, k, mt * MT:(mt + 1) * MT], rhs=h[:, k, :CH],
                                                                  start=(k == 0), stop=(k == NK - 1)),
                     reads=[wbf, h], writes=[p])
            if ev % 2 == 0:
                P.op("act", lambda e, p=p, o=o: e.copy(out=o[:MT, :CH], in_=p[:MT, :CH]), reads=[p], writes=[o])
            else:
                P.op("dve", lambda e, p=p, o=o: e.tensor_copy(out=o[:MT, :CH], in_=p[:MT, :CH]), reads=[p], writes=[o])
            ot_, orow = outT.at(mt * MT, (mt + 1) * MT)
            P.dma("sp" if ev % 2 == 0 else "act", ot_, orow[:, pad + c * CH: pad + (c + 1) * CH], o, o[:MT, :CH])
            ev += 1
    P.end()


def emit_out(P, G, l, xsrc, xdst, L, j, ygT, w_dram, KT, colmajor):
    P.begin("out")
    NKT = E // KT
    wbf = P.sb("wobf", [128, NKT, D], BF16)
    wst = [P.sb("wost%d" % i, [128, D], F32) for i in range(2)]
    yg = [P.sb("yg%d" % i, [128, NKT, 512], BF16) for i in range(2)]
    xt = [P.sb("xo%d" % i, [128, D], F32) for i in range(3)]
    zt = [P.sb("zt%d" % i, [128, D], F32) for i in range(2)]
    st = [P.sb("st%d" % i, [128, 2, 6], F32) for i in range(2)]
    mv = [P.sb("mv%d" % i, [128, 4], F32) for i in range(2)]
    po = [P.ps("po%d" % i, [128, D], F32) for i in range(2)]
    modt = G.modt[j]
    for k in range(NKT):
        s = wst[k % 2]
        P.dma("sp" if k % 2 == 0 else "act", s, s[:KT, :], G.wsrc, w_dram[k * KT:(k + 1) * KT, :])
        P.op("pool" if k % 2 == 0 else "dve", lambda e, s=s, k=k: e.tensor_copy(out=wbf[:KT, k, :], in_=s[:KT, :]),
             reads=[s], writes=[wbf])
    CH = min(512, L)
    NS = CH // 128
    ntile = L // 128
    nchunk = L // CH
    ygv = ygT.t.ap().rearrange("(k p) t -> p k t", p=KT)
    lng = P.sb("lng", [128, D], F32)
    lnb = P.sb("lnb", [128, D], F32)
    P.dma("sp", lng, lng[:], G.wsrc, G.ln_g.t.ap()[l:l + 1, :].partition_broadcast(128))
    P.dma("act", lnb, lnb[:], G.wsrc, G.ln_b.t.ap()[l:l + 1, :].partition_broadcast(128))

    def loadx(ti):
        t = xt[ti % 3]
        P.dma("sp", t, t[:], xsrc, x_rows(xsrc, ti, colmajor, L))

    def loady(c):
        y = yg[c % 2]
        P.dma("act", y, y[:KT, :, :CH], ygT, ygv[:, :, c * CH:(c + 1) * CH])

    loadx(0)
    loadx(1)
    loady(0)
    for c in range(nchunk):
        if c + 1 < nchunk:
            loady(c + 1)
        y = yg[c % 2]
        for sub in range(NS):
            ti = c * NS + sub
            if ti + 2 < ntile:
                loadx(ti + 2)
            t = xt[ti % 3]
            p = po[ti % 2]
            z = zt[ti % 2]
            s = st[ti % 2]
            m = mv[ti % 2]
            for n in range(2):
                for k in range(NKT):
                    P.op("pe", lambda e, p=p, y=y, k=k, n=n, sub=sub: e.matmul(
                        p[:, n * 512:(n + 1) * 512], lhsT=y[:KT, k, sub * 128:(sub + 1) * 128],
                        rhs=wbf[:KT, k, n * 512:(n + 1) * 512], start=(k == 0), stop=(k == NKT - 1)),
                        reads=[y, wbf], writes=[p])
            P.op("dve", lambda e, p=p, z=z: e.tensor_tensor(out=z[:], in0=p[:], in1=modt[:, 2048:3072], op=ALU.mult),
                 reads=[p, modt], writes=[z])
            P.op("dve", lambda e, t=t, z=z: e.scalar_tensor_tensor(out=z[:], in0=t[:], scalar=ALPHA, in1=z[:],
                                                                   op0=ALU.mult, op1=ALU.add),
                 reads=[t, z], writes=[z])
            for n in range(2):
                P.op("dve", lambda e, s=s, z=z, n=n: e.bn_stats(out=s[:, n, :], in_=z[:, n * 512:(n + 1) * 512]),
                     reads=[z], writes=[s])
            P.op("dve", lambda e, s=s, m=m: e.bn_aggr(out=m[:, 0:2], in_=s[:].rearrange("p a b -> p (a b)")),
                 reads=[s], writes=[m])
            P.op("act", lambda e, m=m: e.activation(out=m[:, 2:3], in_=m[:, 1:2], func=AF.Sqrt, bias=G.eps[:, 0:1], scale=1.0),
                 reads=[m, G.eps], writes=[m])
            P.op("dve", lambda e, m=m: e.reciprocal(out=m[:, 3:4], in_=m[:, 2:3]), reads=[m], writes=[m])
            P.op("dve", lambda e, m=m, z=z: e.tensor_scalar(out=z[:], in0=z[:], scalar1=m[:, 0:1], scalar2=m[:, 3:4],
                                                            op0=ALU.subtract, op1=ALU.mult), reads=[m, z], writes=[z])
            P.op("pool", lambda e, z=z: e.tensor_tensor(out=z[:], in0=z[:], in1=lng[:], op=ALU.mult),
                 reads=[z, lng], writes=[z])
            P.op("pool", lambda e, z=z: e.tensor_tensor(out=z[:], in0=z[:], in1=lnb[:], op=ALU.add),
                 reads=[z, lnb], writes=[z])
            P.dma("sp", xdst, x_rows(xdst, ti, colmajor, L), z, z[:])
    P.end()


def run_streams(gens, stagger=0):
    active = list(gens)
    for i, g in enumerate(list(active)):
        for _ in range(i * stagger):
            try:
                next(g)
            except StopIteration:
                active.remove(g)
                break
    while active:
        for g in list(active):
            try:
                next(g)
            except StopIteration:
                active.remove(g)


def emit_rg(P, G, occ, uT, ucT, pad, L, ygT, ygcT, need_ctx):
    P.begin("rg")
    TC = 512
    NT = E // 96
    NS = 4
    par = P.sb("rgpar", [96, NT, 16], F32)
    cst = P.sb("rgcst", [96, NT, 4], F32)
    zero = P.sb("zero1", [96, 1], F32)
    P.op("dve", lambda e: e.memset(zero[:], 0.0), writes=[zero])
    P.dma("sp", par, par[:], G.wsrc, G.rgpar.t.ap()[occ])
    P.op("act", lambda e: e.activation(out=cst[:, :, 0:2], in_=par[:, :, 9:11], func=AF.Exp, scale=-1.0), reads=[par], writes=[cst])
    P.op("act", lambda e: e.activation(out=cst[:, :, 0:2], in_=cst[:, :, 0:2], func=AF.Ln, bias=1.0), reads=[cst], writes=[cst])
    P.op("dve", lambda e: e.tensor_scalar_mul(out=cst[:, :, 2:4], in0=cst[:, :, 0:2], scalar1=-16.0), reads=[cst], writes=[cst])
    P.op("dve", lambda e: e.tensor_scalar_mul(out=cst[:, :, 0:2], in0=cst[:, :, 0:2], scalar1=-8.0), reads=[cst], writes=[cst])

    def rev(ap, d):
        return ap if d == 0 else ap[:, ::-1]

    def stream(s):
        n_ = "s%d" % s
        wg32 = P.sb("wg32" + n_, [96, 4, 96], F32)
        w = P.sb("wg" + n_, [96, 4, 96], BF16)
        ut = [P.sb("ut%d" % i + n_, [96, TC + 3], F32) for i in range(2)]
        gt = P.sb("gt" + n_, [96, TC], F32)
        xct = P.sb("xcs" + n_, [96, TC], F32)
        xbt = P.sb("xb" + n_, [96, TC], BF16)
        r = P.sb("rr" + n_, [96, TC], F32)
        i_ = P.sb("ii" + n_, [96, TC], F32)
        a = P.sb("aa" + n_, [96, TC], F32)
        m = P.sb("mm" + n_, [96, TC], F32)
        b = P.sb("bx" + n_, [96, TC], F32)
        hh = [P.sb("hh%d" % i + n_, [96, TC], F32) for i in range(2)]
        hfb = P.sb("hfb" + n_, [96, TC], BF16)
        hfl = P.sb("hfl" + n_, [96, TC], BF16)
        hfc = P.sb("hfc" + n_, [96, CTX], F32)
        y = P.sb("yy" + n_, [96, TC], BF16)
        p = P.ps("pr" + n_, [96, TC], F32)
        cnt = {"c": 0}

        def chunk(tj, src, col0, n, d, hinit):
            ci = cnt["c"]
            cnt["c"] += 1
            u = ut[ci % 2]
            h = hh[ci % 2]
            st_, srow = src.at(tj * 96, (tj + 1) * 96)
            P.dma("sp", u, u[:, :n + 3], st_, srow[:, col0:col0 + n + 3])
            yield
            P.op("act", lambda e: e.activation(out=xct[:, :n], in_=u[:, 0:n], func=AF.Identity,
                                               scale=par[:, tj, 0:1], bias=par[:, tj, 4:5]), reads=[u, par], writes=[xct])
            yield
            for k in range(1, 4):
                P.op("dve", lambda e, k=k: e.scalar_tensor_tensor(out=xct[:, :n], in0=u[:, k:k + n], scalar=par[:, tj, k:k + 1],
                                                                 in1=xct[:, :n], op0=ALU.mult, op1=ALU.add),
                     reads=[u, par, xct], writes=[xct])
            yield
            P.op("act", lambda e: e.copy(out=xbt[:, :n], in_=xct[:, :n]), reads=[xct], writes=[xbt])
            yield
            for g in range(2):
                P.op("pe", lambda e, g=g: e.matmul(p[:, :n], lhsT=w[:, 2 * d + g, :], rhs=xbt[:, :n], start=True, stop=True),
                     reads=[w, xbt], writes=[p])
                yield
                dst = r if g == 0 else i_
                bcol = par[:, tj, 5 + 2 * d + g: 6 + 2 * d + g]
                P.op("act", lambda e, dst=dst, bcol=bcol: e.activation(out=dst[:, :n], in_=p[:, :n], func=AF.Sigmoid, bias=bcol),
                     reads=[p, par], writes=[dst])
                yield
            P.op("act", lambda e: e.activation(out=a[:, :n], in_=rev(r[:, :n], d), func=AF.Exp, scale=cst[:, tj, d:d + 1]),
                 reads=[r, cst], writes=[a])
            P.op("pool", lambda e: e.tensor_tensor(out=i_[:, :n], in0=i_[:, :n], in1=xct[:, :n], op=ALU.mult), reads=[i_, xct], writes=[i_])
            yield
            P.op("act", lambda e: e.activation(out=m[:, :n], in_=rev(r[:, :n], d), func=AF.Exp, scale=cst[:, tj, 2 + d:3 + d]),
                 reads=[r, cst], writes=[m])
            yield
            P.op("act", lambda e: e.activation(out=m[:, :n], in_=m[:, :n], func=AF.Sqrt, scale=-1.0, bias=1.0), reads=[m], writes=[m])
            yield
            P.op("pool", lambda e: e.tensor_tensor(out=b[:, :n], in0=rev(i_[:, :n], d), in1=m[:, :n], op=ALU.mult), reads=[i_, m], writes=[b])
            yield
            init_t, init_ap = hinit
            P.op("dve", lambda e: e.tensor_tensor_scan(out=h[:, :n], data0=a[:, :n], data1=b[:, :n], initial=init_ap,
                                                       op0=ALU.mult, op1=ALU.add), reads=[a, b, init_t], writes=[h])
            yield
            return h, (h, h[:, n - 1:n])

        def gate_out(tj, h, n, hfwd_ap, hfwd_t, gsrc, gcol0, dst, dcol0):
            gs_, grow = gsrc.at(E + tj * 96, E + (tj + 1) * 96)
            P.dma("sp", gt, gt[:, :n], gs_, grow[:, gcol0:gcol0 + n])
            yield
            P.op("act", lambda e: e.activation(out=gt[:, :n], in_=gt[:, :n], func=AF.Silu), reads=[gt], writes=[gt])
            P.op("dve", lambda e: e.tensor_tensor(out=b[:, :n], in0=h[:, :n][:, ::-1], in1=hfwd_ap, op=ALU.add),
                 reads=[h, hfwd_t], writes=[b])
            yield
            P.op("pool", lambda e: e.tensor_tensor(out=y[:, :n], in0=b[:, :n], in1=gt[:, :n], op=ALU.mult), reads=[b, gt], writes=[y])
            yield
            P.dma("sp", dst, dst.t.ap()[tj * 96:(tj + 1) * 96, dcol0:dcol0 + n], y, y[:, :n])
            yield

        for tj in range(s, NT, NS):
            hfd = G.hfd[tj]
            P.dma("sp", wg32, wg32[:], G.wsrc, G.rgw.t.ap()[occ, tj])
            P.op("pool", lambda e: e.tensor_copy(out=w[:], in_=wg32[:]), reads=[wg32], writes=[w])
            yield
            z0 = (zero, zero[:, 0:1])
            h, carry = yield from chunk(tj, ucT, 0, CTX, 0, z0)
            if need_ctx:
                P.op("pool", lambda e, h=h: e.tensor_copy(out=hfc[:], in_=h[:, :CTX]), reads=[h], writes=[hfc])
            for c in range(L // TC):
                h, carry = yield from chunk(tj, uT, c * TC, TC, 0, carry)
                P.op("act", lambda e, h=h: e.copy(out=hfb[:], in_=h[:]), reads=[h], writes=[hfb])
                yield
                P.dma("sp", hfd, hfd.t.ap()[:, c * TC:(c + 1) * TC], hfb, hfb[:])
                yield
            h, carry = yield from chunk(tj, ucT, 0, CTX, 1, z0)
            if need_ctx:
                yield from gate_out(tj, h, CTX, hfc[:], hfc, ucT, pad, ygcT, 0)
            for c in range(L // TC - 1, -1, -1):
                P.dma("sp", hfl, hfl[:], hfd, hfd.t.ap()[:, c * TC:(c + 1) * TC])
                h, carry = yield from chunk(tj, uT, c * TC, TC, 1, carry)
                yield from gate_out(tj, h, TC, hfl[:], hfl, uT, pad + c * TC, ygT, c * TC)

    run_streams([stream(s) for s in range(NS)], stagger=0)
    P.end()


PI = float(np.pi)
SKIP_CTX_HY = False
PROFILE_SCOPES = False
CBL = 4
CBC = 32
FH = 64
NEMB = 17


def emit_hyfilt(P, G, occ, Lf, L, zt_d, ztr_d, trow_d, tapsd, normd):
    P.begin("hyfilt")
    NCT = E // 128
    w1 = P.sb("fw1", [NEMB, FH], F32)
    w2 = P.sb("fw2", [FH, FH], F32)
    fp = P.sb("fpar", [FH, 4], F32)
    fb = P.sb("fb", [FH, 2], F32)
    w3s = [P.sb("w3s%d" % i, [FH, 1536], F32) for i in range(2)]
    w3 = P.sb("w3bf", [FH, 4 * E], BF16)
    hid = [P.sb("hid%d" % i, [FH, Lf], BF16) for i in range(2)]
    zc = [P.sb("zc%d" % i, [NEMB, 512], F32) for i in range(2)]
    vv = [P.sb("vv%d" % i, [FH, 512], F32) for i in range(2)]
    mk = [P.sb("mk%d" % i, [FH, 512], F32) for i in range(2)]
    h1 = [P.sb("h1%d" % i, [FH, 512], F32) for i in range(2)]
    nd = P.sb("negd", [128, NCT], F32)
    tw = [P.sb("tw%d" % i, [128, 512], F32) for i in range(2)]
    win = [P.sb("win%d" % i, [128, 512], F32) for i in range(2)]
    tp = [P.sb("tp%d" % i, [128, 512], BF16) for i in range(4)]
    junk = P.sb("junk", [128, 512], BF16)
    ac = [P.sb("ac%d" % i, [128, 1], F32) for i in range(4)]
    nacc = P.sb("nacc", [128, 2, NCT], F32)
    pa = [P.ps("pfa%d" % i, [FH, 512], F32) for i in range(2)]
    pb = [P.ps("pfb%d" % i, [128, 512], F32) for i in range(4)]
    P.dma("sp", w1, w1[:], G.wsrc, G.hy_f_w1.t.ap()[occ])
    P.dma("sp", w2, w2[:], G.wsrc, G.hy_f_w2.t.ap()[occ])
    P.dma("sp", fp, fp[:, 0:3], G.wsrc, G.hyfp.t.ap()[occ])
    P.dma("sp", nd, nd[:], G.wsrc, G.negd.t.ap())
    P.op("dve", lambda e: e.memset(nacc[:], 0.0), writes=[nacc])
    P.op("dve", lambda e: e.tensor_scalar_mul(out=fb[:, 0:1], in0=fp[:, 1:2], scalar1=fp[:, 0:1]), reads=[fp], writes=[fb])
    P.op("dve", lambda e: e.tensor_scalar_mul(out=fb[:, 1:2], in0=fp[:, 2:3], scalar1=fp[:, 0:1]), reads=[fp], writes=[fb])
    for i in range(4):
        s = w3s[i % 2]
        P.dma("act", s, s[:], G.wsrc, G.hy_f_w3.t.ap()[occ][:, i * 1536:(i + 1) * 1536])
        P.op("pool", lambda e, s=s, i=i: e.tensor_copy(out=w3[:, i * 1536:(i + 1) * 1536], in_=s[:]), reads=[s], writes=[w3])

    def sin_layer(ps, n, bcol, out_ap, out_t, v, m):
        P.op("act", lambda e: e.activation(out=v[:, :n], in_=ps[:, :n], func=AF.Identity, scale=fp[:, 0:1], bias=bcol),
             reads=[ps, fp, fb], writes=[v])
        for it in range(2):
            for thr, add, cmp in ((PI, -2 * PI, ALU.is_gt), (-PI, 2 * PI, ALU.is_lt)):
                P.op("dve", lambda e, thr=thr, add=add, cmp=cmp: e.tensor_scalar(out=m[:, :n], in0=v[:, :n], scalar1=thr, scalar2=add,
                                                                               op0=cmp, op1=ALU.mult), reads=[v], writes=[m])
                P.op("dve", lambda e: e.tensor_tensor(out=v[:, :n], in0=v[:, :n], in1=m[:, :n], op=ALU.add), reads=[v, m], writes=[v])
        P.op("act", lambda e: e.activation(out=out_ap, in_=v[:, :n], func=AF.Sin), reads=[v], writes=[out_t])

    it = 0
    for dr, ztab in ((0, zt_d), (1, ztr_d)):
        for c0 in range(0, Lf, 512):
            n = min(512, Lf - c0)
            z, v, m, hh1 = zc[it % 2], vv[it % 2], mk[it % 2], h1[it % 2]
            p1 = pa[it % 2]
            it += 1
            P.dma("sp", z, z[:, :n], G.wsrc, ztab.t.ap()[:, c0:c0 + n])
            P.op("pe", lambda e, p1=p1, z=z, n=n: e.matmul(p1[:, :n], lhsT=w1[:], rhs=z[:, :n], start=True, stop=True),
                 reads=[w1, z], writes=[p1])
            sin_layer(p1, n, fb[:, 0:1], hh1[:, :n], hh1, v, m)
            P.op("pe", lambda e, p1=p1, hh1=hh1, n=n: e.matmul(p1[:, :n], lhsT=w2[:], rhs=hh1[:, :n], start=True, stop=True),
                 reads=[w2, hh1], writes=[p1])
            sin_layer(p1, n, fb[:, 1:2], hid[dr][:, c0:c0 + n], hid[dr], v, m)
    it = 0
    for ct in range(NCT):
        for dr in range(2):
            for c0 in range(0, Lf, 512):
                n = min(512, Lf - c0)
                t_, w_ = tw[it % 2], win[it % 2]
                P.dma("sp", t_, t_[:, :n], G.wsrc, trow_d.t.ap()[dr, c0:c0 + n].partition_broadcast(128))
                P.op("act", lambda e, t_=t_, w_=w_, n=n, ct=ct: e.activation(out=w_[:, :n], in_=t_[:, :n], func=AF.Exp, scale=nd[:, ct:ct + 1]),
                     reads=[t_, nd], writes=[w_])
                for o in range(2):
                    k = (it * 2 + o) % 4
                    p, tq, a_ = pb[k], tp[k], ac[k]
                    col = o * 2 * E + dr * E + ct * 128
                    P.op("pe", lambda e, p=p, col=col, c0=c0, n=n, dr=dr: e.matmul(p[:, :n], lhsT=w3[:, col:col + 128], rhs=hid[dr][:, c0:c0 + n],
                                                                          start=True, stop=True), reads=[w3, hid[dr]], writes=[p])
                    P.op("dve", lambda e, p=p, tq=tq, w_=w_, n=n: e.scalar_tensor_tensor(out=tq[:, :n], in0=w_[:, :n], scalar=0.05, in1=p[:, :n],
                                                                                      op0=ALU.add, op1=ALU.mult), reads=[w_, p], writes=[tq])
                    if dr == 1 and c0 == 0:
                        P.op("dve", lambda e, tq=tq: e.memset(tq[:, 0:1], 0.0), reads=[tq], writes=[tq])
                    P.op("act", lambda e, tq=tq, a_=a_, n=n: e.activation(out=junk[:, :n], in_=tq[:, :n], func=AF.Abs, accum_out=a_[:, 0:1]),
                         reads=[tq], writes=[junk, a_])
                    P.op("dve", lambda e, a_=a_, o=o, ct=ct: e.tensor_tensor(out=nacc[:, o, ct:ct + 1], in0=nacc[:, o, ct:ct + 1], in1=a_[:, 0:1], op=ALU.add),
                         reads=[a_, nacc], writes=[nacc])
                    pos = c0 if dr == 0 else 2 * L - Lf + c0
                    P.dma("act", tapsd, tapsd.t.ap()[o, ct * 128:(ct + 1) * 128, pos:pos + n], tq, tq[:, :n])
                it += 1
    P.dma("sp", normd, normd.t.ap().rearrange("o (c p) -> p o c", p=128), nacc, nacc[:], allow_slow_non_contiguous=True)
    if getattr(G, "dbg_hid", None) is not None:
        for dr in range(2):
            P.dma("sp", G.dbg_hid, G.dbg_hid.t.ap()[dr], hid[dr], hid[dr][:])
        P.dma("sp", G.dbg_nacc, G.dbg_nacc.t.ap(), nacc, nacc[:])
    P.end()


def emit_hyconv(P, G, occ, uT, L, Pn, CB, tb_d, tf_d, hypar_d, tapsd, normd, ygT, nb_limit=None):
    assert L == Pn * Pn
    P.begin("hyconv")
    NS = 2
    GS = 512 // Pn
    NB = E // CB
    W = L + 3
    tb = P.sb("tb16", [Pn, 13 * Pn], BF16)
    tf = P.sb("tf32", [Pn, 4 * Pn], F32)
    FA0, FA1 = tb[:, 0:2 * Pn], tb[:, 2 * Pn:4 * Pn]
    FCre, FCim, FCimN = tb[:, 4 * Pn:5 * Pn], tb[:, 5 * Pn:6 * Pn], tb[:, 6 * Pn:7 * Pn]
    FCc, FCc2 = tb[:, 7 * Pn:9 * Pn], tb[:, 9 * Pn:11 * Pn]
    CA, SA = tb[:, 11 * Pn:12 * Pn], tb[:, 12 * Pn:13 * Pn]

    def bc(ap2d):
        return ap2d.unsqueeze(1).to_broadcast([Pn, CB, Pn])

    TW = (bc(tf[:, 0:Pn]), bc(tf[:, Pn:2 * Pn]))
    TWT = (bc(tf[:, 2 * Pn:3 * Pn]), bc(tf[:, 3 * Pn:4 * Pn]))
    P.dma("sp", tb, tb[:], G.wsrc, tb_d.t.ap())
    P.dma("sp", tf, tf[:], G.wsrc, tf_d.t.ap())
    hv = hypar_d.t.ap()

    def bcw(ap2d, n):
        return ap2d.unsqueeze(2).to_broadcast([Pn, n, Pn])

    def stream(s):
        n_ = "h%d" % s
        xs = P.sb("xs" + n_, [Pn, 3 * CB, Pn + 2], F32)
        vx = P.sb("vx" + n_, [Pn, 3 * CB, Pn], F32)
        tca = P.sb("tca" + n_, [Pn, 3 * CB, Pn], F32)
        tcb = P.sb("tcb" + n_, [Pn, 3 * CB, Pn], F32)
        vb = P.sb("vb" + n_, [Pn, CB, Pn], BF16)
        gt = P.sb("gth" + n_, [Pn, CB, Pn], F32)
        kt = [P.sb("kt%d" % i + n_, [Pn, CB, 2, Pn], BF16) for i in range(2)]
        KF = [P.sb("KF%d" % i + n_, [Pn, 2, CB, Pn], F32) for i in range(2)]
        hp = P.sb("hp" + n_, [Pn, 14 * CB], F32)
        rn = P.sb("rn" + n_, [Pn, 2, CB], F32)
        Asb = [P.sb("Asb%d" % i + n_, [Pn, 2 * CB * Pn], F32) for i in range(2)]
        t1 = P.sb("t1" + n_, [Pn, CB, Pn], F32)
        t2 = P.sb("t2" + n_, [Pn, CB, Pn], F32)
        t3 = P.sb("t3" + n_, [Pn, CB, Pn], F32)
        t4 = P.sb("t4" + n_, [Pn, CB, Pn], F32)
        Bre = [P.sb("Bre%d" % i + n_, [Pn, CB, Pn], BF16) for i in range(2)]
        Bim = [P.sb("Bim%d" % i + n_, [Pn, CB, Pn], BF16) for i in range(2)]
        zz = P.sb("zz" + n_, [Pn, CB, Pn], F32)
        zzb = P.sb("zzb" + n_, [Pn, CB, Pn], BF16)
        tcm = P.sb("tcm" + n_, [Pn, CB, Pn], F32)
        ygo = P.sb("ygo" + n_, [Pn, CB, Pn], BF16)
        pA = P.ps("pA" + n_, [Pn, 2 * CB * Pn], F32)
        pC = P.ps("pC" + n_, [Pn, 2 * CB * Pn], F32)
        cnt = {"x": 0}
        yv = pC[:, 0:CB * Pn].rearrange("p (c k) -> p c k", c=CB)

        def cmul(src, tabs, tab_re, tab_im, ore, oim, srcv):
            cnt["x"] += 1
            sre, sim = srcv(0), srcv(1)
            P.op("dve", lambda e: e.tensor_tensor(out=t1[:], in0=sre, in1=tab_re, op=ALU.mult), reads=[src] + tabs, writes=[t1])
            P.op("pool", lambda e: e.tensor_tensor(out=t3[:], in0=sre, in1=tab_im, op=ALU.mult), reads=[src] + tabs, writes=[t3])
            yield
            P.op("dve", lambda e: e.tensor_tensor(out=t2[:], in0=sim, in1=tab_im, op=ALU.mult), reads=[src] + tabs, writes=[t2])
            yield
            P.op("dve", lambda e: e.tensor_tensor(out=t4[:], in0=sim, in1=tab_re, op=ALU.mult), reads=[src] + tabs, writes=[t4])
            yield
            P.op("dve", lambda e: e.tensor_tensor(out=ore[0], in0=t1[:], in1=t2[:], op=ALU.subtract), reads=[t1, t2], writes=[ore[1]])
            P.op("pool", lambda e: e.tensor_tensor(out=oim[0], in0=t3[:], in1=t4[:], op=ALU.add), reads=[t3, t4], writes=[oim[1]])
            yield

        def fwd(lhs_list):
            k = cnt["x"] % 2
            for cb in range(CB):
                ents = lhs_list[cb]
                for j, (lap, tab, tt) in enumerate(ents):
                    P.op("pe", lambda e, cb=cb, lap=lap, tab=tab, j=j, nj=len(ents): e.matmul(
                        pA[:, cb * 2 * Pn:(cb + 1) * 2 * Pn], lhsT=lap, rhs=tab, start=(j == 0), stop=(j == nj - 1)),
                        reads=[tt, tb], writes=[pA])
            yield
            A_ = Asb[k]
            P.op("act", lambda e: e.copy(out=A_[:], in_=pA[:]), reads=[pA], writes=[A_])
            yield
            Av = A_[:].rearrange("p (c r k) -> p c r k", c=CB, r=2)
            br, bi = Bre[k], Bim[k]
            yield from cmul(A_, [tf], TW[0], TW[1], (br[:], br), (bi[:], bi), lambda r_: Av[:, :, r_, :])
            for g in range(CB // GS):
                rb = br[:, g * GS:(g + 1) * GS, :].rearrange("p c k -> p (c k)")
                ib = bi[:, g * GS:(g + 1) * GS, :].rearrange("p c k -> p (c k)")
                ore = pC[:, g * 512:(g + 1) * 512]
                oim = pC[:, CB * Pn + g * 512:CB * Pn + (g + 1) * 512]
                P.op("pe", lambda e, ore=ore, rb=rb: e.matmul(ore, lhsT=FCre, rhs=rb, start=True, stop=False), reads=[tb, br], writes=[pC])
                P.op("pe", lambda e, ore=ore, ib=ib: e.matmul(ore, lhsT=FCimN, rhs=ib, start=False, stop=True), reads=[tb, bi], writes=[pC])
                P.op("pe", lambda e, oim=oim, ib=ib: e.matmul(oim, lhsT=FCre, rhs=ib, start=True, stop=False), reads=[tb, bi], writes=[pC])
                P.op("pe", lambda e, oim=oim, rb=rb: e.matmul(oim, lhsT=FCim, rhs=rb, start=False, stop=True), reads=[tb, br], writes=[pC])
            yield
            Y_ = Asb[(k + 1) % 2]
            P.op("act", lambda e: e.copy(out=Y_[:], in_=pC[:]), reads=[pC], writes=[Y_])
            yield
            return Y_

        def inv(Y_, o):
            k = cnt["x"] % 2
            Yv = Y_[:].rearrange("p (r c k) -> p r c k", r=2, c=CB)
            zr, zi = Bre[k], Bim[k]
            yield from cmul(Y_, [KF[o]], KF[o][:, 0, :, :], KF[o][:, 1, :, :], (zr[:], zr), (zi[:], zi), lambda r_: Yv[:, r_, :, :])
            for cb in range(CB):
                P.op("pe", lambda e, cb=cb: e.matmul(pA[:, cb * 2 * Pn:(cb + 1) * 2 * Pn], lhsT=zr[:, cb, :], rhs=FCc, start=True, stop=False),
                     reads=[zr, tb], writes=[pA])
                P.op("pe", lambda e, cb=cb: e.matmul(pA[:, cb * 2 * Pn:(cb + 1) * 2 * Pn], lhsT=zi[:, cb, :], rhs=FCc2, start=False, stop=True),
                     reads=[zi, tb], writes=[pA])
            yield
            k2 = cnt["x"] % 2
            A_ = Asb[k2]
            P.op("act", lambda e: e.copy(out=A_[:], in_=pA[:]), reads=[pA], writes=[A_])
            yield
            Av = A_[:].rearrange("p (c r k) -> p c r k", c=CB, r=2)
            cr, ci = Bre[k2], Bim[k2]
            yield from cmul(A_, [tf], TWT[0], TWT[1], (cr[:], cr), (ci[:], ci), lambda r_: Av[:, :, r_, :])
            for g in range(CB // GS):
                rb = cr[:, g * GS:(g + 1) * GS, :].rearrange("p c k -> p (c k)")
                ib = ci[:, g * GS:(g + 1) * GS, :].rearrange("p c k -> p (c k)")
                oy = pC[:, g * 512:(g + 1) * 512]
                P.op("pe", lambda e, oy=oy, rb=rb: e.matmul(oy, lhsT=CA, rhs=rb, start=True, stop=False), reads=[tb, cr], writes=[pC])
                P.op("pe", lambda e, oy=oy, ib=ib: e.matmul(oy, lhsT=SA, rhs=ib, start=False, stop=True), reads=[tb, ci], writes=[pC])
            yield

        nbs = NB if nb_limit is None else nb_limit
        for bi_ in range(s, nbs, NS):
            c0 = bi_ * CB
            P.dma("sp", hp, hp[:], G.wsrc, hv[occ, bi_, :].partition_broadcast(Pn))
            P.dma("sp", rn, rn[:], normd, normd.t.ap()[:, c0:c0 + CB].partition_broadcast(Pn))
            for q in range(3):
                ut_, urow = uT.at(q * E + c0, q * E + c0 + CB)
                halo = bass.AP(urow.tensor, c0 * W, [[Pn, Pn], [W, CB], [1, Pn + 2]])
                P.dma("sp", xs, xs[:, q * CB:(q + 1) * CB, :], ut_, halo)
            ut_, urow = uT.at(3 * E + c0, 3 * E + c0 + CB)
            P.dma("sp", gt, gt[:], ut_, urow[:, 1:1 + L].rearrange("c (a b) -> a c b", b=Pn))
            for o in range(2):
                P.dma("sp", kt[o], kt[o][:], tapsd, tapsd.t.ap()[o, c0:c0 + CB, :].rearrange("c (h a b) -> a c h b", h=2, b=Pn))
            yield
            P.op("dve", lambda e: e.reciprocal(out=rn[:], in_=rn[:]), reads=[rn], writes=[rn])
            P.op("dve", lambda e: e.tensor_tensor(out=vx[:], in0=xs[:, :, 0:Pn], in1=bcw(hp[:, 0:3 * CB], 3 * CB), op=ALU.mult), reads=[xs, hp], writes=[vx])
            P.op("pool", lambda e: e.tensor_tensor(out=tca[:], in0=xs[:, :, 1:Pn + 1], in1=bcw(hp[:, 3 * CB:6 * CB], 3 * CB), op=ALU.mult), reads=[xs, hp], writes=[tca])
            yield
            P.op("pool", lambda e: e.tensor_tensor(out=tcb[:], in0=xs[:, :, 2:Pn + 2], in1=bcw(hp[:, 6 * CB:9 * CB], 3 * CB), op=ALU.mult), reads=[xs, hp], writes=[tcb])
            P.op("dve", lambda e: e.tensor_tensor(out=vx[:], in0=vx[:], in1=bcw(hp[:, 9 * CB:12 * CB], 3 * CB), op=ALU.add), reads=[vx, hp], writes=[vx])
            yield
            P.op("dve", lambda e: e.tensor_tensor(out=vx[:], in0=vx[:], in1=tca[:], op=ALU.add), reads=[vx, tca], writes=[vx])
            yield
            P.op("dve", lambda e: e.tensor_tensor(out=vx[:], in0=vx[:], in1=tcb[:], op=ALU.add), reads=[vx, tcb], writes=[vx])
            yield
            P.op("act", lambda e: e.copy(out=vb[:], in_=vx[:, 0:CB, :]), reads=[vx], writes=[vb])
            P.op("act", lambda e: e.activation(out=gt[:], in_=gt[:], func=AF.Silu), reads=[gt], writes=[gt])
            yield
            for o in range(2):
                Y_ = yield from fwd([[(kt[o][:, cb, 0, :], FA0, kt[o]), (kt[o][:, cb, 1, :], FA1, kt[o])] for cb in range(CB)])
                P.op("dve", lambda e, Y_=Y_, o=o: e.tensor_tensor(
                    out=KF[o][:], in0=Y_[:].rearrange("p (r c k) -> p r c k", r=2, c=CB),
                    in1=rn[:, o, :].unsqueeze(1).unsqueeze(3).to_broadcast([Pn, 2, CB, Pn]), op=ALU.mult), reads=[Y_, rn], writes=[KF[o]])
                yield
            Y_ = yield from fwd([[(vb[:, cb, :], FA0, vb)] for cb in range(CB)])
            yield from inv(Y_, 0)
            P.op("pool", lambda e: e.tensor_tensor(out=tcm[:], in0=vx[:, 0:CB, :], in1=bcw(hp[:, 12 * CB:13 * CB], CB), op=ALU.mult), reads=[vx, hp], writes=[tcm])
            yield
            P.op("dve", lambda e: e.tensor_tensor(out=tcm[:], in0=tcm[:], in1=yv, op=ALU.add), reads=[tcm, pC], writes=[tcm])
            yield
            P.op("dve", lambda e: e.tensor_tensor(out=zz[:], in0=tcm[:], in1=vx[:, CB:2 * CB, :], op=ALU.mult), reads=[tcm, vx], writes=[zz])
            yield
            P.op("act", lambda e: e.copy(out=zzb[:], in_=zz[:]), reads=[zz], writes=[zzb])
            yield
            Y_ = yield from fwd([[(zzb[:, cb, :], FA0, zzb)] for cb in range(CB)])
            yield from inv(Y_, 1)
            P.op("pool", lambda e: e.tensor_tensor(out=tcm[:], in0=zz[:], in1=bcw(hp[:, 13 * CB:14 * CB], CB), op=ALU.mult), reads=[zz, hp], writes=[tcm])
            P.op("dve", lambda e: e.tensor_tensor(out=zz[:], in0=vx[:, 2 * CB:3 * CB, :], in1=gt[:], op=ALU.mult), reads=[vx, gt], writes=[zz])
            yield
            P.op("dve", lambda e: e.tensor_tensor(out=tcm[:], in0=tcm[:], in1=yv, op=ALU.add), reads=[tcm, pC], writes=[tcm])
            yield
            P.op("dve", lambda e: e.tensor_tensor(out=ygo[:], in0=tcm[:], in1=zz[:], op=ALU.mult), reads=[tcm, zz], writes=[ygo])
            yield
            P.dma("sp", ygT, ygT.t.ap()[c0:c0 + CB, :].rearrange("c (a b) -> a c b", b=Pn), ygo, ygo[:])
            yield

    run_streams([stream(s) for s in range(NS)], stagger=40)
    P.end()


class G_:
    pass


def emit_setup(P, G, L):
    P.begin("setup")
    zt = P.sb("zpad", [128, 4], F32)
    P.op("dve", lambda e: e.memset(G.ones[:], 1.0), writes=[G.ones])
    P.op("dve", lambda e: e.memset(G.eps[:], LN_EPS), writes=[G.eps])
    P.op("pool", lambda e: e.memset(zt[:], 0.0), writes=[zt])
    P.dma("sp", G.ident, G.ident[:], G.wsrc, G.identd.t.ap())
    P.dma("sp", G.csil, G.csil[:], G.wsrc, G.cc.t.ap())
    P.op("act", lambda e: e.activation(out=G.csil[:], in_=G.csil[:], func=AF.Silu), reads=[G.csil], writes=[G.csil])
    for t_, n in [(t, L) for t in G.uT.ts] + [(t, CTX) for t in G.ucT.ts]:
        rows = t_.t.ap().shape[0]
        for r0 in range(0, rows, 128):
            P.dma("sp", t_, t_.t.ap()[r0:r0 + 128, 0:1], zt, zt[:, 0:1], allow_slow_non_contiguous=True)
            P.dma("act", t_, t_.t.ap()[r0:r0 + 128, 1 + n:3 + n], zt, zt[:, 0:2], allow_slow_non_contiguous=True)
    P.end()


def build_nc(L, nlayers=DEPTH, debug=False):
    nc = bass.Bass("TRN2", target_bir_lowering=False)
    P = Prog(nc)
    G = G_()

    def inp(name, shape, dt=F32):
        return P.dram(name, shape, dt, kind="ExternalInput")

    G.x_in = inp("x", [L, D])
    G.ctx_in = inp("ctx", [CTX, D])
    G.cc = inp("cc", [128, 2, 8])
    G.identd = inp("ident", [128, 128], BF16)
    G.w_mod = inp("w_mod", [DEPTH, D, 3 * D])
    G.b_mod = inp("b_mod", [DEPTH, 3 * D])
    G.ln_g = inp("ln_g", [DEPTH, D])
    G.ln_b = inp("ln_b", [DEPTH, D])
    G.rg_w_in = inp("rg_w_in", [2, D, 2 * E])
    G.rgpar = inp("rgpar", [2, 96, E // 96, 16])
    G.rgw = inp("rgw", [2, E // 96, 96, 4, 96])
    G.rg_w_out = inp("rg_w_out", [2, E, D])
    G.hy_w_in = inp("hy_w_in", [2, D, 4 * E])
    G.hy_w_out = inp("hy_w_out", [2, E, D])
    G.hy_f_w1 = inp("hy_f_w1", [2, NEMB, FH])
    G.hy_f_w2 = inp("hy_f_w2", [2, FH, FH])
    G.hy_f_w3 = inp("hy_f_w3", [2, FH, 4 * E])
    G.hyfp = inp("hyfp", [2, FH, 3])
    G.hypar = inp("hypar", [2, E // CBL, 14 * CBL])
    G.hyparc = inp("hyparc", [2, E // CBC, 14 * CBC])
    G.tb16c = inp("tb16c", [16, 13 * 16], BF16)
    G.tf32c = inp("tf32c", [16, 4 * 16])
    G.ztc = inp("ztc", [NEMB, CTX])
    G.ztrc = inp("ztrc", [NEMB, CTX])
    G.trowc = inp("trowc", [2, CTX])
    G.tapsc = P.dram("tapsc", [2, E, 2 * CTX], BF16)
    G.normc = P.dram("normc", [2, E], F32)
    G.negd = inp("negd", [128, E // 128])
    G.tb16 = inp("tb16", [128, 1664], BF16)
    G.tf32 = inp("tf32", [128, 512])
    G.zt = inp("zt", [NEMB, L])
    G.ztr = inp("ztr", [NEMB, L])
    G.trow = inp("trow", [2, L])
    G.tapsd = P.dram("tapsd", [2, E, 2 * L], BF16, kind=("ExternalOutput" if debug == 2 else "Internal"))
    G.normd = P.dram("normd", [2, E], F32, kind=("ExternalOutput" if debug == 2 else "Internal"))
    G.wsrc = G.w_mod
    out = P.dram("out", [L, D], F32, kind="ExternalOutput")
    kind = "ExternalOutput" if debug else "Internal"
    G.xA = P.dram("xA", [L, D], F32, kind=kind)
    G.xB = P.dram("xB", [L, D], F32)
    G.ctxA = P.dram("ctxA", [CTX, D], F32, kind=kind)
    G.ctxB = P.dram("ctxB", [CTX, D], F32, kind=kind)
    G.uT = Q([P.dram("uT%d" % q, [E, L + 3], F32, kind=kind) for q in range(4)], E)
    G.ucT = Q([P.dram("ucT%d" % q, [E, CTX + 3], F32) for q in range(4)], E)
    G.ygT = P.dram("ygT", [E, L], BF16, kind=kind)
    G.ygcT = P.dram("ygcT", [E, CTX], BF16)
    G.hfd = [P.dram("hfd%d" % i, [96, L], BF16) for i in range(E // 96)]
    G.ones = P.sb("ones", [128, 128], F32, persist=True)
    G.ident = P.sb("identsb", [128, 128], BF16, persist=True)
    G.csil = P.sb("csil", [128, 2, 8], F32, persist=True)
    G.eps = P.sb("epsc", [128, 1], F32, persist=True)
    G.modt = [P.sb("modt%d" % j, [128, 3 * D], F32, persist=True) for j in range(2)]

    emit_setup(P, G, L)
    if debug == 2:
        G.dbg_hid = P.dram("dbg_hid", [2, FH, L], BF16, kind="ExternalOutput")
        G.dbg_nacc = P.dram("dbg_nacc", [128, 2, E // 128], F32, kind="ExternalOutput")
        emit_mod(P, G, 1)
        emit_proj(P, G, G.x_in, L, 0, G.hy_w_in.t.ap()[0], 4 * E, 128, G.uT, 1, False)
        emit_hyfilt(P, G, 0, L, L, G.zt, G.ztr, G.trow, G.tapsd, G.normd)
        emit_hyconv(P, G, 0, G.uT, L, 128, CBL, G.tb16, G.tf32, G.hypar, G.tapsd, G.normd, G.ygT, nb_limit=2)
        P.begin()
        P.op("sp", lambda e: e.dma_start(out=out.t.ap()[0:1, 0:2], in_=G.ygcT.t.ap()[1:2, 0:1].bitcast(F32) if False else G.xB.t.ap()[1:2, 0:2]), reads=[G.ygT], writes=[out], dma=out)
        P.end()
        P.root.close()
        return nc
    xs, cs = G.x_in, G.ctx_in
    xbufs, cbufs = [G.xA, G.xB], [G.ctxA, G.ctxB]
    for l in range(nlayers):
        kindl, occ = l % 2, l // 2
        need_ctx = any(jj % 2 == 0 for jj in range(l + 1, DEPTH))
        colmajor = occ % 2 == 1
        xd = out if l == nlayers - 1 and not debug else xbufs[l % 2]
        if l == nlayers - 1 and debug:
            xd = out
        cd = cbufs[l % 2]
        emit_mod(P, G, l)
        if kindl == 0:
            emit_proj(P, G, xs, L, 0, G.rg_w_in.t.ap()[occ], 2 * E, 96, G.uT, 1, colmajor)
            emit_proj(P, G, cs, CTX, 1, G.rg_w_in.t.ap()[occ], 2 * E, 96, G.ucT, 1, False)
            emit_rg(P, G, occ, G.uT, G.ucT, 1, L, G.ygT, G.ygcT, need_ctx)
            emit_out(P, G, l, xs, xd, L, 0, G.ygT, G.rg_w_out.t.ap()[occ], 96, colmajor)
            if need_ctx:
                emit_out(P, G, l, cs, cd, CTX, 1, G.ygcT, G.rg_w_out.t.ap()[occ], 96, False)
        else:
            emit_proj(P, G, xs, L, 0, G.hy_w_in.t.ap()[occ], 4 * E, 128, G.uT, 1, colmajor)
            emit_hyfilt(P, G, occ, L, L, G.zt, G.ztr, G.trow, G.tapsd, G.normd)
            emit_hyconv(P, G, occ, G.uT, L, 128, CBL, G.tb16, G.tf32, G.hypar, G.tapsd, G.normd, G.ygT)
            emit_out(P, G, l, xs, xd, L, 0, G.ygT, G.hy_w_out.t.ap()[occ], 128, colmajor)
            if need_ctx and not SKIP_CTX_HY:
                emit_proj(P, G, cs, CTX, 1, G.hy_w_in.t.ap()[occ], 4 * E, 128, G.ucT, 1, False)
                emit_hyfilt(P, G, occ, CTX, CTX, G.ztc, G.ztrc, G.trowc, G.tapsc, G.normc)
                emit_hyconv(P, G, occ, G.ucT, CTX, 16, CBC, G.tb16c, G.tf32c, G.hyparc, G.tapsc, G.normc, G.ygcT)
                emit_out(P, G, l, cs, cd, CTX, 1, G.ygcT, G.hy_w_out.t.ap()[occ], 128, False)
        xs = xd
        if need_ctx:
            cs = cd
    P.begin()
    P.op("sp", lambda e: e.dma_start(out=G.ygcT.t.ap()[0:1, 0:2], in_=G.ygcT.t.ap()[1:2, 0:2]), reads=[out], writes=[G.ygcT], dma=G.ygcT)
    P.end()
    P.root.close()
    return nc


def emb_tables(Lf):
    t = np.linspace(0.0, 1.0, Lf, dtype=np.float32)
    bands = np.linspace(1e-4, 7, 8, dtype=np.float32)
    w = (np.float32(2.0 * np.pi) * np.arange(Lf, dtype=np.float32) / np.float32(Lf))
    z = np.concatenate([t[:, None], np.cos(bands[None, :] * w[:, None]), -np.sin(bands[None, :] * w[:, None])], axis=1).astype(np.float32)
    idx = np.concatenate([[0], Lf - np.arange(1, Lf)])
    zt = np.ascontiguousarray(z.T)
    ztr = np.ascontiguousarray(z[idx].T)
    trow = np.stack([t, t[idx]]).astype(np.float32)
    return zt, ztr, trow


def hy_tables(L, Pn=128, sfx=""):
    import ml_dtypes
    assert L == Pn * Pn
    N = 2 * L
    N1 = 2 * Pn
    n1 = np.arange(Pn, dtype=np.float64)[:, None]
    k1 = np.arange(Pn, dtype=np.float64)[None, :]
    ang = 2 * np.pi * n1 * (k1 + 0.5) / N1
    FA0 = np.concatenate([np.cos(ang), -np.sin(ang)], 1)
    ang1 = 2 * np.pi * (n1 + Pn) * (k1 + 0.5) / N1
    FA1 = -np.concatenate([np.cos(ang1), -np.sin(ang1)], 1)
    angt = 2 * np.pi * n1 * (k1 + 0.5) / N
    angc = 2 * np.pi * n1 * k1 / Pn
    FCre, FCim = np.cos(angc), -np.sin(angc)
    FCc = np.concatenate([np.cos(angc.T), np.sin(angc.T)], 1)
    FCc2 = np.concatenate([-np.sin(angc.T), np.cos(angc.T)], 1)
    CA = (2.0 / N) * np.cos(ang.T)
    SA = -(2.0 / N) * np.sin(ang.T)
    tb16 = np.concatenate([FA0, FA1, FCre, FCim, -FCim, FCc, FCc2, CA, SA], 1).astype(np.float32).astype(ml_dtypes.bfloat16)
    tf32 = np.concatenate([np.cos(angt), -np.sin(angt), np.cos(angt.T), np.sin(angt.T)], 1).astype(np.float32)
    zt, ztr, trow = emb_tables(L)
    max_decay = np.log(1e-2) / 0.3
    min_decay = np.log(1e-2) / 1.5
    deltas = np.abs(np.linspace(min_decay, max_decay, E, dtype=np.float32))
    negd = np.ascontiguousarray((-deltas).reshape(E // 128, 128).T).astype(np.float32)
    return {"tb16" + sfx: tb16, "tf32" + sfx: tf32, "zt" + sfx: zt, "ztr" + sfx: ztr, "trow" + sfx: trow, "negd": negd}


def host_inputs(inputs, b):
    import ml_dtypes
    f = lambda a: np.ascontiguousarray(a, dtype=np.float32)
    cc = np.zeros((128, 2, 8), np.float32)
    cc[:, 0, :] = inputs["c"][b].reshape(8, 128).T
    cc[:, 1, :] = inputs["c_ctx"].reshape(8, 128).T
    NT = E // 96
    rgpar = np.zeros((2, 96, NT, 16), np.float32)

    def chan(v):
        return v.reshape(2, NT, 96).transpose(0, 2, 1)

    for k in range(4):
        rgpar[:, :, :, k] = chan(inputs["rg_conv_w"][:, k, :])
    rgpar[:, :, :, 4] = chan(inputs["rg_conv_b"])
    for d in range(2):
        rgpar[:, :, :, 5 + 2 * d] = chan(inputs["rg_b_r"][:, d, :])
        rgpar[:, :, :, 6 + 2 * d] = chan(inputs["rg_b_i"][:, d, :])
        rgpar[:, :, :, 9 + d] = chan(inputs["rg_lambda"][:, d, :])
    rgw = np.zeros((2, NT, 96, 4, 96), np.float32)
    for d in range(2):
        rgw[:, :, :, 2 * d + 0, :] = inputs["rg_w_r"][:, d]
        rgw[:, :, :, 2 * d + 1, :] = inputs["rg_w_i"][:, d]
    L = inputs["x"].shape[1]
    hy = hy_tables(L)
    hy.update(hy_tables(CTX, 16, "c"))

    def mk_hypar(CB):
        NB = E // CB
        hypar = np.zeros((2, NB, 14 * CB), np.float32)
        cw = inputs["hy_conv_w"].reshape(2, 3, 3, NB, CB)
        for tap in range(3):
            hypar[:, :, tap * 3 * CB:(tap + 1) * 3 * CB] = cw[:, tap].transpose(0, 2, 1, 3).reshape(2, NB, 3 * CB)
        hypar[:, :, 9 * CB:12 * CB] = inputs["hy_conv_b"].reshape(2, 3, NB, CB).transpose(0, 2, 1, 3).reshape(2, NB, 3 * CB)
        hypar[:, :, 12 * CB:13 * CB] = inputs["hy_d"][:, 0].reshape(2, NB, CB)
        hypar[:, :, 13 * CB:14 * CB] = inputs["hy_d"][:, 1].reshape(2, NB, CB)
        return hypar

    hypar = mk_hypar(CBL)
    hyparc = mk_hypar(CBC)
    hyfp = np.stack([inputs["hy_f_freq"], inputs["hy_f_b1"], inputs["hy_f_b2"]], axis=-1).astype(np.float32)
    extra = {"hy_w_in": f(inputs["hy_w_in"]), "hy_w_out": f(inputs["hy_w_out"]), "hy_f_w1": f(inputs["hy_f_w1"]),
             "hy_f_w2": f(inputs["hy_f_w2"]), "hy_f_w3": f(inputs["hy_f_w3"]), "hyfp": hyfp, "hypar": hypar, "hyparc": hyparc}
    extra.update(hy)
    return {
        **extra,
        "x": f(inputs["x"][b]), "ctx": f(inputs["ctx"][b]), "cc": cc,
        "ident": np.eye(128, dtype=np.float32).astype(ml_dtypes.bfloat16),
        "w_mod": f(inputs["w_mod"]), "b_mod": f(inputs["b_mod"]), "ln_g": f(inputs["ln_g"]), "ln_b": f(inputs["ln_b"]),
        "rg_w_in": f(inputs["rg_w_in"]), "rgpar": rgpar, "rgw": rgw, "rg_w_out": f(inputs["rg_w_out"]),
    }


def kernel(**inputs):
    L = inputs["x"].shape[1]
    nc = build_nc(L)
    in_maps = [host_inputs(inputs, b) for b in range(2)]
    res = run_bass_kernel_spmd(nc, in_maps, core_ids=[0, 1])
    return np.stack([res.results[b]["out"] for b in range(2)], axis=0).astype(np.float32)
```

```python
import contextlib
import numpy as np
import concourse.bass as bass
import concourse.mybir as mybir
from concourse.bass_utils import run_bass_kernel_spmd

F32 = mybir.dt.float32
BF16 = mybir.dt.bfloat16
ALU = mybir.AluOpType
AF = mybir.ActivationFunctionType
AX = mybir.AxisListType

D = 1024
E = 1536
GRID_W = 64
CTX = 256
DEPTH = 4
ALPHA = (2 * DEPTH) ** 0.25
LN_EPS = 1e-5


class Buf:
    def __init__(self, name):
        self.name = name
        self.w = {}
        self.r = {}


class T:
    def __init__(self, t, name):
        self.t = t
        self.buf = Buf(name)

    def __getitem__(self, k):
        return self.t[k]


class Q:
    def __init__(self, ts, rows_per):
        self.ts = ts
        self.rp = rows_per

    def at(self, r0, r1):
        q = r0 // self.rp
        assert (r1 - 1) // self.rp == q
        t = self.ts[q]
        return t, t.t.ap()[r0 - q * self.rp:r1 - q * self.rp]


class Prog:
    ENG = ("sp", "act", "dve", "pool", "pe")

    def __init__(self, nc):
        self.nc = nc
        self.root = contextlib.ExitStack()
        self.phase = None
        self.sems = {}
        self.latest = {}
        self.ops = {e: [] for e in self.ENG}
        self.waited = {e: {} for e in self.ENG}
        self.nsem = 0

    def sem(self, key):
        if key not in self.sems:
            free = getattr(self, "free_phys", None)
            if free:
                h, base = free.pop()
                self.sems[key] = h
                self.latest[key] = base
            else:
                self.nsem += 1
                self.sems[key] = self.root.enter_context(self.nc.semaphore("s%d" % self.nsem))
            if self.phase is not None and isinstance(key, tuple) and key[1] in getattr(self, "phase_names", ()):
                self.phase_keys.append(key)
        return self.sems[key]

    def _stack(self, persist):
        return self.root if (persist or self.phase is None) else self.phase

    def _scope_id(self):
        self.nscope = getattr(self, "nscope", 0) + 1
        return self.nscope

    def _uname(self, name):
        self.nname = getattr(self, "nname", 0) + 1
        return "%s_%d" % (name, self.nname)

    def sb(self, name, shape, dt, persist=False):
        name = self._uname(name)
        if not persist and self.phase is not None:
            self.phase_names.add(name)
        t = self._stack(persist).enter_context(self.nc.sbuf_tensor(name, list(shape), dt))
        return T(t, name)

    def ps(self, name, shape, dt, persist=False):
        name = self._uname(name)
        t = self._stack(persist).enter_context(self.nc.psum_tensor(name, list(shape), dt))
        return T(t, name)

    def dram(self, name, shape, dt, kind="Internal"):
        t = self.nc.dram_tensor(name, list(shape), dt, kind=kind)
        return T(t, name)

    def op(self, eng, fn, reads=(), writes=(), dma=None):
        need = {}

        def add(evs):
            for k, v in evs.items():
                if need.get(k, 0) < v:
                    need[k] = v

        own = ("dma", dma.buf.name) if dma is not None else None
        for b in reads:
            add(b.buf.w)
        for b in writes:
            add({k: v for k, v in b.buf.w.items() if k != own})
            add(b.buf.r)
        waits = []
        for k, v in need.items():
            if k == eng and eng == "pe":
                continue
            if k not in self.sems:
                continue
            if self.waited[eng].get(k, 0) >= v:
                continue
            self.waited[eng][k] = v
            waits.append((k, v))
        if dma is not None:
            key = ("dma", dma.buf.name)
            inc = 16
        else:
            key = eng
            inc = 1
        self.sem(key)
        val = self.latest.get(key, 0) + inc
        self.latest[key] = val
        self.ops[eng].append((waits, fn, key, inc))
        for b in reads:
            if b.buf.r.get(key, 0) < val:
                b.buf.r[key] = val
        for b in writes:
            if b.buf.w.get(key, 0) < val:
                b.buf.w[key] = val

    def dma(self, q, out_t, out_ap, in_t, in_ap, **kw):
        self.op(q, lambda e: e.dma_start(out=out_ap, in_=in_ap, **kw), reads=[in_t], writes=[out_t], dma=out_t)

    def barrier(self):
        for e in self.ENG:
            waits = []
            for k, v in self.latest.items():
                if self.waited[e].get(k, 0) >= v:
                    continue
                self.waited[e][k] = v
                waits.append((k, v))
            self.ops[e].append((waits, None, None, 0))

    def begin(self, name="phase"):
        self.phase_label = name
        self.phase = contextlib.ExitStack()
        self.phase_names = set()
        self.phase_keys = []
        if not hasattr(self, "free_phys"):
            self.free_phys = []

    def end(self):
        self.barrier()
        scope = self.nc.named_scope("%s_%d" % (self.phase_label, self._scope_id()), notify=True) if PROFILE_SCOPES else contextlib.nullcontext()
        with scope, self.nc.Block() as block:
            decos = {"sp": block.sync, "act": block.scalar, "dve": block.vector,
                     "pool": block.gpsimd, "pe": block.tensor}
            for en in self.ENG:
                ops = self.ops[en]

                def body(e, ops=ops):
                    for waits, fn, key, inc in ops:
                        for k, v in waits:
                            e.wait_ge(self.sem(k), v)
                        if fn is not None:
                            fn(e).then_inc(self.sem(key), inc)

                decos[en](body)
        self.ops = {e: [] for e in self.ENG}
        for key in self.phase_keys:
            self.free_phys.append((self.sems.pop(key), self.latest.pop(key)))
            for e in self.ENG:
                self.waited[e].pop(key, None)
        self.phase.close()
        self.phase = None


def emit_mod(P, G, l):
    P.begin("mod")
    wm = [P.sb("wm%d" % i, [128, 8, 512], F32) for i in range(2)]
    rep = P.sb("rep", [128, 2, 8, 128], F32)
    bm = P.sb("bm", [1, 3072], F32)
    pm = [P.ps("pmod%d" % i, [128, 512], F32) for i in range(2)]
    wv = G.w_mod.t.ap()[l].rearrange("(k p) n -> p k n", p=128)
    P.dma("act", bm, bm[:], G.wsrc, G.b_mod.t.ap()[l:l + 1, :])
    for j in range(2):
        for k in range(8):
            P.op("dve", lambda e, j=j, k=k: e.tensor_scalar_mul(out=rep[:, j, k, :], in0=G.ones[:], scalar1=G.csil[:, j, k:k + 1]),
                 reads=[G.ones, G.csil], writes=[rep])
    for n in range(6):
        w = wm[n % 2]
        P.dma("sp", w, w[:], G.wsrc, wv[:, :, n * 512:(n + 1) * 512])
        for j in range(2):
            p = pm[j]
            for k in range(8):
                P.op("pe", lambda e, p=p, j=j, k=k, w=w: e.matmul(p[:], lhsT=rep[:, j, k, :], rhs=w[:, k, :],
                                                                start=(k == 0), stop=False),
                     reads=[rep, w], writes=[p])
            P.op("pe", lambda e, p=p, n=n: e.matmul(p[:], lhsT=G.ones[0:1, :], rhs=bm[0:1, n * 512:(n + 1) * 512],
                                                    start=False, stop=True), reads=[G.ones, bm], writes=[p])
            P.op("act", lambda e, p=p, j=j, n=n: e.copy(out=G.modt[j][:, n * 512:(n + 1) * 512], in_=p[:]),
                 reads=[p], writes=[G.modt[j]])
    for j in range(2):
        P.op("dve", lambda e, j=j: e.tensor_scalar_add(out=G.modt[j][:, 1024:2048], in0=G.modt[j][:, 1024:2048], scalar1=1.0),
             reads=[G.modt[j]], writes=[G.modt[j]])
    P.end()


def x_rows(xt, ti, colmajor, L):
    ap = xt.t.ap()
    if not colmajor:
        return ap[ti * 128:(ti + 1) * 128, :]
    rows = L // GRID_W
    per_col = rows // 128
    col, rb = ti // per_col, ti % per_col
    v = ap.rearrange("(r c) d -> c r d", c=GRID_W)
    return v[col, rb * 128:(rb + 1) * 128, :]


def emit_proj(P, G, xsrc, L, j, w_dram, W, MT, outT, pad, colmajor):
    P.begin("proj")
    NK = 8
    wbf = P.sb("wbf", [128, NK, W], BF16)
    wst = [P.sb("wst%d" % i, [128, 1536], F32) for i in range(2)]
    xt = [P.sb("xt%d" % i, [128, D], F32) for i in range(3)]
    hb = [P.sb("hb%d" % i, [128, D], BF16) for i in range(2)]
    hT = [P.sb("hT%d" % i, [128, NK, 512], BF16) for i in range(2)]
    ost = [P.sb("ost%d" % i, [128, 512], F32) for i in range(4)]
    pT = [P.ps("pT%d" % i, [128, NK * 128], BF16) for i in range(2)]
    pm = [P.ps("pm%d" % i, [128, 512], F32) for i in range(4)]
    modt = G.modt[j]
    wv = w_dram
    i = 0
    for k in range(NK):
        for cc in range(W // 1536):
            s = wst[i % 2]
            P.dma("sp" if i % 2 == 0 else "act", s, s[:], G.wsrc, wv[k * 128:(k + 1) * 128, cc * 1536:(cc + 1) * 1536])
            eng = "pool" if i % 2 == 0 else "dve"
            P.op(eng, lambda e, s=s, k=k, cc=cc: e.tensor_copy(out=wbf[:, k, cc * 1536:(cc + 1) * 1536], in_=s[:]),
                 reads=[s], writes=[wbf])
            i += 1
    CH = min(512, L)
    NS = CH // 128
    ntile = L // 128
    nchunk = L // CH
    nmt = W // MT

    def load(ti):
        t = xt[ti % 3]
        P.dma("sp", t, t[:], xsrc, x_rows(xsrc, ti, colmajor, L))

    load(0)
    load(1)
    ev = 0
    for c in range(nchunk):
        h = hT[c % 2]
        for sub in range(NS):
            ti = c * NS + sub
            if ti + 2 < ntile:
                load(ti + 2)
            t = xt[ti % 3]
            b = hb[ti % 2]
            p = pT[ti % 2]
            P.op("dve", lambda e, t=t: e.tensor_tensor(out=t[:], in0=t[:], in1=modt[:, 1024:2048], op=ALU.mult),
                 reads=[t, modt], writes=[t])
            P.op("pool", lambda e, t=t, b=b: e.tensor_tensor(out=b[:], in0=t[:], in1=modt[:, 0:1024], op=ALU.add),
                 reads=[t, modt], writes=[b])
            for k in range(NK):
                P.op("pe", lambda e, p=p, b=b, k=k: e.transpose(out=p[:, k * 128:(k + 1) * 128], in_=b[:, k * 128:(k + 1) * 128],
                                                              identity=G.ident[:]),
                     reads=[b, G.ident], writes=[p])
            P.op("act", lambda e, h=h, p=p, sub=sub: e.copy(out=h[:, :, sub * 128:(sub + 1) * 128],
                                                            in_=p[:].rearrange("p (k t) -> p k t", k=NK)),
                 reads=[p], writes=[h])
        for mt in range(nmt):
            p = pm[ev % 4]
            o = ost[ev % 4]
            for k in range(NK):
                P.op("pe", lambda e, p=p, h=h, k=k, mt=mt: e.matmul(p[:MT, :CH], lhsT=wbf[:, k, mt * MT:(mt + 1) * MT], rhs=h[:, k, :CH],
                                                                  start=(k == 0), stop=(k == NK - 1)),
                     reads=[wbf, h], writes=[p])
            if ev % 2 == 0:
                P.op("act", lambda e, p=p, o=o: e.copy(out=o[:MT, :CH], in_=p[:MT, :CH]), reads=[p], writes=[o])
            else:
                P.op("dve", lambda e, p=p, o=o: e.tensor_copy(out=o[:MT, :CH], in_=p[:MT, :CH]), reads=[p], writes=[o])
            ot_, orow = outT.at(mt * MT, (mt + 1) * MT)
            P.dma("sp" if ev % 2 == 0 else "act", ot_, orow[:, pad + c * CH: pad + (c + 1) * CH], o, o[:MT, :CH])
            ev += 1
    P.end()


def emit_out(P, G, l, xsrc, xdst, L, j, ygT, w_dram, KT, colmajor):
    P.begin("out")
    NKT = E // KT
    wbf = P.sb("wobf", [128, NKT, D], BF16)
    wst = [P.sb("wost%d" % i, [128, D], F32) for i in range(2)]
    yg = [P.sb("yg%d" % i, [128, NKT, 512], BF16) for i in range(2)]
    xt = [P.sb("xo%d" % i, [128, D], F32) for i in range(3)]
    zt = [P.sb("zt%d" % i, [128, D], F32) for i in range(2)]
    st = [P.sb("st%d" % i, [128, 2, 6], F32) for i in range(2)]
    mv = [P.sb("mv%d" % i, [128, 4], F32) for i in range(2)]
    po = [P.ps("po%d" % i, [128, D], F32) for i in range(2)]
    modt = G.modt[j]
    for k in range(NKT):
        s = wst[k % 2]
        P.dma("sp" if k % 2 == 0 else "act", s, s[:KT, :], G.wsrc, w_dram[k * KT:(k + 1) * KT, :])
        P.op("pool" if k % 2 == 0 else "dve", lambda e, s=s, k=k: e.tensor_copy(out=wbf[:KT, k, :], in_=s[:KT, :]),
             reads=[s], writes=[wbf])
    CH = min(512, L)
    NS = CH // 128
    ntile = L // 128
    nchunk = L // CH
    ygv = ygT.t.ap().rearrange("(k p) t -> p k t", p=KT)
    lng = P.sb("lng", [128, D], F32)
    lnb = P.sb("lnb", [128, D], F32)
    P.dma("sp", lng, lng[:], G.wsrc, G.ln_g.t.ap()[l:l + 1, :].partition_broadcast(128))
    P.dma("act", lnb, lnb[:], G.wsrc, G.ln_b.t.ap()[l:l + 1, :].partition_broadcast(128))

    def loadx(ti):
        t = xt[ti % 3]
        P.dma("sp", t, t[:], xsrc, x_rows(xsrc, ti, colmajor, L))

    def loady(c):
        y = yg[c % 2]
        P.dma("act", y, y[:KT, :, :CH], ygT, ygv[:, :, c * CH:(c + 1) * CH])

    loadx(0)
    loadx(1)
    loady(0)
    for c in range(nchunk):
        if c + 1 < nchunk:
            loady(c + 1)
        y = yg[c % 2]
        for sub in range(NS):
            ti = c * NS + sub
            if ti + 2 < ntile:
                loadx(ti + 2)
            t = xt[ti % 3]
            p = po[ti % 2]
            z = zt[ti % 2]
            s = st[ti % 2]
            m = mv[ti % 2]
            for n in range(2):
                for k in range(NKT):
                    P.op("pe", lambda e, p=p, y=y, k=k, n=n, sub=sub: e.matmul(
                        p[:, n * 512:(n + 1) * 512], lhsT=y[:KT, k, sub * 128:(sub + 1) * 128],
                        rhs=wbf[:KT, k, n * 512:(n + 1) * 512], start=(k == 0), stop=(k == NKT - 1)),
                        reads=[y, wbf], writes=[p])
            P.op("dve", lambda e, p=p, z=z: e.tensor_tensor(out=z[:], in0=p[:], in1=modt[:, 2048:3072], op=ALU.mult),
                 reads=[p, modt], writes=[z])
            P.op("dve", lambda e, t=t, z=z: e.scalar_tensor_tensor(out=z[:], in0=t[:], scalar=ALPHA, in1=z[:],
                                                                   op0=ALU.mult, op1=ALU.add),
                 reads=[t, z], writes=[z])
            for n in range(2):
                P.op("dve", lambda e, s=s, z=z, n=n: e.bn_stats(out=s[:, n, :], in_=z[:, n * 512:(n + 1) * 512]),
                     reads=[z], writes=[s])
            P.op("dve", lambda e, s=s, m=m: e.bn_aggr(out=m[:, 0:2], in_=s[:].rearrange("p a b -> p (a b)")),
                 reads=[s], writes=[m])
            P.op("act", lambda e, m=m: e.activation(out=m[:, 2:3], in_=m[:, 1:2], func=AF.Sqrt, bias=G.eps[:, 0:1], scale=1.0),
                 reads=[m, G.eps], writes=[m])
            P.op("dve", lambda e, m=m: e.reciprocal(out=m[:, 3:4], in_=m[:, 2:3]), reads=[m], writes=[m])
            P.op("dve", lambda e, m=m, z=z: e.tensor_scalar(out=z[:], in0=z[:], scalar1=m[:, 0:1], scalar2=m[:, 3:4],
                                                            op0=ALU.subtract, op1=ALU.mult), reads=[m, z], writes=[z])
            P.op("pool", lambda e, z=z: e.tensor_tensor(out=z[:], in0=z[:], in1=lng[:], op=ALU.mult),
                 reads=[z, lng], writes=[z])
            P.op("pool", lambda e, z=z: e.tensor_tensor(out=z[:], in0=z[:], in1=lnb[:], op=ALU.add),
                 reads=[z, lnb], writes=[z])
            P.dma("sp", xdst, x_rows(xdst, ti, colmajor, L), z, z[:])
    P.end()


def run_streams(gens, stagger=0):
    active = list(gens)
    for i, g in enumerate(list(active)):
        for _ in range(i * stagger):
            try:
                next(g)
            except StopIteration:
                active.remove(g)
                break
    while active:
        for g in list(active):
            try:
                next(g)
            except StopIteration:
                active.remove(g)


def emit_rg(P, G, occ, uT, ucT, pad, L, ygT, ygcT, need_ctx):
    P.begin("rg")
    TC = 512
    NT = E // 96
    NS = 6
    par = P.sb("rgpar", [96, NT, 16], F32)
    cst = P.sb("rgcst", [96, NT, 4], F32)
    zero = P.sb("zero1", [96, 1], F32)
    P.op("dve", lambda e: e.memset(zero[:], 0.0), writes=[zero])
    P.dma("sp", par, par[:], G.wsrc, G.rgpar.t.ap()[occ])
    P.op("act", lambda e: e.activation(out=cst[:, :, 0:2], in_=par[:, :, 9:11], func=AF.Exp, scale=-1.0), reads=[par], writes=[cst])
    P.op("act", lambda e: e.activation(out=cst[:, :, 0:2], in_=cst[:, :, 0:2], func=AF.Ln, bias=1.0), reads=[cst], writes=[cst])
    P.op("dve", lambda e: e.tensor_scalar_mul(out=cst[:, :, 2:4], in0=cst[:, :, 0:2], scalar1=-16.0), reads=[cst], writes=[cst])
    P.op("dve", lambda e: e.tensor_scalar_mul(out=cst[:, :, 0:2], in0=cst[:, :, 0:2], scalar1=-8.0), reads=[cst], writes=[cst])

    def rev(ap, d):
        return ap if d == 0 else ap[:, ::-1]

    def stream(s):
        n_ = "s%d" % s
        wg32 = P.sb("wg32" + n_, [96, 4, 96], F32)
        w = P.sb("wg" + n_, [96, 4, 96], BF16)
        ut = [P.sb("ut%d" % i + n_, [96, TC + 3], F32) for i in range(2)]
        gt = P.sb("gt" + n_, [96, TC], F32)
        xct = P.sb("xcs" + n_, [96, TC], F32)
        xbt = P.sb("xb" + n_, [96, TC], BF16)
        r = P.sb("rr" + n_, [96, TC], F32)
        i_ = P.sb("ii" + n_, [96, TC], F32)
        a = P.sb("aa" + n_, [96, TC], F32)
        m = P.sb("mm" + n_, [96, TC], F32)
        b = P.sb("bx" + n_, [96, TC], F32)
        hh = [P.sb("hh%d" % i + n_, [96, TC], F32) for i in range(2)]
        hfb = P.sb("hfb" + n_, [96, TC], BF16)
        hfl = P.sb("hfl" + n_, [96, TC], BF16)
        hfc = P.sb("hfc" + n_, [96, CTX], F32)
        y = P.sb("yy" + n_, [96, TC], BF16)
        p = P.ps("pr" + n_, [96, TC], F32)
        cnt = {"c": 0}

        def chunk(tj, src, col0, n, d, hinit):
            ci = cnt["c"]
            cnt["c"] += 1
            u = ut[ci % 2]
            h = hh[ci % 2]
            st_, srow = src.at(tj * 96, (tj + 1) * 96)
            P.dma("sp", u, u[:, :n + 3], st_, srow[:, col0:col0 + n + 3])
            yield
            P.op("act", lambda e: e.activation(out=xct[:, :n], in_=u[:, 0:n], func=AF.Identity,
                                               scale=par[:, tj, 0:1], bias=par[:, tj, 4:5]), reads=[u, par], writes=[xct])
            yield
            for k in range(1, 4):
                P.op("dve", lambda e, k=k: e.scalar_tensor_tensor(out=xct[:, :n], in0=u[:, k:k + n], scalar=par[:, tj, k:k + 1],
                                                                 in1=xct[:, :n], op0=ALU.mult, op1=ALU.add),
                     reads=[u, par, xct], writes=[xct])
            yield
            P.op("act", lambda e: e.copy(out=xbt[:, :n], in_=xct[:, :n]), reads=[xct], writes=[xbt])
            yield
            for g in range(2):
                P.op("pe", lambda e, g=g: e.matmul(p[:, :n], lhsT=w[:, 2 * d + g, :], rhs=xbt[:, :n], start=True, stop=True),
                     reads=[w, xbt], writes=[p])
                yield
                dst = r if g == 0 else i_
                bcol = par[:, tj, 5 + 2 * d + g: 6 + 2 * d + g]
                P.op("act", lambda e, dst=dst, bcol=bcol: e.activation(out=dst[:, :n], in_=p[:, :n], func=AF.Sigmoid, bias=bcol),
                     reads=[p, par], writes=[dst])
                yield
            P.op("act", lambda e: e.activation(out=a[:, :n], in_=rev(r[:, :n], d), func=AF.Exp, scale=cst[:, tj, d:d + 1]),
                 reads=[r, cst], writes=[a])
            P.op("pool", lambda e: e.tensor_tensor(out=i_[:, :n], in0=i_[:, :n], in1=xct[:, :n], op=ALU.mult), reads=[i_, xct], writes=[i_])
            yield
            P.op("act", lambda e: e.activation(out=m[:, :n], in_=rev(r[:, :n], d), func=AF.Exp, scale=cst[:, tj, 2 + d:3 + d]),
                 reads=[r, cst], writes=[m])
            yield
            P.op("act", lambda e: e.activation(out=m[:, :n], in_=m[:, :n], func=AF.Sqrt, scale=-1.0, bias=1.0), reads=[m], writes=[m])
            yield
            P.op("pool", lambda e: e.tensor_tensor(out=b[:, :n], in0=rev(i_[:, :n], d), in1=m[:, :n], op=ALU.mult), reads=[i_, m], writes=[b])
            yield
            init_t, init_ap = hinit
            P.op("dve", lambda e: e.tensor_tensor_scan(out=h[:, :n], data0=a[:, :n], data1=b[:, :n], initial=init_ap,
                                                       op0=ALU.mult, op1=ALU.add), reads=[a, b, init_t], writes=[h])
            yield
            return h, (h, h[:, n - 1:n])

        def gate_out(tj, h, n, hfwd_ap, hfwd_t, gsrc, gcol0, dst, dcol0):
            gs_, grow = gsrc.at(E + tj * 96, E + (tj + 1) * 96)
            P.dma("sp", gt, gt[:, :n], gs_, grow[:, gcol0:gcol0 + n])
            yield
            P.op("act", lambda e: e.activation(out=gt[:, :n], in_=gt[:, :n], func=AF.Silu), reads=[gt], writes=[gt])
            P.op("dve", lambda e: e.tensor_tensor(out=b[:, :n], in0=h[:, :n][:, ::-1], in1=hfwd_ap, op=ALU.add),
                 reads=[h, hfwd_t], writes=[b])
            yield
            P.op("pool", lambda e: e.tensor_tensor(out=y[:, :n], in0=b[:, :n], in1=gt[:, :n], op=ALU.mult), reads=[b, gt], writes=[y])
            yield
            P.dma("sp", dst, dst.t.ap()[tj * 96:(tj + 1) * 96, dcol0:dcol0 + n], y, y[:, :n])
            yield

        for tj in range(s, NT, NS):
            hfd = G.hfd[tj]
            P.dma("sp", wg32, wg32[:], G.wsrc, G.rgw.t.ap()[occ, tj])
            P.op("pool", lambda e: e.tensor_copy(out=w[:], in_=wg32[:]), reads=[wg32], writes=[w])
            yield
            z0 = (zero, zero[:, 0:1])
            h, carry = yield from chunk(tj, ucT, 0, CTX, 0, z0)
            if need_ctx:
                P.op("pool", lambda e, h=h: e.tensor_copy(out=hfc[:], in_=h[:, :CTX]), reads=[h], writes=[hfc])
            for c in range(L // TC):
                h, carry = yield from chunk(tj, uT, c * TC, TC, 0, carry)
                P.op("act", lambda e, h=h: e.copy(out=hfb[:], in_=h[:]), reads=[h], writes=[hfb])
                yield
                P.dma("sp", hfd, hfd.t.ap()[:, c * TC:(c + 1) * TC], hfb, hfb[:])
                yield
            h, carry = yield from chunk(tj, ucT, 0, CTX, 1, z0)
            if need_ctx:
                yield from gate_out(tj, h, CTX, hfc[:], hfc, ucT, pad, ygcT, 0)
            for c in range(L // TC - 1, -1, -1):
                P.dma("sp", hfl, hfl[:], hfd, hfd.t.ap()[:, c * TC:(c + 1) * TC])
                h, carry = yield from chunk(tj, uT, c * TC, TC, 1, carry)
                yield from gate_out(tj, h, TC, hfl[:], hfl, uT, pad + c * TC, ygT, c * TC)

    run_streams([stream(s) for s in range(NS)], stagger=0)
    P.end()


PI = float(np.pi)
SKIP_CTX_HY = False
PROFILE_SCOPES = False
HYDBG_NB = 2
CBL = 4
CBC = 32
FH = 64
NEMB = 17


def emit_hyfilt(P, G, occ, Lf, L, zt_d, ztr_d, trow_d, tapsd, normd):
    P.begin("hyfilt")
    NCT = E // 128
    w1 = P.sb("fw1", [NEMB, FH], F32)
    w2 = P.sb("fw2", [FH, FH], F32)
    fp = P.sb("fpar", [FH, 4], F32)
    fb = P.sb("fb", [FH, 2], F32)
    w3s = [P.sb("w3s%d" % i, [FH, 1536], F32) for i in range(2)]
    w3 = P.sb("w3bf", [FH, 4 * E], BF16)
    hid = [P.sb("hid%d" % i, [FH, Lf], BF16) for i in range(2)]
    zc = [P.sb("zc%d" % i, [NEMB, 512], F32) for i in range(2)]
    vv = [P.sb("vv%d" % i, [FH, 512], F32) for i in range(2)]
    mk = [P.sb("mk%d" % i, [FH, 512], F32) for i in range(2)]
    h1 = [P.sb("h1%d" % i, [FH, 512], F32) for i in range(2)]
    nd = P.sb("negd", [128, NCT], F32)
    tw = [P.sb("tw%d" % i, [128, 512], F32) for i in range(2)]
    win = [P.sb("win%d" % i, [128, 512], F32) for i in range(4)]
    tp = [P.sb("tp%d" % i, [128, 512], BF16) for i in range(8)]
    junk = P.sb("junk", [128, 512], BF16)
    ac = [P.sb("ac%d" % i, [128, 1], F32) for i in range(8)]
    nacc = P.sb("nacc", [128, 2, NCT], F32)
    pa = [P.ps("pfa%d" % i, [FH, 512], F32) for i in range(2)]
    pb = [P.ps("pfb%d" % i, [128, 512], F32) for i in range(6)]
    P.dma("sp", w1, w1[:], G.wsrc, G.hy_f_w1.t.ap()[occ])
    P.dma("sp", w2, w2[:], G.wsrc, G.hy_f_w2.t.ap()[occ])
    P.dma("sp", fp, fp[:, 0:3], G.wsrc, G.hyfp.t.ap()[occ])
    P.dma("sp", nd, nd[:], G.wsrc, G.negd.t.ap())
    P.op("dve", lambda e: e.memset(nacc[:], 0.0), writes=[nacc])
    P.op("dve", lambda e: e.tensor_scalar_mul(out=fb[:, 0:1], in0=fp[:, 1:2], scalar1=fp[:, 0:1]), reads=[fp], writes=[fb])
    P.op("dve", lambda e: e.tensor_scalar_mul(out=fb[:, 1:2], in0=fp[:, 2:3], scalar1=fp[:, 0:1]), reads=[fp], writes=[fb])
    for i in range(4):
        s = w3s[i % 2]
        P.dma("act", s, s[:], G.wsrc, G.hy_f_w3.t.ap()[occ][:, i * 1536:(i + 1) * 1536])
        P.op("pool", lambda e, s=s, i=i: e.tensor_copy(out=w3[:, i * 1536:(i + 1) * 1536], in_=s[:]), reads=[s], writes=[w3])

    def sin_layer(ps, n, bcol, out_ap, out_t, v, m):
        P.op("act", lambda e: e.activation(out=v[:, :n], in_=ps[:, :n], func=AF.Identity, scale=fp[:, 0:1], bias=bcol),
             reads=[ps, fp, fb], writes=[v])
        for it in range(2):
            for thr, add, cmp in ((PI, -2 * PI, ALU.is_gt), (-PI, 2 * PI, ALU.is_lt)):
                P.op("dve", lambda e, thr=thr, add=add, cmp=cmp: e.tensor_scalar(out=m[:, :n], in0=v[:, :n], scalar1=thr, scalar2=add,
                                                                               op0=cmp, op1=ALU.mult), reads=[v], writes=[m])
                P.op("dve", lambda e: e.tensor_tensor(out=v[:, :n], in0=v[:, :n], in1=m[:, :n], op=ALU.add), reads=[v, m], writes=[v])
        P.op("act", lambda e: e.activation(out=out_ap, in_=v[:, :n], func=AF.Sin), reads=[v], writes=[out_t])

    it = 0
    for dr, ztab in ((0, zt_d), (1, ztr_d)):
        for c0 in range(0, Lf, 512):
            n = min(512, Lf - c0)
            z, v, m, hh1 = zc[it % 2], vv[it % 2], mk[it % 2], h1[it % 2]
            p1 = pa[it % 2]
            it += 1
            P.dma("sp", z, z[:, :n], G.wsrc, ztab.t.ap()[:, c0:c0 + n])
            P.op("pe", lambda e, p1=p1, z=z, n=n: e.matmul(p1[:, :n], lhsT=w1[:], rhs=z[:, :n], start=True, stop=True),
                 reads=[w1, z], writes=[p1])
            sin_layer(p1, n, fb[:, 0:1], hh1[:, :n], hh1, v, m)
            P.op("pe", lambda e, p1=p1, hh1=hh1, n=n: e.matmul(p1[:, :n], lhsT=w2[:], rhs=hh1[:, :n], start=True, stop=True),
                 reads=[w2, hh1], writes=[p1])
            sin_layer(p1, n, fb[:, 1:2], hid[dr][:, c0:c0 + n], hid[dr], v, m)
    its = []
    itt = 0
    for dr in range(2):
        for c0 in range(0, Lf, 512):
            n = min(512, Lf - c0)
            for ct in range(NCT):
                its.append((dr, c0, n, ct, ct == 0))
    twcur = {}

    def S1(i):
        dr, c0, n, ct, first = its[i]
        if first:
            t_ = tw[(i // NCT) % 2]
            twcur[(dr, c0)] = t_
            P.dma("sp", t_, t_[:, :n], G.wsrc, trow_d.t.ap()[dr, c0:c0 + n].partition_broadcast(128))
        t_ = twcur[(dr, c0)]
        w_ = win[i % 4]
        P.op("act", lambda e: e.activation(out=w_[:, :n], in_=t_[:, :n], func=AF.Exp, scale=nd[:, ct:ct + 1]), reads=[t_, nd], writes=[w_])
        for o in range(2):
            p = pb[(i * 2 + o) % 6]
            col = o * 2 * E + dr * E + ct * 128
            P.op("pe", lambda e, p=p, col=col: e.matmul(p[:, :n], lhsT=w3[:, col:col + 128], rhs=hid[dr][:, c0:c0 + n], start=True, stop=True),
                 reads=[w3, hid[dr]], writes=[p])

    def S2(i):
        dr, c0, n, ct, first = its[i]
        w_ = win[i % 4]
        for o in range(2):
            kk = i * 2 + o
            p, tq, a_ = pb[kk % 6], tp[kk % 8], ac[kk % 8]
            P.op("dve", lambda e, p=p, tq=tq: e.scalar_tensor_tensor(out=tq[:, :n], in0=w_[:, :n], scalar=0.05, in1=p[:, :n],
                                                                    op0=ALU.add, op1=ALU.mult), reads=[w_, p], writes=[tq])
            if dr == 1 and c0 == 0:
                P.op("dve", lambda e, tq=tq: e.memset(tq[:, 0:1], 0.0), reads=[tq], writes=[tq])
            P.op("act", lambda e, tq=tq, a_=a_: e.activation(out=junk[:, :n], in_=tq[:, :n], func=AF.Abs, accum_out=a_[:, 0:1]),
                 reads=[tq], writes=[junk, a_])
            pos = c0 if dr == 0 else 2 * L - Lf + c0
            P.dma(("sp", "act", "pool")[kk % 3], tapsd, tapsd.t.ap()[o, ct * 128:(ct + 1) * 128, pos:pos + n], tq, tq[:, :n])

    def S3(i):
        dr, c0, n, ct, first = its[i]
        for o in range(2):
            a_ = ac[(i * 2 + o) % 8]
            P.op("dve", lambda e, a_=a_, o=o: e.tensor_tensor(out=nacc[:, o, ct:ct + 1], in0=nacc[:, o, ct:ct + 1], in1=a_[:, 0:1], op=ALU.add),
                 reads=[a_, nacc], writes=[nacc])

    NI = len(its)
    for i in range(-2, NI + 1):
        if 0 <= i + 2 < NI:
            S1(i + 2)
        if 0 <= i < NI:
            S2(i)
        if 0 <= i - 1 < NI:
            S3(i - 1)
    P.dma("sp", normd, normd.t.ap().rearrange("o (c p) -> p o c", p=128), nacc, nacc[:], allow_slow_non_contiguous=True)
    if getattr(G, "dbg_hid", None) is not None:
        for dr in range(2):
            P.dma("sp", G.dbg_hid, G.dbg_hid.t.ap()[dr], hid[dr], hid[dr][:])
        P.dma("sp", G.dbg_nacc, G.dbg_nacc.t.ap(), nacc, nacc[:])
    P.end()


def emit_hyconv(P, G, occ, uT, L, Pn, CB, tb_d, tf_d, hypar_d, tapsd, normd, ygT, nb_limit=None):
    assert L == Pn * Pn
    P.begin("hyconv")
    NS = 2
    GS = 512 // Pn
    NB = E // CB
    W = L + 3
    tb = P.sb("tb16", [Pn, 17 * Pn], BF16)
    tf = P.sb("tf32", [Pn, 4 * Pn], F32)
    FA0, FA1 = tb[:, 0:2 * Pn], tb[:, 2 * Pn:4 * Pn]
    FCre, FCim, FCimN = tb[:, 4 * Pn:5 * Pn], tb[:, 5 * Pn:6 * Pn], tb[:, 6 * Pn:7 * Pn]
    FCc, FCc2 = tb[:, 7 * Pn:9 * Pn], tb[:, 9 * Pn:11 * Pn]
    CA, SA = tb[:, 11 * Pn:12 * Pn], tb[:, 12 * Pn:13 * Pn]
    FCreN, FCcN, CAN = tb[:, 13 * Pn:14 * Pn], tb[:, 14 * Pn:16 * Pn], tb[:, 16 * Pn:17 * Pn]

    def bc(ap2d):
        return ap2d.unsqueeze(1).to_broadcast([Pn, CB, Pn])

    TW = (bc(tf[:, 0:Pn]), bc(tf[:, Pn:2 * Pn]))
    TWT = (bc(tf[:, 2 * Pn:3 * Pn]), bc(tf[:, 3 * Pn:4 * Pn]))
    P.dma("sp", tb, tb[:], G.wsrc, tb_d.t.ap())
    P.dma("sp", tf, tf[:], G.wsrc, tf_d.t.ap())
    hv = hypar_d.t.ap()

    def bcw(ap2d, n):
        return ap2d.unsqueeze(2).to_broadcast([Pn, n, Pn])

    def stream(s):
        n_ = "h%d" % s
        xs = P.sb("xs" + n_, [Pn, 3 * CB, Pn + 2], F32)
        vx = P.sb("vx" + n_, [Pn, 3 * CB, Pn], F32)
        tca = P.sb("tca" + n_, [Pn, 3 * CB, Pn], F32)
        tcb = P.sb("tcb" + n_, [Pn, 3 * CB, Pn], F32)
        vb = P.sb("vb" + n_, [Pn, CB, Pn], BF16)
        gt = P.sb("gth" + n_, [Pn, CB, Pn], F32)
        kt = [P.sb("kt%d" % i + n_, [Pn, CB, 2, Pn], BF16) for i in range(2)]
        KF = [P.sb("KF%d" % i + n_, [Pn, 2, CB, Pn], F32) for i in range(2)]
        hp = P.sb("hp" + n_, [Pn, 14 * CB], F32)
        rn = P.sb("rn" + n_, [Pn, 2, CB], F32)
        Asb = [P.sb("Asb%d" % i + n_, [Pn, 2 * CB * Pn], F32) for i in range(2)]
        tq = [[P.sb("tq%d_%d" % (i, j) + n_, [Pn, CB, Pn], BF16) for j in range(4)] for i in range(2)]
        zz = P.sb("zz" + n_, [Pn, CB, Pn], F32)
        zzb = P.sb("zzb" + n_, [Pn, CB, Pn], BF16)
        tcm = P.sb("tcm" + n_, [Pn, CB, Pn], F32)
        ygo = P.sb("ygo" + n_, [Pn, CB, Pn], BF16)
        pA = P.ps("pA" + n_, [Pn, 2 * CB * Pn], F32)
        pC = P.ps("pC" + n_, [Pn, 2 * CB * Pn], F32)
        cnt = {"x": 0}
        yv = pC[:, 0:CB * Pn].rearrange("p (c k) -> p c k", c=CB)

        def cmul(src, tabs, tab_re, tab_im, srcv):
            ts = tq[cnt["x"] % 2]
            cnt["x"] += 1
            sre, sim = srcv(0), srcv(1)
            for j, (s_, t_) in enumerate(((sre, tab_re), (sim, tab_im), (sre, tab_im), (sim, tab_re))):
                P.op("dve", lambda e, j=j, s_=s_, t_=t_: e.tensor_tensor(out=ts[j][:], in0=s_, in1=t_, op=ALU.mult), reads=[src] + tabs, writes=[ts[j]])
                yield
            return ts

        def fwd(lhs_list):
            k = cnt["x"] % 2
            for cb in range(CB):
                ents = lhs_list[cb]
                for j, (lap, tab, tt) in enumerate(ents):
                    P.op("pe", lambda e, cb=cb, lap=lap, tab=tab, j=j, nj=len(ents): e.matmul(
                        pA[:, cb * 2 * Pn:(cb + 1) * 2 * Pn], lhsT=lap, rhs=tab, start=(j == 0), stop=(j == nj - 1)),
                        reads=[tt, tb], writes=[pA])
            yield
            A_ = Asb[k]
            P.op("act", lambda e: e.copy(out=A_[:], in_=pA[:]), reads=[pA], writes=[A_])
            yield
            Av = A_[:].rearrange("p (c r k) -> p c r k", c=CB, r=2)
            ts = yield from cmul(A_, [tf], TW[0], TW[1], lambda r_: Av[:, :, r_, :])
            for g in range(CB // GS):
                tg = [t[:, g * GS:(g + 1) * GS, :].rearrange("p c k -> p (c k)") for t in ts]
                ore = pC[:, g * 512:(g + 1) * 512]
                oim = pC[:, CB * Pn + g * 512:CB * Pn + (g + 1) * 512]
                for dst, terms in ((ore, ((FCre, 0), (FCreN, 1), (FCimN, 2), (FCimN, 3))), (oim, ((FCre, 2), (FCre, 3), (FCim, 0), (FCimN, 1)))):
                    for j, (tab, ti) in enumerate(terms):
                        P.op("pe", lambda e, dst=dst, tab=tab, rhs=tg[ti], j=j: e.matmul(dst, lhsT=tab, rhs=rhs, start=(j == 0), stop=(j == 3)),
                             reads=[tb, ts[ti]], writes=[pC])
            yield
            Y_ = Asb[(k + 1) % 2]
            P.op("act", lambda e: e.copy(out=Y_[:], in_=pC[:]), reads=[pC], writes=[Y_])
            yield
            return Y_

        def inv(Y_, o):
            Yv = Y_[:].rearrange("p (r c k) -> p r c k", r=2, c=CB)
            ts = yield from cmul(Y_, [KF[o]], KF[o][:, 0, :, :], KF[o][:, 1, :, :], lambda r_: Yv[:, r_, :, :])
            for cb in range(CB):
                for j, (tab, ti) in enumerate(((FCc, 0), (FCcN, 1), (FCc2, 2), (FCc2, 3))):
                    P.op("pe", lambda e, cb=cb, tab=tab, ti=ti, j=j: e.matmul(pA[:, cb * 2 * Pn:(cb + 1) * 2 * Pn], lhsT=ts[ti][:, cb, :], rhs=tab,
                                                                            start=(j == 0), stop=(j == 3)), reads=[ts[ti], tb], writes=[pA])
            yield
            A_ = Asb[cnt["x"] % 2]
            P.op("act", lambda e: e.copy(out=A_[:], in_=pA[:]), reads=[pA], writes=[A_])
            yield
            Av = A_[:].rearrange("p (c r k) -> p c r k", c=CB, r=2)
            ts2 = yield from cmul(A_, [tf], TWT[0], TWT[1], lambda r_: Av[:, :, r_, :])
            for g in range(CB // GS):
                tg = [t[:, g * GS:(g + 1) * GS, :].rearrange("p c k -> p (c k)") for t in ts2]
                oy = pC[:, g * 512:(g + 1) * 512]
                for j, (tab, ti) in enumerate(((CA, 0), (CAN, 1), (SA, 2), (SA, 3))):
                    P.op("pe", lambda e, oy=oy, tab=tab, rhs=tg[ti], j=j: e.matmul(oy, lhsT=tab, rhs=rhs, start=(j == 0), stop=(j == 3)),
                         reads=[tb, ts2[ti]], writes=[pC])
            yield

        nbs = NB if nb_limit is None else nb_limit
        for bi_ in range(s, nbs, NS):
            c0 = bi_ * CB
            P.dma("sp", hp, hp[:], G.wsrc, hv[occ, bi_, :].partition_broadcast(Pn))
            P.dma("sp", rn, rn[:], normd, normd.t.ap()[:, c0:c0 + CB].partition_broadcast(Pn))
            for q in range(3):
                ut_, urow = uT.at(q * E + c0, q * E + c0 + CB)
                halo = bass.AP(urow.tensor, c0 * W, [[Pn, Pn], [W, CB], [1, Pn + 2]])
                P.dma("sp", xs, xs[:, q * CB:(q + 1) * CB, :], ut_, halo)
            ut_, urow = uT.at(3 * E + c0, 3 * E + c0 + CB)
            P.dma("sp", gt, gt[:], ut_, urow[:, 1:1 + L].rearrange("c (a b) -> a c b", b=Pn))
            for o in range(2):
                P.dma("sp", kt[o], kt[o][:], tapsd, tapsd.t.ap()[o, c0:c0 + CB, :].rearrange("c (h a b) -> a c h b", h=2, b=Pn))
            yield
            P.op("dve", lambda e: e.reciprocal(out=rn[:], in_=rn[:]), reads=[rn], writes=[rn])
            P.op("dve", lambda e: e.tensor_tensor(out=vx[:], in0=xs[:, :, 0:Pn], in1=bcw(hp[:, 0:3 * CB], 3 * CB), op=ALU.mult), reads=[xs, hp], writes=[vx])
            P.op("dve", lambda e: e.tensor_tensor(out=tca[:], in0=xs[:, :, 1:Pn + 1], in1=bcw(hp[:, 3 * CB:6 * CB], 3 * CB), op=ALU.mult), reads=[xs, hp], writes=[tca])
            yield
            P.op("dve", lambda e: e.tensor_tensor(out=tcb[:], in0=xs[:, :, 2:Pn + 2], in1=bcw(hp[:, 6 * CB:9 * CB], 3 * CB), op=ALU.mult), reads=[xs, hp], writes=[tcb])
            P.op("dve", lambda e: e.tensor_tensor(out=vx[:], in0=vx[:], in1=bcw(hp[:, 9 * CB:12 * CB], 3 * CB), op=ALU.add), reads=[vx, hp], writes=[vx])
            yield
            P.op("dve", lambda e: e.tensor_tensor(out=vx[:], in0=vx[:], in1=tca[:], op=ALU.add), reads=[vx, tca], writes=[vx])
            yield
            P.op("dve", lambda e: e.tensor_tensor(out=vx[:], in0=vx[:], in1=tcb[:], op=ALU.add), reads=[vx, tcb], writes=[vx])
            yield
            P.op("act", lambda e: e.copy(out=vb[:], in_=vx[:, 0:CB, :]), reads=[vx], writes=[vb])
            P.op("act", lambda e: e.activation(out=gt[:], in_=gt[:], func=AF.Silu), reads=[gt], writes=[gt])
            yield
            for o in range(2):
                Y_ = yield from fwd([[(kt[o][:, cb, 0, :], FA0, kt[o]), (kt[o][:, cb, 1, :], FA1, kt[o])] for cb in range(CB)])
                P.op("dve", lambda e, Y_=Y_, o=o: e.tensor_tensor(
                    out=KF[o][:], in0=Y_[:].rearrange("p (r c k) -> p r c k", r=2, c=CB),
                    in1=rn[:, o, :].unsqueeze(1).unsqueeze(3).to_broadcast([Pn, 2, CB, Pn]), op=ALU.mult), reads=[Y_, rn], writes=[KF[o]])
                yield
            Y_ = yield from fwd([[(vb[:, cb, :], FA0, vb)] for cb in range(CB)])
            yield from inv(Y_, 0)
            P.op("dve", lambda e: e.tensor_tensor(out=tcm[:], in0=vx[:, 0:CB, :], in1=bcw(hp[:, 12 * CB:13 * CB], CB), op=ALU.mult), reads=[vx, hp], writes=[tcm])
            yield
            P.op("dve", lambda e: e.tensor_tensor(out=tcm[:], in0=tcm[:], in1=yv, op=ALU.add), reads=[tcm, pC], writes=[tcm])
            yield
            P.op("dve", lambda e: e.tensor_tensor(out=zz[:], in0=tcm[:], in1=vx[:, CB:2 * CB, :], op=ALU.mult), reads=[tcm, vx], writes=[zz])
            yield
            P.op("act", lambda e: e.copy(out=zzb[:], in_=zz[:]), reads=[zz], writes=[zzb])
            yield
            Y_ = yield from fwd([[(zzb[:, cb, :], FA0, zzb)] for cb in range(CB)])
            yield from inv(Y_, 1)
            P.op("dve", lambda e: e.tensor_tensor(out=tcm[:], in0=zz[:], in1=bcw(hp[:, 13 * CB:14 * CB], CB), op=ALU.mult), reads=[zz, hp], writes=[tcm])
            P.op("dve", lambda e: e.tensor_tensor(out=zz[:], in0=vx[:, 2 * CB:3 * CB, :], in1=gt[:], op=ALU.mult), reads=[vx, gt], writes=[zz])
            yield
            P.op("dve", lambda e: e.tensor_tensor(out=tcm[:], in0=tcm[:], in1=yv, op=ALU.add), reads=[tcm, pC], writes=[tcm])
            yield
            P.op("dve", lambda e: e.tensor_tensor(out=ygo[:], in0=tcm[:], in1=zz[:], op=ALU.mult), reads=[tcm, zz], writes=[ygo])
            yield
            P.dma("sp", ygT, ygT.t.ap()[c0:c0 + CB, :].rearrange("c (a b) -> a c b", b=Pn), ygo, ygo[:])
            yield

    run_streams([stream(s) for s in range(NS)], stagger=40)
    P.end()


class G_:
    pass


def emit_setup(P, G, L):
    P.begin("setup")
    zt = P.sb("zpad", [128, 4], F32)
    P.op("dve", lambda e: e.memset(G.ones[:], 1.0), writes=[G.ones])
    P.op("dve", lambda e: e.memset(G.eps[:], LN_EPS), writes=[G.eps])
    P.op("pool", lambda e: e.memset(zt[:], 0.0), writes=[zt])
    P.dma("sp", G.ident, G.ident[:], G.wsrc, G.identd.t.ap())
    P.dma("sp", G.csil, G.csil[:], G.wsrc, G.cc.t.ap())
    P.op("act", lambda e: e.activation(out=G.csil[:], in_=G.csil[:], func=AF.Silu), reads=[G.csil], writes=[G.csil])
    for t_, n in [(t, L) for t in G.uT.ts] + [(t, CTX) for t in G.ucT.ts]:
        rows = t_.t.ap().shape[0]
        for r0 in range(0, rows, 128):
            P.dma("sp", t_, t_.t.ap()[r0:r0 + 128, 0:1], zt, zt[:, 0:1], allow_slow_non_contiguous=True)
            P.dma("act", t_, t_.t.ap()[r0:r0 + 128, 1 + n:3 + n], zt, zt[:, 0:2], allow_slow_non_contiguous=True)
    P.end()


def build_nc(L, nlayers=DEPTH, debug=False):
    nc = bass.Bass("TRN2", target_bir_lowering=False)
    P = Prog(nc)
    G = G_()

    def inp(name, shape, dt=F32):
        return P.dram(name, shape, dt, kind="ExternalInput")

    G.x_in = inp("x", [L, D])
    G.ctx_in = inp("ctx", [CTX, D])
    G.cc = inp("cc", [128, 2, 8])
    G.identd = inp("ident", [128, 128], BF16)
    G.w_mod = inp("w_mod", [DEPTH, D, 3 * D])
    G.b_mod = inp("b_mod", [DEPTH, 3 * D])
    G.ln_g = inp("ln_g", [DEPTH, D])
    G.ln_b = inp("ln_b", [DEPTH, D])
    G.rg_w_in = inp("rg_w_in", [2, D, 2 * E])
    G.rgpar = inp("rgpar", [2, 96, E // 96, 16])
    G.rgw = inp("rgw", [2, E // 96, 96, 4, 96])
    G.rg_w_out = inp("rg_w_out", [2, E, D])
    G.hy_w_in = inp("hy_w_in", [2, D, 4 * E])
    G.hy_w_out = inp("hy_w_out", [2, E, D])
    G.hy_f_w1 = inp("hy_f_w1", [2, NEMB, FH])
    G.hy_f_w2 = inp("hy_f_w2", [2, FH, FH])
    G.hy_f_w3 = inp("hy_f_w3", [2, FH, 4 * E])
    G.hyfp = inp("hyfp", [2, FH, 3])
    G.hypar = inp("hypar", [2, E // CBL, 14 * CBL])
    G.hyparc = inp("hyparc", [2, E // CBC, 14 * CBC])
    G.tb16c = inp("tb16c", [16, 17 * 16], BF16)
    G.tf32c = inp("tf32c", [16, 4 * 16])
    G.ztc = inp("ztc", [NEMB, CTX])
    G.ztrc = inp("ztrc", [NEMB, CTX])
    G.trowc = inp("trowc", [2, CTX])
    G.tapsc = P.dram("tapsc", [2, E, 2 * CTX], BF16)
    G.normc = P.dram("normc", [2, E], F32)
    G.negd = inp("negd", [128, E // 128])
    G.tb16 = inp("tb16", [128, 17 * 128], BF16)
    G.tf32 = inp("tf32", [128, 512])
    G.zt = inp("zt", [NEMB, L])
    G.ztr = inp("ztr", [NEMB, L])
    G.trow = inp("trow", [2, L])
    G.tapsd = P.dram("tapsd", [2, E, 2 * L], BF16, kind=("ExternalOutput" if debug == 2 else "Internal"))
    G.normd = P.dram("normd", [2, E], F32, kind=("ExternalOutput" if debug == 2 else "Internal"))
    G.wsrc = G.w_mod
    out = P.dram("out", [L, D], F32, kind="ExternalOutput")
    kind = "ExternalOutput" if debug else "Internal"
    G.xA = P.dram("xA", [L, D], F32, kind=kind)
    G.xB = P.dram("xB", [L, D], F32)
    G.ctxA = P.dram("ctxA", [CTX, D], F32, kind=kind)
    G.ctxB = P.dram("ctxB", [CTX, D], F32, kind=kind)
    G.uT = Q([P.dram("uT%d" % q, [E, L + 3], F32, kind=kind) for q in range(4)], E)
    G.ucT = Q([P.dram("ucT%d" % q, [E, CTX + 3], F32) for q in range(4)], E)
    G.ygT = P.dram("ygT", [E, L], BF16, kind=kind)
    G.ygcT = P.dram("ygcT", [E, CTX], BF16)
    G.hfd = [P.dram("hfd%d" % i, [96, L], BF16) for i in range(E // 96)]
    G.ones = P.sb("ones", [128, 128], F32, persist=True)
    G.ident = P.sb("identsb", [128, 128], BF16, persist=True)
    G.csil = P.sb("csil", [128, 2, 8], F32, persist=True)
    G.eps = P.sb("epsc", [128, 1], F32, persist=True)
    G.modt = [P.sb("modt%d" % j, [128, 3 * D], F32, persist=True) for j in range(2)]

    emit_setup(P, G, L)
    if debug == 2:
        G.dbg_hid = P.dram("dbg_hid", [2, FH, L], BF16, kind="ExternalOutput")
        G.dbg_nacc = P.dram("dbg_nacc", [128, 2, E // 128], F32, kind="ExternalOutput")
        emit_mod(P, G, 1)
        emit_proj(P, G, G.x_in, L, 0, G.hy_w_in.t.ap()[0], 4 * E, 128, G.uT, 1, False)
        emit_hyfilt(P, G, 0, L, L, G.zt, G.ztr, G.trow, G.tapsd, G.normd)
        emit_hyconv(P, G, 0, G.uT, L, 128, CBL, G.tb16, G.tf32, G.hypar, G.tapsd, G.normd, G.ygT, nb_limit=HYDBG_NB)
        P.begin()
        P.op("sp", lambda e: e.dma_start(out=out.t.ap()[0:1, 0:2], in_=G.ygcT.t.ap()[1:2, 0:1].bitcast(F32) if False else G.xB.t.ap()[1:2, 0:2]), reads=[G.ygT], writes=[out], dma=out)
        P.end()
        P.root.close()
        return nc
    xs, cs = G.x_in, G.ctx_in
    xbufs, cbufs = [G.xA, G.xB], [G.ctxA, G.ctxB]
    for l in range(nlayers):
        kindl, occ = l % 2, l // 2
        need_ctx = any(jj % 2 == 0 for jj in range(l + 1, DEPTH))
        colmajor = occ % 2 == 1
        xd = out if l == nlayers - 1 and not debug else xbufs[l % 2]
        if l == nlayers - 1 and debug:
            xd = out
        cd = cbufs[l % 2]
        emit_mod(P, G, l)
        if kindl == 0:
            emit_proj(P, G, xs, L, 0, G.rg_w_in.t.ap()[occ], 2 * E, 96, G.uT, 1, colmajor)
            emit_proj(P, G, cs, CTX, 1, G.rg_w_in.t.ap()[occ], 2 * E, 96, G.ucT, 1, False)
            emit_rg(P, G, occ, G.uT, G.ucT, 1, L, G.ygT, G.ygcT, need_ctx)
            emit_out(P, G, l, xs, xd, L, 0, G.ygT, G.rg_w_out.t.ap()[occ], 96, colmajor)
            if need_ctx:
                emit_out(P, G, l, cs, cd, CTX, 1, G.ygcT, G.rg_w_out.t.ap()[occ], 96, False)
        else:
            emit_proj(P, G, xs, L, 0, G.hy_w_in.t.ap()[occ], 4 * E, 128, G.uT, 1, colmajor)
            emit_hyfilt(P, G, occ, L, L, G.zt, G.ztr, G.trow, G.tapsd, G.normd)
            emit_hyconv(P, G, occ, G.uT, L, 128, CBL, G.tb16, G.tf32, G.hypar, G.tapsd, G.normd, G.ygT)
            emit_out(P, G, l, xs, xd, L, 0, G.ygT, G.hy_w_out.t.ap()[occ], 128, colmajor)
            if need_ctx and not SKIP_CTX_HY:
                emit_proj(P, G, cs, CTX, 1, G.hy_w_in.t.ap()[occ], 4 * E, 128, G.ucT, 1, False)
                emit_hyfilt(P, G, occ, CTX, CTX, G.ztc, G.ztrc, G.trowc, G.tapsc, G.normc)
                emit_hyconv(P, G, occ, G.ucT, CTX, 16, CBC, G.tb16c, G.tf32c, G.hyparc, G.tapsc, G.normc, G.ygcT)
                emit_out(P, G, l, cs, cd, CTX, 1, G.ygcT, G.hy_w_out.t.ap()[occ], 128, False)
        xs = xd
        if need_ctx:
            cs = cd
    P.begin()
    P.op("sp", lambda e: e.dma_start(out=G.ygcT.t.ap()[0:1, 0:2], in_=G.ygcT.t.ap()[1:2, 0:2]), reads=[out], writes=[G.ygcT], dma=G.ygcT)
    P.end()
    P.root.close()
    return nc


def emb_tables(Lf):
    t = np.linspace(0.0, 1.0, Lf, dtype=np.float32)
    bands = np.linspace(1e-4, 7, 8, dtype=np.float32)
    w = (np.float32(2.0 * np.pi) * np.arange(Lf, dtype=np.float32) / np.float32(Lf))
    z = np.concatenate([t[:, None], np.cos(bands[None, :] * w[:, None]), -np.sin(bands[None, :] * w[:, None])], axis=1).astype(np.float32)
    idx = np.concatenate([[0], Lf - np.arange(1, Lf)])
    zt = np.ascontiguousarray(z.T)
    ztr = np.ascontiguousarray(z[idx].T)
    trow = np.stack([t, t[idx]]).astype(np.float32)
    return zt, ztr, trow


def hy_tables(L, Pn=128, sfx=""):
    import ml_dtypes
    assert L == Pn * Pn
    N = 2 * L
    N1 = 2 * Pn
    n1 = np.arange(Pn, dtype=np.float64)[:, None]
    k1 = np.arange(Pn, dtype=np.float64)[None, :]
    ang = 2 * np.pi * n1 * (k1 + 0.5) / N1
    FA0 = np.concatenate([np.cos(ang), -np.sin(ang)], 1)
    ang1 = 2 * np.pi * (n1 + Pn) * (k1 + 0.5) / N1
    FA1 = -np.concatenate([np.cos(ang1), -np.sin(ang1)], 1)
    angt = 2 * np.pi * n1 * (k1 + 0.5) / N
    angc = 2 * np.pi * n1 * k1 / Pn
    FCre, FCim = np.cos(angc), -np.sin(angc)
    FCc = np.concatenate([np.cos(angc.T), np.sin(angc.T)], 1)
    FCc2 = np.concatenate([-np.sin(angc.T), np.cos(angc.T)], 1)
    CA = (2.0 / N) * np.cos(ang.T)
    SA = -(2.0 / N) * np.sin(ang.T)
    tb16 = np.concatenate([FA0, FA1, FCre, FCim, -FCim, FCc, FCc2, CA, SA, -FCre, -FCc, -CA], 1).astype(np.float32).astype(ml_dtypes.bfloat16)
    tf32 = np.concatenate([np.cos(angt), -np.sin(angt), np.cos(angt.T), np.sin(angt.T)], 1).astype(np.float32)
    zt, ztr, trow = emb_tables(L)
    max_decay = np.log(1e-2) / 0.3
    min_decay = np.log(1e-2) / 1.5
    deltas = np.abs(np.linspace(min_decay, max_decay, E, dtype=np.float32))
    negd = np.ascontiguousarray((-deltas).reshape(E // 128, 128).T).astype(np.float32)
    return {"tb16" + sfx: tb16, "tf32" + sfx: tf32, "zt" + sfx: zt, "ztr" + sfx: ztr, "trow" + sfx: trow, "negd": negd}


def host_inputs(inputs, b):
    import ml_dtypes
    f = lambda a: np.ascontiguousarray(a, dtype=np.float32)
    cc = np.zeros((128, 2, 8), np.float32)
    cc[:, 0, :] = inputs["c"][b].reshape(8, 128).T
    cc[:, 1, :] = inputs["c_ctx"].reshape(8, 128).T
    NT = E // 96
    rgpar = np.zeros((2, 96, NT, 16), np.float32)

    def chan(v):
        return v.reshape(2, NT, 96).transpose(0, 2, 1)

    for k in range(4):
        rgpar[:, :, :, k] = chan(inputs["rg_conv_w"][:, k, :])
    rgpar[:, :, :, 4] = chan(inputs["rg_conv_b"])
    for d in range(2):
        rgpar[:, :, :, 5 + 2 * d] = chan(inputs["rg_b_r"][:, d, :])
        rgpar[:, :, :, 6 + 2 * d] = chan(inputs["rg_b_i"][:, d, :])
        rgpar[:, :, :, 9 + d] = chan(inputs["rg_lambda"][:, d, :])
    rgw = np.zeros((2, NT, 96, 4, 96), np.float32)
    for d in range(2):
        rgw[:, :, :, 2 * d + 0, :] = inputs["rg_w_r"][:, d]
        rgw[:, :, :, 2 * d + 1, :] = inputs["rg_w_i"][:, d]
    L = inputs["x"].shape[1]
    hy = hy_tables(L)
    hy.update(hy_tables(CTX, 16, "c"))

    def mk_hypar(CB):
        NB = E // CB
        hypar = np.zeros((2, NB, 14 * CB), np.float32)
        cw = inputs["hy_conv_w"].reshape(2, 3, 3, NB, CB)
        for tap in range(3):
            hypar[:, :, tap * 3 * CB:(tap + 1) * 3 * CB] = cw[:, tap].transpose(0, 2, 1, 3).reshape(2, NB, 3 * CB)
        hypar[:, :, 9 * CB:12 * CB] = inputs["hy_conv_b"].reshape(2, 3, NB, CB).transpose(0, 2, 1, 3).reshape(2, NB, 3 * CB)
        hypar[:, :, 12 * CB:13 * CB] = inputs["hy_d"][:, 0].reshape(2, NB, CB)
        hypar[:, :, 13 * CB:14 * CB] = inputs["hy_d"][:, 1].reshape(2, NB, CB)
        return hypar

    hypar = mk_hypar(CBL)
    hyparc = mk_hypar(CBC)
    hyfp = np.stack([inputs["hy_f_freq"], inputs["hy_f_b1"], inputs["hy_f_b2"]], axis=-1).astype(np.float32)
    extra = {"hy_w_in": f(inputs["hy_w_in"]), "hy_w_out": f(inputs["hy_w_out"]), "hy_f_w1": f(inputs["hy_f_w1"]),
             "hy_f_w2": f(inputs["hy_f_w2"]), "hy_f_w3": f(inputs["hy_f_w3"]), "hyfp": hyfp, "hypar": hypar, "hyparc": hyparc}
    extra.update(hy)
    return {
        **extra,
        "x": f(inputs["x"][b]), "ctx": f(inputs["ctx"][b]), "cc": cc,
        "ident": np.eye(128, dtype=np.float32).astype(ml_dtypes.bfloat16),
        "w_mod": f(inputs["w_mod"]), "b_mod": f(inputs["b_mod"]), "ln_g": f(inputs["ln_g"]), "ln_b": f(inputs["ln_b"]),
        "rg_w_in": f(inputs["rg_w_in"]), "rgpar": rgpar, "rgw": rgw, "rg_w_out": f(inputs["rg_w_out"]),
    }


def kernel(**inputs):
    L = inputs["x"].shape[1]
    nc = build_nc(L)
    in_maps = [host_inputs(inputs, b) for b in range(2)]
    res = run_bass_kernel_spmd(nc, in_maps, core_ids=[0, 1])
    return np.stack([res.results[b]["out"] for b in range(2)], axis=0).astype(np.float32)
```

```python
import contextlib
import numpy as np
import concourse.bass as bass
import concourse.mybir as mybir
from concourse.bass_utils import run_bass_kernel_spmd

F32 = mybir.dt.float32
BF16 = mybir.dt.bfloat16
ALU = mybir.AluOpType
AF = mybir.ActivationFunctionType
AX = mybir.AxisListType

D = 1024
E = 1536
GRID_W = 64
CTX = 256
DEPTH = 4
ALPHA = (2 * DEPTH) ** 0.25
LN_EPS = 1e-5


class Buf:
    def __init__(self, name):
        self.name = name
        self.w = {}
        self.r = {}


class T:
    def __init__(self, t, name):
        self.t = t
        self.buf = Buf(name)

    def __getitem__(self, k):
        return self.t[k]


class Q:
    def __init__(self, ts, rows_per):
        self.ts = ts
        self.rp = rows_per

    def at(self, r0, r1):
        q = r0 // self.rp
        assert (r1 - 1) // self.rp == q
        t = self.ts[q]
        return t, t.t.ap()[r0 - q * self.rp:r1 - q * self.rp]


class Prog:
    ENG = ("sp", "act", "dve", "pool", "pe")

    def __init__(self, nc):
        self.nc = nc
        self.root = contextlib.ExitStack()
        self.phase = None
        self.sems = {}
        self.latest = {}
        self.ops = {e: [] for e in self.ENG}
        self.waited = {e: {} for e in self.ENG}
        self.nsem = 0

    def sem(self, key):
        if key not in self.sems:
            free = getattr(self, "free_phys", None)
            if free:
                h, base = free.pop()
                self.sems[key] = h
                self.latest[key] = base
            else:
                self.nsem += 1
                self.sems[key] = self.root.enter_context(self.nc.semaphore("s%d" % self.nsem))
            if self.phase is not None and isinstance(key, tuple) and key[1] in getattr(self, "phase_names", ()):
                self.phase_keys.append(key)
        return self.sems[key]

    def _stack(self, persist):
        return self.root if (persist or self.phase is None) else self.phase

    def _scope_id(self):
        self.nscope = getattr(self, "nscope", 0) + 1
        return self.nscope

    def _uname(self, name):
        self.nname = getattr(self, "nname", 0) + 1
        return "%s_%d" % (name, self.nname)

    def sb(self, name, shape, dt, persist=False):
        name = self._uname(name)
        if not persist and self.phase is not None:
            self.phase_names.add(name)
        t = self._stack(persist).enter_context(self.nc.sbuf_tensor(name, list(shape), dt))
        return T(t, name)

    def ps(self, name, shape, dt, persist=False):
        name = self._uname(name)
        t = self._stack(persist).enter_context(self.nc.psum_tensor(name, list(shape), dt))
        return T(t, name)

    def dram(self, name, shape, dt, kind="Internal"):
        t = self.nc.dram_tensor(name, list(shape), dt, kind=kind)
        return T(t, name)

    def op(self, eng, fn, reads=(), writes=(), dma=None):
        need = {}

        def add(evs):
            for k, v in evs.items():
                if need.get(k, 0) < v:
                    need[k] = v

        own = ("dma", dma.buf.name) if dma is not None else None
        for b in reads:
            add(b.buf.w)
        for b in writes:
            add({k: v for k, v in b.buf.w.items() if k != own})
            add(b.buf.r)
        waits = []
        for k, v in need.items():
            if k == eng and eng == "pe":
                continue
            if k not in self.sems:
                continue
            if self.waited[eng].get(k, 0) >= v:
                continue
            self.waited[eng][k] = v
            waits.append((k, v))
        if dma is not None:
            key = ("dma", dma.buf.name)
            inc = 16
        else:
            key = eng
            inc = 1
        self.sem(key)
        val = self.latest.get(key, 0) + inc
        self.latest[key] = val
        self.ops[eng].append((waits, fn, key, inc))
        for b in reads:
            if b.buf.r.get(key, 0) < val:
                b.buf.r[key] = val
        for b in writes:
            if b.buf.w.get(key, 0) < val:
                b.buf.w[key] = val

    def dma(self, q, out_t, out_ap, in_t, in_ap, **kw):
        self.op(q, lambda e: e.dma_start(out=out_ap, in_=in_ap, **kw), reads=[in_t], writes=[out_t], dma=out_t)

    def barrier(self):
        for e in self.ENG:
            waits = []
            for k, v in self.latest.items():
                if self.waited[e].get(k, 0) >= v:
                    continue
                self.waited[e][k] = v
                waits.append((k, v))
            self.ops[e].append((waits, None, None, 0))

    def begin(self, name="phase"):
        self.phase_label = name
        self.phase = contextlib.ExitStack()
        self.phase_names = set()
        self.phase_keys = []
        if not hasattr(self, "free_phys"):
            self.free_phys = []

    def end(self):
        self.barrier()
        scope = self.nc.named_scope("%s_%d" % (self.phase_label, self._scope_id()), notify=True) if PROFILE_SCOPES else contextlib.nullcontext()
        with scope, self.nc.Block() as block:
            decos = {"sp": block.sync, "act": block.scalar, "dve": block.vector,
                     "pool": block.gpsimd, "pe": block.tensor}
            for en in self.ENG:
                ops = self.ops[en]

                def body(e, ops=ops):
                    for waits, fn, key, inc in ops:
                        for k, v in waits:
                            e.wait_ge(self.sem(k), v)
                        if fn is not None:
                            fn(e).then_inc(self.sem(key), inc)

                decos[en](body)
        self.ops = {e: [] for e in self.ENG}
        for key in self.phase_keys:
            self.free_phys.append((self.sems.pop(key), self.latest.pop(key)))
            for e in self.ENG:
                self.waited[e].pop(key, None)
        self.phase.close()
        self.phase = None


def emit_mod(P, G, l):
    P.begin("mod")
    wm = [P.sb("wm%d" % i, [128, 8, 512], F32) for i in range(2)]
    rep = P.sb("rep", [128, 2, 8, 128], F32)
    bm = P.sb("bm", [1, 3072], F32)
    pm = [P.ps("pmod%d" % i, [128, 512], F32) for i in range(2)]
    wv = G.w_mod.t.ap()[l].rearrange("(k p) n -> p k n", p=128)
    P.dma("act", bm, bm[:], G.wsrc, G.b_mod.t.ap()[l:l + 1, :])
    for j in range(2):
        for k in range(8):
            P.op("dve", lambda e, j=j, k=k: e.tensor_scalar_mul(out=rep[:, j, k, :], in0=G.ones[:], scalar1=G.csil[:, j, k:k + 1]),
                 reads=[G.ones, G.csil], writes=[rep])
    for n in range(6):
        w = wm[n % 2]
        P.dma("sp", w, w[:], G.wsrc, wv[:, :, n * 512:(n + 1) * 512])
        for j in range(2):
            p = pm[j]
            for k in range(8):
                P.op("pe", lambda e, p=p, j=j, k=k, w=w: e.matmul(p[:], lhsT=rep[:, j, k, :], rhs=w[:, k, :],
                                                                start=(k == 0), stop=False),
                     reads=[rep, w], writes=[p])
            P.op("pe", lambda e, p=p, n=n: e.matmul(p[:], lhsT=G.ones[0:1, :], rhs=bm[0:1, n * 512:(n + 1) * 512],
                                                    start=False, stop=True), reads=[G.ones, bm], writes=[p])
            P.op("act", lambda e, p=p, j=j, n=n: e.copy(out=G.modt[j][:, n * 512:(n + 1) * 512], in_=p[:]),
                 reads=[p], writes=[G.modt[j]])
    for j in range(2):
        P.op("dve", lambda e, j=j: e.tensor_scalar_add(out=G.modt[j][:, 1024:2048], in0=G.modt[j][:, 1024:2048], scalar1=1.0),
             reads=[G.modt[j]], writes=[G.modt[j]])
    P.end()


def x_rows(xt, ti, colmajor, L):
    ap = xt.t.ap()
    if not colmajor:
        return ap[ti * 128:(ti + 1) * 128, :]
    rows = L // GRID_W
    per_col = rows // 128
    col, rb = ti // per_col, ti % per_col
    v = ap.rearrange("(r c) d -> c r d", c=GRID_W)
    return v[col, rb * 128:(rb + 1) * 128, :]


def emit_proj(P, G, xsrc, L, j, w_dram, W, MT, outT, pad, colmajor):
    P.begin("proj")
    NK = 8
    wbf = P.sb("wbf", [128, NK, W], BF16)
    wst = [P.sb("wst%d" % i, [128, 1536], F32) for i in range(2)]
    xt = [P.sb("xt%d" % i, [128, D], F32) for i in range(3)]
    hb = [P.sb("hb%d" % i, [128, D], BF16) for i in range(2)]
    hT = [P.sb("hT%d" % i, [128, NK, 512], BF16) for i in range(2)]
    ost = [P.sb("ost%d" % i, [128, 512], F32) for i in range(4)]
    pT = [P.ps("pT%d" % i, [128, NK * 128], BF16) for i in range(2)]
    pm = [P.ps("pm%d" % i, [128, 512], F32) for i in range(4)]
    modt = G.modt[j]
    wv = w_dram
    i = 0
    for k in range(NK):
        for cc in range(W // 1536):
            s = wst[i % 2]
            P.dma("sp" if i % 2 == 0 else "act", s, s[:], G.wsrc, wv[k * 128:(k + 1) * 128, cc * 1536:(cc + 1) * 1536])
            eng = "pool" if i % 2 == 0 else "dve"
            P.op(eng, lambda e, s=s, k=k, cc=cc: e.tensor_copy(out=wbf[:, k, cc * 1536:(cc + 1) * 1536], in_=s[:]),
                 reads=[s], writes=[wbf])
            i += 1
    CH = min(512, L)
    NS = CH // 128
    ntile = L // 128
    nchunk = L // CH
    nmt = W // MT

    def load(ti):
        t = xt[ti % 3]
        P.dma("sp", t, t[:], xsrc, x_rows(xsrc, ti, colmajor, L))

    load(0)
    load(1)
    ev = 0
    for c in range(nchunk):
        h = hT[c % 2]
        for sub in range(NS):
            ti = c * NS + sub
            if ti + 2 < ntile:
                load(ti + 2)
            t = xt[ti % 3]
            b = hb[ti % 2]
            p = pT[ti % 2]
            P.op("dve", lambda e, t=t: e.tensor_tensor(out=t[:], in0=t[:], in1=modt[:, 1024:2048], op=ALU.mult),
                 reads=[t, modt], writes=[t])
            P.op("pool", lambda e, t=t, b=b: e.tensor_tensor(out=b[:], in0=t[:], in1=modt[:, 0:1024], op=ALU.add),
                 reads=[t, modt], writes=[b])
            for k in range(NK):
                P.op("pe", lambda e, p=p, b=b, k=k: e.transpose(out=p[:, k * 128:(k + 1) * 128], in_=b[:, k * 128:(k + 1) * 128],
                                                              identity=G.ident[:]),
                     reads=[b, G.ident], writes=[p])
            P.op("act", lambda e, h=h, p=p, sub=sub: e.copy(out=h[:, :, sub * 128:(sub + 1) * 128],
                                                            in_=p[:].rearrange("p (k t) -> p k t", k=NK)),
                 reads=[p], writes=[h])
        for mt in range(nmt):
            p = pm[ev % 4]
            o = ost[ev % 4]
            for k in range(NK):
                P.op("pe", lambda e, p=p, h=h, k=k, mt=mt: e.matmul(p[:MT, :CH], lhsT=wbf[:, k, mt * MT:(mt + 1) * MT], rhs=h[:, k, :CH],
                                                                  start=(k == 0), stop=(k == NK - 1)),
                     reads=[wbf, h], writes=[p])
            if ev % 2 == 0:
                P.op("act", lambda e, p=p, o=o: e.copy(out=o[:MT, :CH], in_=p[:MT, :CH]), reads=[p], writes=[o])
            else:
                P.op("dve", lambda e, p=p, o=o: e.tensor_copy(out=o[:MT, :CH], in_=p[:MT, :CH]), reads=[p], writes=[o])
            ot_, orow = outT.at(mt * MT, (mt + 1) * MT)
            P.dma("sp" if ev % 2 == 0 else "act", ot_, orow[:, pad + c * CH: pad + (c + 1) * CH], o, o[:MT, :CH])
            ev += 1
    P.end()


def emit_out(P, G, l, xsrc, xdst, L, j, ygT, w_dram, KT, colmajor):
    P.begin("out")
    NKT = E // KT
    wbf = P.sb("wobf", [128, NKT, D], BF16)
    wst = [P.sb("wost%d" % i, [128, D], F32) for i in range(2)]
    yg = [P.sb("yg%d" % i, [128, NKT, 512], BF16) for i in range(2)]
    xt = [P.sb("xo%d" % i, [128, D], F32) for i in range(3)]
    zt = [P.sb("zt%d" % i, [128, D], F32) for i in range(2)]
    st = [P.sb("st%d" % i, [128, 2, 6], F32) for i in range(2)]
    mv = [P.sb("mv%d" % i, [128, 4], F32) for i in range(2)]
    po = [P.ps("po%d" % i, [128, D], F32) for i in range(2)]
    modt = G.modt[j]
    for k in range(NKT):
        s = wst[k % 2]
        P.dma("sp" if k % 2 == 0 else "act", s, s[:KT, :], G.wsrc, w_dram[k * KT:(k + 1) * KT, :])
        P.op("pool" if k % 2 == 0 else "dve", lambda e, s=s, k=k: e.tensor_copy(out=wbf[:KT, k, :], in_=s[:KT, :]),
             reads=[s], writes=[wbf])
    CH = min(512, L)
    NS = CH // 128
    ntile = L // 128
    nchunk = L // CH
    ygv = ygT.t.ap().rearrange("(k p) t -> p k t", p=KT)
    lng = P.sb("lng", [128, D], F32)
    lnb = P.sb("lnb", [128, D], F32)
    P.dma("sp", lng, lng[:], G.wsrc, G.ln_g.t.ap()[l:l + 1, :].partition_broadcast(128))
    P.dma("act", lnb, lnb[:], G.wsrc, G.ln_b.t.ap()[l:l + 1, :].partition_broadcast(128))

    def loadx(ti):
        t = xt[ti % 3]
        P.dma("sp", t, t[:], xsrc, x_rows(xsrc, ti, colmajor, L))

    def loady(c):
        y = yg[c % 2]
        P.dma("act", y, y[:KT, :, :CH], ygT, ygv[:, :, c * CH:(c + 1) * CH])

    loadx(0)
    loadx(1)
    loady(0)
    for c in range(nchunk):
        if c + 1 < nchunk:
            loady(c + 1)
        y = yg[c % 2]
        for sub in range(NS):
            ti = c * NS + sub
            if ti + 2 < ntile:
                loadx(ti + 2)
            t = xt[ti % 3]
            p = po[ti % 2]
            z = zt[ti % 2]
            s = st[ti % 2]
            m = mv[ti % 2]
            for n in range(2):
                for k in range(NKT):
                    P.op("pe", lambda e, p=p, y=y, k=k, n=n, sub=sub: e.matmul(
                        p[:, n * 512:(n + 1) * 512], lhsT=y[:KT, k, sub * 128:(sub + 1) * 128],
                        rhs=wbf[:KT, k, n * 512:(n + 1) * 512], start=(k == 0), stop=(k == NKT - 1)),
                        reads=[y, wbf], writes=[p])
            P.op("dve", lambda e, p=p, z=z: e.tensor_tensor(out=z[:], in0=p[:], in1=modt[:, 2048:3072], op=ALU.mult),
                 reads=[p, modt], writes=[z])
            P.op("dve", lambda e, t=t, z=z: e.scalar_tensor_tensor(out=z[:], in0=t[:], scalar=ALPHA, in1=z[:],
                                                                   op0=ALU.mult, op1=ALU.add),
                 reads=[t, z], writes=[z])
            for n in range(2):
                P.op("dve", lambda e, s=s, z=z, n=n: e.bn_stats(out=s[:, n, :], in_=z[:, n * 512:(n + 1) * 512]),
                     reads=[z], writes=[s])
            P.op("dve", lambda e, s=s, m=m: e.bn_aggr(out=m[:, 0:2], in_=s[:].rearrange("p a b -> p (a b)")),
                 reads=[s], writes=[m])
            P.op("act", lambda e, m=m: e.activation(out=m[:, 2:3], in_=m[:, 1:2], func=AF.Sqrt, bias=G.eps[:, 0:1], scale=1.0),
                 reads=[m, G.eps], writes=[m])
            P.op("dve", lambda e, m=m: e.reciprocal(out=m[:, 3:4], in_=m[:, 2:3]), reads=[m], writes=[m])
            P.op("dve", lambda e, m=m, z=z: e.tensor_scalar(out=z[:], in0=z[:], scalar1=m[:, 0:1], scalar2=m[:, 3:4],
                                                            op0=ALU.subtract, op1=ALU.mult), reads=[m, z], writes=[z])
            P.op("pool", lambda e, z=z: e.tensor_tensor(out=z[:], in0=z[:], in1=lng[:], op=ALU.mult),
                 reads=[z, lng], writes=[z])
            P.op("pool", lambda e, z=z: e.tensor_tensor(out=z[:], in0=z[:], in1=lnb[:], op=ALU.add),
                 reads=[z, lnb], writes=[z])
            P.dma("sp", xdst, x_rows(xdst, ti, colmajor, L), z, z[:])
    P.end()


def run_streams(gens, stagger=0):
    active = list(gens)
    for i, g in enumerate(list(active)):
        for _ in range(i * stagger):
            try:
                next(g)
            except StopIteration:
                active.remove(g)
                break
    while active:
        for g in list(active):
            try:
                next(g)
            except StopIteration:
                active.remove(g)


def emit_rg(P, G, occ, uT, ucT, pad, L, ygT, ygcT, need_ctx):
    P.begin("rg")
    TC = 512
    NT = E // 96
    NS = 6
    par = P.sb("rgpar", [96, NT, 16], F32)
    cst = P.sb("rgcst", [96, NT, 4], F32)
    zero = P.sb("zero1", [96, 1], F32)
    P.op("dve", lambda e: e.memset(zero[:], 0.0), writes=[zero])
    P.dma("sp", par, par[:], G.wsrc, G.rgpar.t.ap()[occ])
    P.op("act", lambda e: e.activation(out=cst[:, :, 0:2], in_=par[:, :, 9:11], func=AF.Exp, scale=-1.0), reads=[par], writes=[cst])
    P.op("act", lambda e: e.activation(out=cst[:, :, 0:2], in_=cst[:, :, 0:2], func=AF.Ln, bias=1.0), reads=[cst], writes=[cst])
    P.op("dve", lambda e: e.tensor_scalar_mul(out=cst[:, :, 2:4], in0=cst[:, :, 0:2], scalar1=-16.0), reads=[cst], writes=[cst])
    P.op("dve", lambda e: e.tensor_scalar_mul(out=cst[:, :, 0:2], in0=cst[:, :, 0:2], scalar1=-8.0), reads=[cst], writes=[cst])

    def rev(ap, d):
        return ap if d == 0 else ap[:, ::-1]

    def stream(s):
        n_ = "s%d" % s
        wg32 = P.sb("wg32" + n_, [96, 4, 96], F32)
        w = P.sb("wg" + n_, [96, 4, 96], BF16)
        ut = [P.sb("ut%d" % i + n_, [96, TC + 3], F32) for i in range(2)]
        gt = P.sb("gt" + n_, [96, TC], F32)
        xct = P.sb("xcs" + n_, [96, TC], F32)
        xbt = P.sb("xb" + n_, [96, TC], BF16)
        r = P.sb("rr" + n_, [96, TC], F32)
        i_ = P.sb("ii" + n_, [96, TC], F32)
        a = P.sb("aa" + n_, [96, TC], F32)
        m = P.sb("mm" + n_, [96, TC], F32)
        b = P.sb("bx" + n_, [96, TC], F32)
        hh = [P.sb("hh%d" % i + n_, [96, TC], F32) for i in range(2)]
        hfb = P.sb("hfb" + n_, [96, TC], BF16)
        hfl = P.sb("hfl" + n_, [96, TC], BF16)
        hfc = P.sb("hfc" + n_, [96, CTX], F32)
        y = P.sb("yy" + n_, [96, TC], BF16)
        p = P.ps("pr" + n_, [96, TC], F32)
        cnt = {"c": 0}

        def chunk(tj, src, col0, n, d, hinit):
            ci = cnt["c"]
            cnt["c"] += 1
            u = ut[ci % 2]
            h = hh[ci % 2]
            st_, srow = src.at(tj * 96, (tj + 1) * 96)
            P.dma("sp", u, u[:, :n + 3], st_, srow[:, col0:col0 + n + 3])
            yield
            P.op("act", lambda e: e.activation(out=xct[:, :n], in_=u[:, 0:n], func=AF.Identity,
                                               scale=par[:, tj, 0:1], bias=par[:, tj, 4:5]), reads=[u, par], writes=[xct])
            yield
            for k in range(1, 4):
                P.op("dve", lambda e, k=k: e.scalar_tensor_tensor(out=xct[:, :n], in0=u[:, k:k + n], scalar=par[:, tj, k:k + 1],
                                                                 in1=xct[:, :n], op0=ALU.mult, op1=ALU.add),
                     reads=[u, par, xct], writes=[xct])
            yield
            P.op("act", lambda e: e.copy(out=xbt[:, :n], in_=xct[:, :n]), reads=[xct], writes=[xbt])
            yield
            for g in range(2):
                P.op("pe", lambda e, g=g: e.matmul(p[:, :n], lhsT=w[:, 2 * d + g, :], rhs=xbt[:, :n], start=True, stop=True),
                     reads=[w, xbt], writes=[p])
                yield
                dst = r if g == 0 else i_
                bcol = par[:, tj, 5 + 2 * d + g: 6 + 2 * d + g]
                P.op("act", lambda e, dst=dst, bcol=bcol: e.activation(out=dst[:, :n], in_=p[:, :n], func=AF.Sigmoid, bias=bcol),
                     reads=[p, par], writes=[dst])
                yield
            P.op("act", lambda e: e.activation(out=a[:, :n], in_=rev(r[:, :n], d), func=AF.Exp, scale=cst[:, tj, d:d + 1]),
                 reads=[r, cst], writes=[a])
            P.op("dve", lambda e: e.tensor_tensor(out=i_[:, :n], in0=i_[:, :n], in1=xct[:, :n], op=ALU.mult), reads=[i_, xct], writes=[i_])
            yield
            P.op("act", lambda e: e.activation(out=m[:, :n], in_=rev(r[:, :n], d), func=AF.Exp, scale=cst[:, tj, 2 + d:3 + d]),
                 reads=[r, cst], writes=[m])
            yield
            P.op("act", lambda e: e.activation(out=m[:, :n], in_=m[:, :n], func=AF.Sqrt, scale=-1.0, bias=1.0), reads=[m], writes=[m])
            yield
            P.op("dve", lambda e: e.tensor_tensor(out=b[:, :n], in0=rev(i_[:, :n], d), in1=m[:, :n], op=ALU.mult), reads=[i_, m], writes=[b])
            yield
            init_t, init_ap = hinit
            P.op("dve", lambda e: e.tensor_tensor_scan(out=h[:, :n], data0=a[:, :n], data1=b[:, :n], initial=init_ap,
                                                       op0=ALU.mult, op1=ALU.add), reads=[a, b, init_t], writes=[h])
            yield
            return h, (h, h[:, n - 1:n])

        def gate_out(tj, h, n, hfwd_ap, hfwd_t, gsrc, gcol0, dst, dcol0):
            gs_, grow = gsrc.at(E + tj * 96, E + (tj + 1) * 96)
            P.dma("sp", gt, gt[:, :n], gs_, grow[:, gcol0:gcol0 + n])
            yield
            P.op("act", lambda e: e.activation(out=gt[:, :n], in_=gt[:, :n], func=AF.Silu), reads=[gt], writes=[gt])
            P.op("dve", lambda e: e.tensor_tensor(out=b[:, :n], in0=h[:, :n][:, ::-1], in1=hfwd_ap, op=ALU.add),
                 reads=[h, hfwd_t], writes=[b])
            yield
            P.op("dve", lambda e: e.tensor_tensor(out=y[:, :n], in0=b[:, :n], in1=gt[:, :n], op=ALU.mult), reads=[b, gt], writes=[y])
            yield
            P.dma("sp", dst, dst.t.ap()[tj * 96:(tj + 1) * 96, dcol0:dcol0 + n], y, y[:, :n])
            yield

        for tj in range(s, NT, NS):
            hfd = G.hfd[tj]
            P.dma("sp", wg32, wg32[:], G.wsrc, G.rgw.t.ap()[occ, tj])
            P.op("pool", lambda e: e.tensor_copy(out=w[:], in_=wg32[:]), reads=[wg32], writes=[w])
            yield
            z0 = (zero, zero[:, 0:1])
            h, carry = yield from chunk(tj, ucT, 0, CTX, 0, z0)
            if need_ctx:
                P.op("pool", lambda e, h=h: e.tensor_copy(out=hfc[:], in_=h[:, :CTX]), reads=[h], writes=[hfc])
            for c in range(L // TC):
                h, carry = yield from chunk(tj, uT, c * TC, TC, 0, carry)
                P.op("act", lambda e, h=h: e.copy(out=hfb[:], in_=h[:]), reads=[h], writes=[hfb])
                yield
                P.dma("sp", hfd, hfd.t.ap()[:, c * TC:(c + 1) * TC], hfb, hfb[:])
                yield
            h, carry = yield from chunk(tj, ucT, 0, CTX, 1, z0)
            if need_ctx:
                yield from gate_out(tj, h, CTX, hfc[:], hfc, ucT, pad, ygcT, 0)
            for c in range(L // TC - 1, -1, -1):
                P.dma("sp", hfl, hfl[:], hfd, hfd.t.ap()[:, c * TC:(c + 1) * TC])
                h, carry = yield from chunk(tj, uT, c * TC, TC, 1, carry)
                yield from gate_out(tj, h, TC, hfl[:], hfl, uT, pad + c * TC, ygT, c * TC)

    run_streams([stream(s) for s in range(NS)], stagger=0)
    P.end()


PI = float(np.pi)
SKIP_CTX_HY = False
PROFILE_SCOPES = False
HYDBG_NB = 2
CBL = 4
CBC = 32
FH = 64
NEMB = 17


def emit_hyfilt(P, G, occ, Lf, L, zt_d, ztr_d, trow_d, tapsd, normd):
    P.begin("hyfilt")
    NCT = E // 128
    w1 = P.sb("fw1", [NEMB, FH], F32)
    w2 = P.sb("fw2", [FH, FH], F32)
    fp = P.sb("fpar", [FH, 4], F32)
    fb = P.sb("fb", [FH, 2], F32)
    w3s = [P.sb("w3s%d" % i, [FH, 1536], F32) for i in range(2)]
    w3 = P.sb("w3bf", [FH, 4 * E], BF16)
    hid = [P.sb("hid%d" % i, [FH, Lf], BF16) for i in range(2)]
    zc = [P.sb("zc%d" % i, [NEMB, 512], F32) for i in range(2)]
    vv = [P.sb("vv%d" % i, [FH, 512], F32) for i in range(2)]
    mk = [P.sb("mk%d" % i, [FH, 512], F32) for i in range(2)]
    h1 = [P.sb("h1%d" % i, [FH, 512], F32) for i in range(2)]
    nd = P.sb("negd", [128, NCT], F32)
    tw = [P.sb("tw%d" % i, [128, 512], F32) for i in range(2)]
    win = [P.sb("win%d" % i, [128, 512], F32) for i in range(4)]
    tp = [P.sb("tp%d" % i, [128, 512], BF16) for i in range(8)]
    junk = P.sb("junk", [128, 512], BF16)
    ac = [P.sb("ac%d" % i, [128, 1], F32) for i in range(8)]
    nacc = P.sb("nacc", [128, 2, NCT], F32)
    pa = [P.ps("pfa%d" % i, [FH, 512], F32) for i in range(2)]
    pb = [P.ps("pfb%d" % i, [128, 512], F32) for i in range(6)]
    P.dma("sp", w1, w1[:], G.wsrc, G.hy_f_w1.t.ap()[occ])
    P.dma("sp", w2, w2[:], G.wsrc, G.hy_f_w2.t.ap()[occ])
    P.dma("sp", fp, fp[:, 0:3], G.wsrc, G.hyfp.t.ap()[occ])
    P.dma("sp", nd, nd[:], G.wsrc, G.negd.t.ap())
    P.op("dve", lambda e: e.memset(nacc[:], 0.0), writes=[nacc])
    P.op("dve", lambda e: e.tensor_scalar_mul(out=fb[:, 0:1], in0=fp[:, 1:2], scalar1=fp[:, 0:1]), reads=[fp], writes=[fb])
    P.op("dve", lambda e: e.tensor_scalar_mul(out=fb[:, 1:2], in0=fp[:, 2:3], scalar1=fp[:, 0:1]), reads=[fp], writes=[fb])
    for i in range(4):
        s = w3s[i % 2]
        P.dma("act", s, s[:], G.wsrc, G.hy_f_w3.t.ap()[occ][:, i * 1536:(i + 1) * 1536])
        P.op("pool", lambda e, s=s, i=i: e.tensor_copy(out=w3[:, i * 1536:(i + 1) * 1536], in_=s[:]), reads=[s], writes=[w3])

    def sin_layer(ps, n, bcol, out_ap, out_t, v, m):
        P.op("act", lambda e: e.activation(out=v[:, :n], in_=ps[:, :n], func=AF.Identity, scale=fp[:, 0:1], bias=bcol),
             reads=[ps, fp, fb], writes=[v])
        for it in range(2):
            for thr, add, cmp in ((PI, -2 * PI, ALU.is_gt), (-PI, 2 * PI, ALU.is_lt)):
                P.op("dve", lambda e, thr=thr, add=add, cmp=cmp: e.tensor_scalar(out=m[:, :n], in0=v[:, :n], scalar1=thr, scalar2=add,
                                                                               op0=cmp, op1=ALU.mult), reads=[v], writes=[m])
                P.op("dve", lambda e: e.tensor_tensor(out=v[:, :n], in0=v[:, :n], in1=m[:, :n], op=ALU.add), reads=[v, m], writes=[v])
        P.op("act", lambda e: e.activation(out=out_ap, in_=v[:, :n], func=AF.Sin), reads=[v], writes=[out_t])

    it = 0
    for dr, ztab in ((0, zt_d), (1, ztr_d)):
        for c0 in range(0, Lf, 512):
            n = min(512, Lf - c0)
            z, v, m, hh1 = zc[it % 2], vv[it % 2], mk[it % 2], h1[it % 2]
            p1 = pa[it % 2]
            it += 1
            P.dma("sp", z, z[:, :n], G.wsrc, ztab.t.ap()[:, c0:c0 + n])
            P.op("pe", lambda e, p1=p1, z=z, n=n: e.matmul(p1[:, :n], lhsT=w1[:], rhs=z[:, :n], start=True, stop=True),
                 reads=[w1, z], writes=[p1])
            sin_layer(p1, n, fb[:, 0:1], hh1[:, :n], hh1, v, m)
            P.op("pe", lambda e, p1=p1, hh1=hh1, n=n: e.matmul(p1[:, :n], lhsT=w2[:], rhs=hh1[:, :n], start=True, stop=True),
                 reads=[w2, hh1], writes=[p1])
            sin_layer(p1, n, fb[:, 1:2], hid[dr][:, c0:c0 + n], hid[dr], v, m)
    its = []
    itt = 0
    for dr in range(2):
        for c0 in range(0, Lf, 512):
            n = min(512, Lf - c0)
            for ct in range(NCT):
                its.append((dr, c0, n, ct, ct == 0))
    twcur = {}

    def S1(i):
        dr, c0, n, ct, first = its[i]
        if first:
            t_ = tw[(i // NCT) % 2]
            twcur[(dr, c0)] = t_
            P.dma("sp", t_, t_[:, :n], G.wsrc, trow_d.t.ap()[dr, c0:c0 + n].partition_broadcast(128))
        t_ = twcur[(dr, c0)]
        w_ = win[i % 4]
        P.op("act", lambda e: e.activation(out=w_[:, :n], in_=t_[:, :n], func=AF.Exp, scale=nd[:, ct:ct + 1]), reads=[t_, nd], writes=[w_])
        for o in range(2):
            p = pb[(i * 2 + o) % 6]
            col = o * 2 * E + dr * E + ct * 128
            P.op("pe", lambda e, p=p, col=col: e.matmul(p[:, :n], lhsT=w3[:, col:col + 128], rhs=hid[dr][:, c0:c0 + n], start=True, stop=True),
                 reads=[w3, hid[dr]], writes=[p])

    def S2(i):
        dr, c0, n, ct, first = its[i]
        w_ = win[i % 4]
        for o in range(2):
            kk = i * 2 + o
            p, tq, a_ = pb[kk % 6], tp[kk % 8], ac[kk % 8]
            P.op("dve", lambda e, p=p, tq=tq: e.scalar_tensor_tensor(out=tq[:, :n], in0=w_[:, :n], scalar=0.05, in1=p[:, :n],
                                                                    op0=ALU.add, op1=ALU.mult), reads=[w_, p], writes=[tq])
            if dr == 1 and c0 == 0:
                P.op("dve", lambda e, tq=tq: e.memset(tq[:, 0:1], 0.0), reads=[tq], writes=[tq])
            P.op("act", lambda e, tq=tq, a_=a_: e.activation(out=junk[:, :n], in_=tq[:, :n], func=AF.Abs, accum_out=a_[:, 0:1]),
                 reads=[tq], writes=[junk, a_])
            pos = c0 if dr == 0 else 2 * L - Lf + c0
            P.dma(("sp", "act", "pool")[kk % 3], tapsd, tapsd.t.ap()[o, ct * 128:(ct + 1) * 128, pos:pos + n], tq, tq[:, :n])

    def S3(i):
        dr, c0, n, ct, first = its[i]
        for o in range(2):
            a_ = ac[(i * 2 + o) % 8]
            P.op("dve", lambda e, a_=a_, o=o: e.tensor_tensor(out=nacc[:, o, ct:ct + 1], in0=nacc[:, o, ct:ct + 1], in1=a_[:, 0:1], op=ALU.add),
                 reads=[a_, nacc], writes=[nacc])

    NI = len(its)
    for i in range(-2, NI + 1):
        if 0 <= i + 2 < NI:
            S1(i + 2)
        if 0 <= i < NI:
            S2(i)
        if 0 <= i - 1 < NI:
            S3(i - 1)
    P.dma("sp", normd, normd.t.ap().rearrange("o (c p) -> p o c", p=128), nacc, nacc[:], allow_slow_non_contiguous=True)
    if getattr(G, "dbg_hid", None) is not None:
        for dr in range(2):
            P.dma("sp", G.dbg_hid, G.dbg_hid.t.ap()[dr], hid[dr], hid[dr][:])
        P.dma("sp", G.dbg_nacc, G.dbg_nacc.t.ap(), nacc, nacc[:])
    P.end()


def emit_hyconv(P, G, occ, uT, L, Pn, CB, tb_d, tf_d, hypar_d, tapsd, normd, ygT, nb_limit=None):
    assert L == Pn * Pn
    P.begin("hyconv")
    NS = 2
    GS = 512 // Pn
    NB = E // CB
    W = L + 3
    tb = P.sb("tb16", [Pn, 17 * Pn], BF16)
    tf = P.sb("tf32", [Pn, 4 * Pn], F32)
    FA0, FA1 = tb[:, 0:2 * Pn], tb[:, 2 * Pn:4 * Pn]
    FCre, FCim, FCimN = tb[:, 4 * Pn:5 * Pn], tb[:, 5 * Pn:6 * Pn], tb[:, 6 * Pn:7 * Pn]
    FCc, FCc2 = tb[:, 7 * Pn:9 * Pn], tb[:, 9 * Pn:11 * Pn]
    CA, SA = tb[:, 11 * Pn:12 * Pn], tb[:, 12 * Pn:13 * Pn]
    FCreN, FCcN, CAN = tb[:, 13 * Pn:14 * Pn], tb[:, 14 * Pn:16 * Pn], tb[:, 16 * Pn:17 * Pn]

    def bc(ap2d):
        return ap2d.unsqueeze(1).to_broadcast([Pn, CB, Pn])

    TW = (bc(tf[:, 0:Pn]), bc(tf[:, Pn:2 * Pn]))
    TWT = (bc(tf[:, 2 * Pn:3 * Pn]), bc(tf[:, 3 * Pn:4 * Pn]))
    P.dma("sp", tb, tb[:], G.wsrc, tb_d.t.ap())
    P.dma("sp", tf, tf[:], G.wsrc, tf_d.t.ap())
    hv = hypar_d.t.ap()

    def bcw(ap2d, n):
        return ap2d.unsqueeze(2).to_broadcast([Pn, n, Pn])

    def stream(s):
        n_ = "h%d" % s
        xs = P.sb("xs" + n_, [Pn, 3 * CB, Pn + 2], F32)
        vx = P.sb("vx" + n_, [Pn, 3 * CB, Pn], F32)
        tca = P.sb("tca" + n_, [Pn, 3 * CB, Pn], F32)
        tcb = P.sb("tcb" + n_, [Pn, 3 * CB, Pn], F32)
        vb = P.sb("vb" + n_, [Pn, CB, Pn], BF16)
        gt = P.sb("gth" + n_, [Pn, CB, Pn], F32)
        kt = [P.sb("kt%d" % i + n_, [Pn, CB, 2, Pn], BF16) for i in range(2)]
        KF = [P.sb("KF%d" % i + n_, [Pn, 2, CB, Pn], F32) for i in range(2)]
        hp = P.sb("hp" + n_, [Pn, 14 * CB], F32)
        rn = P.sb("rn" + n_, [Pn, 2, CB], F32)
        Asb = [P.sb("Asb%d" % i + n_, [Pn, 2 * CB * Pn], F32) for i in range(2)]
        tq = [[P.sb("tq%d_%d" % (i, j) + n_, [Pn, CB, Pn], BF16) for j in range(4)] for i in range(2)]
        zz = P.sb("zz" + n_, [Pn, CB, Pn], F32)
        zzb = P.sb("zzb" + n_, [Pn, CB, Pn], BF16)
        tcm = P.sb("tcm" + n_, [Pn, CB, Pn], F32)
        ygo = P.sb("ygo" + n_, [Pn, CB, Pn], BF16)
        pA = P.ps("pA" + n_, [Pn, 2 * CB * Pn], F32)
        pC = P.ps("pC" + n_, [Pn, 2 * CB * Pn], F32)
        cnt = {"x": 0}
        yv = pC[:, 0:CB * Pn].rearrange("p (c k) -> p c k", c=CB)

        def cmul(src, tabs, tab_re, tab_im, srcv):
            ts = tq[cnt["x"] % 2]
            cnt["x"] += 1
            sre, sim = srcv(0), srcv(1)
            for j, (s_, t_) in enumerate(((sre, tab_re), (sim, tab_im), (sre, tab_im), (sim, tab_re))):
                P.op("dve", lambda e, j=j, s_=s_, t_=t_: e.tensor_tensor(out=ts[j][:], in0=s_, in1=t_, op=ALU.mult), reads=[src] + tabs, writes=[ts[j]])
                yield
            return ts

        def fwd(lhs_list):
            k = cnt["x"] % 2
            for cb in range(CB):
                ents = lhs_list[cb]
                for j, (lap, tab, tt) in enumerate(ents):
                    P.op("pe", lambda e, cb=cb, lap=lap, tab=tab, j=j, nj=len(ents): e.matmul(
                        pA[:, cb * 2 * Pn:(cb + 1) * 2 * Pn], lhsT=lap, rhs=tab, start=(j == 0), stop=(j == nj - 1)),
                        reads=[tt, tb], writes=[pA])
            yield
            A_ = Asb[k]
            P.op("act", lambda e: e.copy(out=A_[:], in_=pA[:]), reads=[pA], writes=[A_])
            yield
            Av = A_[:].rearrange("p (c r k) -> p c r k", c=CB, r=2)
            ts = yield from cmul(A_, [tf], TW[0], TW[1], lambda r_: Av[:, :, r_, :])
            for g in range(CB // GS):
                tg = [t[:, g * GS:(g + 1) * GS, :].rearrange("p c k -> p (c k)") for t in ts]
                ore = pC[:, g * 512:(g + 1) * 512]
                oim = pC[:, CB * Pn + g * 512:CB * Pn + (g + 1) * 512]
                for dst, terms in ((ore, ((FCre, 0), (FCreN, 1), (FCimN, 2), (FCimN, 3))), (oim, ((FCre, 2), (FCre, 3), (FCim, 0), (FCimN, 1)))):
                    for j, (tab, ti) in enumerate(terms):
                        P.op("pe", lambda e, dst=dst, tab=tab, rhs=tg[ti], j=j: e.matmul(dst, lhsT=tab, rhs=rhs, start=(j == 0), stop=(j == 3)),
                             reads=[tb, ts[ti]], writes=[pC])
            yield
            Y_ = Asb[(k + 1) % 2]
            P.op("act", lambda e: e.copy(out=Y_[:], in_=pC[:]), reads=[pC], writes=[Y_])
            yield
            return Y_

        def inv(Y_, o):
            Yv = Y_[:].rearrange("p (r c k) -> p r c k", r=2, c=CB)
            ts = yield from cmul(Y_, [KF[o]], KF[o][:, 0, :, :], KF[o][:, 1, :, :], lambda r_: Yv[:, r_, :, :])
            for cb in range(CB):
                for j, (tab, ti) in enumerate(((FCc, 0), (FCcN, 1), (FCc2, 2), (FCc2, 3))):
                    P.op("pe", lambda e, cb=cb, tab=tab, ti=ti, j=j: e.matmul(pA[:, cb * 2 * Pn:(cb + 1) * 2 * Pn], lhsT=ts[ti][:, cb, :], rhs=tab,
                                                                            start=(j == 0), stop=(j == 3)), reads=[ts[ti], tb], writes=[pA])
            yield
            A_ = Asb[cnt["x"] % 2]
            P.op("act", lambda e: e.copy(out=A_[:], in_=pA[:]), reads=[pA], writes=[A_])
            yield
            Av = A_[:].rearrange("p (c r k) -> p c r k", c=CB, r=2)
            ts2 = yield from cmul(A_, [tf], TWT[0], TWT[1], lambda r_: Av[:, :, r_, :])
            for g in range(CB // GS):
                tg = [t[:, g * GS:(g + 1) * GS, :].rearrange("p c k -> p (c k)") for t in ts2]
                oy = pC[:, g * 512:(g + 1) * 512]
                for j, (tab, ti) in enumerate(((CA, 0), (CAN, 1), (SA, 2), (SA, 3))):
                    P.op("pe", lambda e, oy=oy, tab=tab, rhs=tg[ti], j=j: e.matmul(oy, lhsT=tab, rhs=rhs, start=(j == 0), stop=(j == 3)),
                         reads=[tb, ts2[ti]], writes=[pC])
            yield

        nbs = NB if nb_limit is None else nb_limit
        for bi_ in range(s, nbs, NS):
            c0 = bi_ * CB
            P.dma("sp", hp, hp[:], G.wsrc, hv[occ, bi_, :].partition_broadcast(Pn))
            P.dma("sp", rn, rn[:], normd, normd.t.ap()[:, c0:c0 + CB].partition_broadcast(Pn))
            for q in range(3):
                ut_, urow = uT.at(q * E + c0, q * E + c0 + CB)
                halo = bass.AP(urow.tensor, c0 * W, [[Pn, Pn], [W, CB], [1, Pn + 2]])
                P.dma("sp", xs, xs[:, q * CB:(q + 1) * CB, :], ut_, halo)
            ut_, urow = uT.at(3 * E + c0, 3 * E + c0 + CB)
            P.dma("sp", gt, gt[:], ut_, urow[:, 1:1 + L].rearrange("c (a b) -> a c b", b=Pn))
            for o in range(2):
                P.dma("sp", kt[o], kt[o][:], tapsd, tapsd.t.ap()[o, c0:c0 + CB, :].rearrange("c (h a b) -> a c h b", h=2, b=Pn))
            yield
            P.op("dve", lambda e: e.reciprocal(out=rn[:], in_=rn[:]), reads=[rn], writes=[rn])
            P.op("dve", lambda e: e.tensor_tensor(out=vx[:], in0=xs[:, :, 0:Pn], in1=bcw(hp[:, 0:3 * CB], 3 * CB), op=ALU.mult), reads=[xs, hp], writes=[vx])
            P.op("dve", lambda e: e.tensor_tensor(out=tca[:], in0=xs[:, :, 1:Pn + 1], in1=bcw(hp[:, 3 * CB:6 * CB], 3 * CB), op=ALU.mult), reads=[xs, hp], writes=[tca])
            yield
            P.op("dve", lambda e: e.tensor_tensor(out=tcb[:], in0=xs[:, :, 2:Pn + 2], in1=bcw(hp[:, 6 * CB:9 * CB], 3 * CB), op=ALU.mult), reads=[xs, hp], writes=[tcb])
            P.op("dve", lambda e: e.tensor_tensor(out=vx[:], in0=vx[:], in1=bcw(hp[:, 9 * CB:12 * CB], 3 * CB), op=ALU.add), reads=[vx, hp], writes=[vx])
            yield
            P.op("dve", lambda e: e.tensor_tensor(out=vx[:], in0=vx[:], in1=tca[:], op=ALU.add), reads=[vx, tca], writes=[vx])
            yield
            P.op("dve", lambda e: e.tensor_tensor(out=vx[:], in0=vx[:], in1=tcb[:], op=ALU.add), reads=[vx, tcb], writes=[vx])
            yield
            P.op("act", lambda e: e.copy(out=vb[:], in_=vx[:, 0:CB, :]), reads=[vx], writes=[vb])
            P.op("act", lambda e: e.activation(out=gt[:], in_=gt[:], func=AF.Silu), reads=[gt], writes=[gt])
            yield
            for o in range(2):
                Y_ = yield from fwd([[(kt[o][:, cb, 0, :], FA0, kt[o]), (kt[o][:, cb, 1, :], FA1, kt[o])] for cb in range(CB)])
                P.op("dve", lambda e, Y_=Y_, o=o: e.tensor_tensor(
                    out=KF[o][:], in0=Y_[:].rearrange("p (r c k) -> p r c k", r=2, c=CB),
                    in1=rn[:, o, :].unsqueeze(1).unsqueeze(3).to_broadcast([Pn, 2, CB, Pn]), op=ALU.mult), reads=[Y_, rn], writes=[KF[o]])
                yield
            Y_ = yield from fwd([[(vb[:, cb, :], FA0, vb)] for cb in range(CB)])
            yield from inv(Y_, 0)
            P.op("dve", lambda e: e.tensor_tensor(out=tcm[:], in0=vx[:, 0:CB, :], in1=bcw(hp[:, 12 * CB:13 * CB], CB), op=ALU.mult), reads=[vx, hp], writes=[tcm])
            yield
            P.op("dve", lambda e: e.tensor_tensor(out=tcm[:], in0=tcm[:], in1=yv, op=ALU.add), reads=[tcm, pC], writes=[tcm])
            yield
            P.op("dve", lambda e: e.tensor_tensor(out=zz[:], in0=tcm[:], in1=vx[:, CB:2 * CB, :], op=ALU.mult), reads=[tcm, vx], writes=[zz])
            yield
            P.op("act", lambda e: e.copy(out=zzb[:], in_=zz[:]), reads=[zz], writes=[zzb])
            yield
            Y_ = yield from fwd([[(zzb[:, cb, :], FA0, zzb)] for cb in range(CB)])
            yield from inv(Y_, 1)
            P.op("dve", lambda e: e.tensor_tensor(out=tcm[:], in0=zz[:], in1=bcw(hp[:, 13 * CB:14 * CB], CB), op=ALU.mult), reads=[zz, hp], writes=[tcm])
            P.op("dve", lambda e: e.tensor_tensor(out=zz[:], in0=vx[:, 2 * CB:3 * CB, :], in1=gt[:], op=ALU.mult), reads=[vx, gt], writes=[zz])
            yield
            P.op("dve", lambda e: e.tensor_tensor(out=tcm[:], in0=tcm[:], in1=yv, op=ALU.add), reads=[tcm, pC], writes=[tcm])
            yield
            P.op("dve", lambda e: e.tensor_tensor(out=ygo[:], in0=tcm[:], in1=zz[:], op=ALU.mult), reads=[tcm, zz], writes=[ygo])
            yield
            P.dma("sp", ygT, ygT.t.ap()[c0:c0 + CB, :].rearrange("c (a b) -> a c b", b=Pn), ygo, ygo[:])
            yield

    run_streams([stream(s) for s in range(NS)], stagger=40)
    P.end()


class G_:
    pass


def emit_setup(P, G, L):
    P.begin("setup")
    zt = P.sb("zpad", [128, 4], F32)
    P.op("dve", lambda e: e.memset(G.ones[:], 1.0), writes=[G.ones])
    P.op("dve", lambda e: e.memset(G.eps[:], LN_EPS), writes=[G.eps])
    P.op("pool", lambda e: e.memset(zt[:], 0.0), writes=[zt])
    P.dma("sp", G.ident, G.ident[:], G.wsrc, G.identd.t.ap())
    P.dma("sp", G.csil, G.csil[:], G.wsrc, G.cc.t.ap())
    P.op("act", lambda e: e.activation(out=G.csil[:], in_=G.csil[:], func=AF.Silu), reads=[G.csil], writes=[G.csil])
    for t_, n in [(t, L) for t in G.uT.ts] + [(t, CTX) for t in G.ucT.ts]:
        rows = t_.t.ap().shape[0]
        for r0 in range(0, rows, 128):
            P.dma("sp", t_, t_.t.ap()[r0:r0 + 128, 0:1], zt, zt[:, 0:1], allow_slow_non_contiguous=True)
            P.dma("act", t_, t_.t.ap()[r0:r0 + 128, 1 + n:3 + n], zt, zt[:, 0:2], allow_slow_non_contiguous=True)
    P.end()


def build_nc(L, nlayers=DEPTH, debug=False):
    nc = bass.Bass("TRN2", target_bir_lowering=False)
    P = Prog(nc)
    G = G_()

    def inp(name, shape, dt=F32):
        return P.dram(name, shape, dt, kind="ExternalInput")

    G.x_in = inp("x", [L, D])
    G.ctx_in = inp("ctx", [CTX, D])
    G.cc = inp("cc", [128, 2, 8])
    G.identd = inp("ident", [128, 128], BF16)
    G.w_mod = inp("w_mod", [DEPTH, D, 3 * D])
    G.b_mod = inp("b_mod", [DEPTH, 3 * D])
    G.ln_g = inp("ln_g", [DEPTH, D])
    G.ln_b = inp("ln_b", [DEPTH, D])
    G.rg_w_in = inp("rg_w_in", [2, D, 2 * E])
    G.rgpar = inp("rgpar", [2, 96, E // 96, 16])
    G.rgw = inp("rgw", [2, E // 96, 96, 4, 96])
    G.rg_w_out = inp("rg_w_out", [2, E, D])
    G.hy_w_in = inp("hy_w_in", [2, D, 4 * E])
    G.hy_w_out = inp("hy_w_out", [2, E, D])
    G.hy_f_w1 = inp("hy_f_w1", [2, NEMB, FH])
    G.hy_f_w2 = inp("hy_f_w2", [2, FH, FH])
    G.hy_f_w3 = inp("hy_f_w3", [2, FH, 4 * E])
    G.hyfp = inp("hyfp", [2, FH, 3])
    G.hypar = inp("hypar", [2, E // CBL, 14 * CBL])
    G.hyparc = inp("hyparc", [2, E // CBC, 14 * CBC])
    G.tb16c = inp("tb16c", [16, 17 * 16], BF16)
    G.tf32c = inp("tf32c", [16, 4 * 16])
    G.ztc = inp("ztc", [NEMB, CTX])
    G.ztrc = inp("ztrc", [NEMB, CTX])
    G.trowc = inp("trowc", [2, CTX])
    G.tapsc = P.dram("tapsc", [2, E, 2 * CTX], BF16)
    G.normc = P.dram("normc", [2, E], F32)
    G.negd = inp("negd", [128, E // 128])
    G.tb16 = inp("tb16", [128, 17 * 128], BF16)
    G.tf32 = inp("tf32", [128, 512])
    G.zt = inp("zt", [NEMB, L])
    G.ztr = inp("ztr", [NEMB, L])
    G.trow = inp("trow", [2, L])
    G.tapsd = P.dram("tapsd", [2, E, 2 * L], BF16, kind=("ExternalOutput" if debug == 2 else "Internal"))
    G.normd = P.dram("normd", [2, E], F32, kind=("ExternalOutput" if debug == 2 else "Internal"))
    G.wsrc = G.w_mod
    out = P.dram("out", [L, D], F32, kind="ExternalOutput")
    kind = "ExternalOutput" if debug else "Internal"
    G.xA = P.dram("xA", [L, D], F32, kind=kind)
    G.xB = P.dram("xB", [L, D], F32)
    G.ctxA = P.dram("ctxA", [CTX, D], F32, kind=kind)
    G.ctxB = P.dram("ctxB", [CTX, D], F32, kind=kind)
    G.uT = Q([P.dram("uT%d" % q, [E, L + 3], F32, kind=kind) for q in range(4)], E)
    G.ucT = Q([P.dram("ucT%d" % q, [E, CTX + 3], F32) for q in range(4)], E)
    G.ygT = P.dram("ygT", [E, L], BF16, kind=kind)
    G.ygcT = P.dram("ygcT", [E, CTX], BF16)
    G.hfd = [P.dram("hfd%d" % i, [96, L], BF16) for i in range(E // 96)]
    G.ones = P.sb("ones", [128, 128], F32, persist=True)
    G.ident = P.sb("identsb", [128, 128], BF16, persist=True)
    G.csil = P.sb("csil", [128, 2, 8], F32, persist=True)
    G.eps = P.sb("epsc", [128, 1], F32, persist=True)
    G.modt = [P.sb("modt%d" % j, [128, 3 * D], F32, persist=True) for j in range(2)]

    emit_setup(P, G, L)
    if debug == 2:
        G.dbg_hid = P.dram("dbg_hid", [2, FH, L], BF16, kind="ExternalOutput")
        G.dbg_nacc = P.dram("dbg_nacc", [128, 2, E // 128], F32, kind="ExternalOutput")
        emit_mod(P, G, 1)
        emit_proj(P, G, G.x_in, L, 0, G.hy_w_in.t.ap()[0], 4 * E, 128, G.uT, 1, False)
        emit_hyfilt(P, G, 0, L, L, G.zt, G.ztr, G.trow, G.tapsd, G.normd)
        emit_hyconv(P, G, 0, G.uT, L, 128, CBL, G.tb16, G.tf32, G.hypar, G.tapsd, G.normd, G.ygT, nb_limit=HYDBG_NB)
        P.begin()
        P.op("sp", lambda e: e.dma_start(out=out.t.ap()[0:1, 0:2], in_=G.ygcT.t.ap()[1:2, 0:1].bitcast(F32) if False else G.xB.t.ap()[1:2, 0:2]), reads=[G.ygT], writes=[out], dma=out)
        P.end()
        P.root.close()
        return nc
    xs, cs = G.x_in, G.ctx_in
    xbufs, cbufs = [G.xA, G.xB], [G.ctxA, G.ctxB]
    for l in range(nlayers):
        kindl, occ = l % 2, l // 2
        need_ctx = any(jj % 2 == 0 for jj in range(l + 1, DEPTH))
        colmajor = occ % 2 == 1
        xd = out if l == nlayers - 1 and not debug else xbufs[l % 2]
        if l == nlayers - 1 and debug:
            xd = out
        cd = cbufs[l % 2]
        emit_mod(P, G, l)
        if kindl == 0:
            emit_proj(P, G, xs, L, 0, G.rg_w_in.t.ap()[occ], 2 * E, 96, G.uT, 1, colmajor)
            emit_proj(P, G, cs, CTX, 1, G.rg_w_in.t.ap()[occ], 2 * E, 96, G.ucT, 1, False)
            emit_rg(P, G, occ, G.uT, G.ucT, 1, L, G.ygT, G.ygcT, need_ctx)
            emit_out(P, G, l, xs, xd, L, 0, G.ygT, G.rg_w_out.t.ap()[occ], 96, colmajor)
            if need_ctx:
                emit_out(P, G, l, cs, cd, CTX, 1, G.ygcT, G.rg_w_out.t.ap()[occ], 96, False)
        else:
            emit_proj(P, G, xs, L, 0, G.hy_w_in.t.ap()[occ], 4 * E, 128, G.uT, 1, colmajor)
            emit_hyfilt(P, G, occ, L, L, G.zt, G.ztr, G.trow, G.tapsd, G.normd)
            emit_hyconv(P, G, occ, G.uT, L, 128, CBL, G.tb16, G.tf32, G.hypar, G.tapsd, G.normd, G.ygT)
            emit_out(P, G, l, xs, xd, L, 0, G.ygT, G.hy_w_out.t.ap()[occ], 128, colmajor)
            if need_ctx and not SKIP_CTX_HY:
                emit_proj(P, G, cs, CTX, 1, G.hy_w_in.t.ap()[occ], 4 * E, 128, G.ucT, 1, False)
                emit_hyfilt(P, G, occ, CTX, CTX, G.ztc, G.ztrc, G.trowc, G.tapsc, G.normc)
                emit_hyconv(P, G, occ, G.ucT, CTX, 16, CBC, G.tb16c, G.tf32c, G.hyparc, G.tapsc, G.normc, G.ygcT)
                emit_out(P, G, l, cs, cd, CTX, 1, G.ygcT, G.hy_w_out.t.ap()[occ], 128, False)
        xs = xd
        if need_ctx:
            cs = cd
    P.begin()
    P.op("sp", lambda e: e.dma_start(out=G.ygcT.t.ap()[0:1, 0:2], in_=G.ygcT.t.ap()[1:2, 0:2]), reads=[out], writes=[G.ygcT], dma=G.ygcT)
    P.end()
    P.root.close()
    return nc


def emb_tables(Lf):
    t = np.linspace(0.0, 1.0, Lf, dtype=np.float32)
    bands = np.linspace(1e-4, 7, 8, dtype=np.float32)
    w = (np.float32(2.0 * np.pi) * np.arange(Lf, dtype=np.float32) / np.float32(Lf))
    z = np.concatenate([t[:, None], np.cos(bands[None, :] * w[:, None]), -np.sin(bands[None, :] * w[:, None])], axis=1).astype(np.float32)
    idx = np.concatenate([[0], Lf - np.arange(1, Lf)])
    zt = np.ascontiguousarray(z.T)
    ztr = np.ascontiguousarray(z[idx].T)
    trow = np.stack([t, t[idx]]).astype(np.float32)
    return zt, ztr, trow


def hy_tables(L, Pn=128, sfx=""):
    import ml_dtypes
    assert L == Pn * Pn
    N = 2 * L
    N1 = 2 * Pn
    n1 = np.arange(Pn, dtype=np.float64)[:, None]
    k1 = np.arange(Pn, dtype=np.float64)[None, :]
    ang = 2 * np.pi * n1 * (k1 + 0.5) / N1
    FA0 = np.concatenate([np.cos(ang), -np.sin(ang)], 1)
    ang1 = 2 * np.pi * (n1 + Pn) * (k1 + 0.5) / N1
    FA1 = -np.concatenate([np.cos(ang1), -np.sin(ang1)], 1)
    angt = 2 * np.pi * n1 * (k1 + 0.5) / N
    angc = 2 * np.pi * n1 * k1 / Pn
    FCre, FCim = np.cos(angc), -np.sin(angc)
    FCc = np.concatenate([np.cos(angc.T), np.sin(angc.T)], 1)
    FCc2 = np.concatenate([-np.sin(angc.T), np.cos(angc.T)], 1)
    CA = (2.0 / N) * np.cos(ang.T)
    SA = -(2.0 / N) * np.sin(ang.T)
    tb16 = np.concatenate([FA0, FA1, FCre, FCim, -FCim, FCc, FCc2, CA, SA, -FCre, -FCc, -CA], 1).astype(np.float32).astype(ml_dtypes.bfloat16)
    tf32 = np.concatenate([np.cos(angt), -np.sin(angt), np.cos(angt.T), np.sin(angt.T)], 1).astype(np.float32)
    zt, ztr, trow = emb_tables(L)
    max_decay = np.log(1e-2) / 0.3
    min_decay = np.log(1e-2) / 1.5
    deltas = np.abs(np.linspace(min_decay, max_decay, E, dtype=np.float32))
    negd = np.ascontiguousarray((-deltas).reshape(E // 128, 128).T).astype(np.float32)
    return {"tb16" + sfx: tb16, "tf32" + sfx: tf32, "zt" + sfx: zt, "ztr" + sfx: ztr, "trow" + sfx: trow, "negd": negd}


def host_inputs(inputs, b):
    import ml_dtypes
    f = lambda a: np.ascontiguousarray(a, dtype=np.float32)
    cc = np.zeros((128, 2, 8), np.float32)
    cc[:, 0, :] = inputs["c"][b].reshape(8, 128).T
    cc[:, 1, :] = inputs["c_ctx"].reshape(8, 128).T
    NT = E // 96
    rgpar = np.zeros((2, 96, NT, 16), np.float32)

    def chan(v):
        return v.reshape(2, NT, 96).transpose(0, 2, 1)

    for k in range(4):
        rgpar[:, :, :, k] = chan(inputs["rg_conv_w"][:, k, :])
    rgpar[:, :, :, 4] = chan(inputs["rg_conv_b"])
    for d in range(2):
        rgpar[:, :, :, 5 + 2 * d] = chan(inputs["rg_b_r"][:, d, :])
        rgpar[:, :, :, 6 + 2 * d] = chan(inputs["rg_b_i"][:, d, :])
        rgpar[:, :, :, 9 + d] = chan(inputs["rg_lambda"][:, d, :])
    rgw = np.zeros((2, NT, 96, 4, 96), np.float32)
    for d in range(2):
        rgw[:, :, :, 2 * d + 0, :] = inputs["rg_w_r"][:, d]
        rgw[:, :, :, 2 * d + 1, :] = inputs["rg_w_i"][:, d]
    L = inputs["x"].shape[1]
    hy = hy_tables(L)
    hy.update(hy_tables(CTX, 16, "c"))

    def mk_hypar(CB):
        NB = E // CB
        hypar = np.zeros((2, NB, 14 * CB), np.float32)
        cw = inputs["hy_conv_w"].reshape(2, 3, 3, NB, CB)
        for tap in range(3):
            hypar[:, :, tap * 3 * CB:(tap + 1) * 3 * CB] = cw[:, tap].transpose(0, 2, 1, 3).reshape(2, NB, 3 * CB)
        hypar[:, :, 9 * CB:12 * CB] = inputs["hy_conv_b"].reshape(2, 3, NB, CB).transpose(0, 2, 1, 3).reshape(2, NB, 3 * CB)
        hypar[:, :, 12 * CB:13 * CB] = inputs["hy_d"][:, 0].reshape(2, NB, CB)
        hypar[:, :, 13 * CB:14 * CB] = inputs["hy_d"][:, 1].reshape(2, NB, CB)
        return hypar

    hypar = mk_hypar(CBL)
    hyparc = mk_hypar(CBC)
    hyfp = np.stack([inputs["hy_f_freq"], inputs["hy_f_b1"], inputs["hy_f_b2"]], axis=-1).astype(np.float32)
    extra = {"hy_w_in": f(inputs["hy_w_in"]), "hy_w_out": f(inputs["hy_w_out"]), "hy_f_w1": f(inputs["hy_f_w1"]),
             "hy_f_w2": f(inputs["hy_f_w2"]), "hy_f_w3": f(inputs["hy_f_w3"]), "hyfp": hyfp, "hypar": hypar, "hyparc": hyparc}
    extra.update(hy)
    return {
        **extra,
        "x": f(inputs["x"][b]), "ctx": f(inputs["ctx"][b]), "cc": cc,
        "ident": np.eye(128, dtype=np.float32).astype(ml_dtypes.bfloat16),
        "w_mod": f(inputs["w_mod"]), "b_mod": f(inputs["b_mod"]), "ln_g": f(inputs["ln_g"]), "ln_b": f(inputs["ln_b"]),
        "rg_w_in": f(inputs["rg_w_in"]), "rgpar": rgpar, "rgw": rgw, "rg_w_out": f(inputs["rg_w_out"]),
    }


def kernel(**inputs):
    L = inputs["x"].shape[1]
    nc = build_nc(L)
    in_maps = [host_inputs(inputs, b) for b in range(2)]
    res = run_bass_kernel_spmd(nc, in_maps, core_ids=[0, 1])
    return np.stack([res.results[b]["out"] for b in range(2)], axis=0).astype(np.float32)
```
